# Optimizing a Trainium2 kernel written in Bass

```python
import jax, jax.numpy as jnp
from jax import lax
import numpy as np

D_MODEL = 1024
BATCH = 8
SEQ = 2048
DEPTH = 4
DEC_BATCH = 128
DEC_SEQ = 8
PAST_LEN = 16384
PAGE_SIZE = 128

N_EVEN = (DEPTH + 1) // 2
N_ODD = DEPTH // 2
POOL_WINDOWS = (2, 4, 8, 16)
N_POOL_GROUPS = len(POOL_WINDOWS)
D_POOL = D_MODEL // 2
POOL_GROUP = D_POOL // N_POOL_GROUPS
POOL_BUF = max(POOL_WINDOWS) - 1
D_CONF = D_MODEL // 2
CONF_WIDTH = 31
D_GCONV = D_MODEL
GCONV_WIDTH = 3
D_FF = 2816
FFN_CONV_WIDTH = 3
N_MEM = 256
N_MEM_HEADS = 4
MEM_HEAD_DIM = D_MODEL // N_MEM_HEADS
RMS_EPS = 1e-6
LN_EPS = 1e-5
NORM_MIX_PRE, NORM_MIX_POST, NORM_X_PRE, NORM_X_POST, NORM_FFN_PRE, NORM_FFN_POST, NORM_MEM = range(7)
N_NORMS = 7

kernel_name = 'hybrid_pool_conformer_shortconv_decoder_step'


def rms_norm(x, g):
    xf = x.astype(jnp.float32)
    y = xf * lax.rsqrt(jnp.mean(xf * xf, axis=-1, keepdims=True) + RMS_EPS)
    return (y * g.astype(jnp.float32)).astype(x.dtype)


def layer_norm(x, g, b):
    xf = x.astype(jnp.float32)
    mu = jnp.mean(xf, axis=-1, keepdims=True)
    xc = xf - mu
    var = jnp.mean(xc * xc, axis=-1, keepdims=True)
    y = xc * lax.rsqrt(var + LN_EPS) * g.astype(jnp.float32) + b.astype(jnp.float32)
    return y.astype(x.dtype)


def causal_dwconv(x_ext, w):
    c = w.shape[1]
    return lax.conv_general_dilated(x_ext, w[:, None, :].astype(x_ext.dtype), window_strides=(1,), padding='VALID',
                                    dimension_numbers=('NWC', 'WIO', 'NWC'), feature_group_count=c)


def multiscale_pool(a_ext, n_new, pos):
    L = a_ext.shape[1]
    af = a_ext.astype(jnp.float32)
    cs = jnp.cumsum(af, axis=1)
    outs = []
    for gi, w in enumerate(POOL_WINDOWS):
        sl = slice(gi * POOL_GROUP, (gi + 1) * POOL_GROUP)
        c = cs[..., sl]
        lagged = jnp.pad(c, ((0, 0), (w, 0), (0, 0)))[:, :L]
        win = (c - lagged)[:, L - n_new:]
        cnt = jnp.minimum(pos + 1, w).astype(jnp.float32)
        outs.append(win / cnt[None, :, None] - af[:, L - n_new:, sl])
    return jnp.stack(outs, axis=2).astype(a_ext.dtype)


def even_mixer(h, pool_prev, conf_prev, pos, w_in, w_pool, pool_scale, conf_w, conf_b, conf_ln_g, conf_ln_b, w_out):
    bsz, n, _ = h.shape
    p = h @ w_in
    a = p[..., :D_POOL]
    u = p[..., D_POOL:D_POOL + D_CONF] * jax.nn.sigmoid(p[..., D_POOL + D_CONF:])
    a_ext = jnp.concatenate([pool_prev, a], axis=1)
    u_ext = jnp.concatenate([conf_prev, u], axis=1)
    pooled = multiscale_pool(a_ext, n, pos)
    ya = jnp.einsum('bsgc,gcd->bsgd', pooled, w_pool).reshape(bsz, n, D_POOL) * pool_scale
    cb = causal_dwconv(u_ext, conf_w) + conf_b
    yb = jax.nn.silu(layer_norm(cb, conf_ln_g, conf_ln_b))
    y = jnp.concatenate([ya, yb], axis=-1) @ w_out
    return y, a_ext[:, -POOL_BUF:], u_ext[:, -(CONF_WIDTH - 1):]


def odd_mixer(h, gconv_prev, w_in, gconv_w, w_out):
    p = h @ w_in
    xin = p[..., :D_GCONV]
    gate_b = p[..., D_GCONV:2 * D_GCONV]
    gate_c = p[..., 2 * D_GCONV:]
    u_ext = jnp.concatenate([gconv_prev, gate_c * xin], axis=1)
    y = gate_b * causal_dwconv(u_ext, gconv_w)
    return y @ w_out, u_ext[:, -(GCONV_WIDTH - 1):]


def mem_kv(mem, g_mem, w_k, w_v):
    bsz = mem.shape[0]
    m = rms_norm(mem, g_mem)
    k = (m @ w_k).reshape(bsz, N_MEM, N_MEM_HEADS, MEM_HEAD_DIM)
    v = (m @ w_v).reshape(bsz, N_MEM, N_MEM_HEADS, MEM_HEAD_DIM)
    return k, v


def mem_attend(h, k, v, w_q, w_o):
    bsz, n, _ = h.shape
    q = (h @ w_q).reshape(bsz, n, N_MEM_HEADS, MEM_HEAD_DIM)
    s = jnp.einsum('bqhd,bkhd->bhqk', q, k).astype(jnp.float32) * (MEM_HEAD_DIM ** -0.5)
    pr = jax.nn.softmax(s, axis=-1).astype(v.dtype)
    o = jnp.einsum('bhqk,bkhd->bqhd', pr, v).reshape(bsz, n, D_MODEL)
    return o @ w_o


def conv_ffn(h, ffn_prev, w_gate, w_up, conv_w, conv_b, w_down):
    g_ext = jnp.concatenate([ffn_prev, h @ w_gate], axis=1)
    gc = causal_dwconv(g_ext, conv_w) + conv_b
    y = (jax.nn.silu(gc) * (h @ w_up)) @ w_down
    return y, g_ext[:, -(FFN_CONV_WIDTH - 1):]


def run_trunk(x, pos, pool_prev, conf_prev, gconv_prev, ffn_prev, mem_k, mem_v, prm):
    new_pool, new_conf, new_gconv, new_ffn = [], [], [], []
    for l in range(DEPTH):
        ng = prm['norm_gains'][l]
        h = rms_norm(x, ng[NORM_MIX_PRE])
        if l % 2 == 0:
            e = l // 2
            y, sp, sc = even_mixer(h, pool_prev[e], conf_prev[e], pos, prm['w_in_even'][e], prm['w_pool'][e],
                                   prm['pool_scale'][e], prm['conf_w'][e], prm['conf_b'][e], prm['conf_ln_g'][e],
                                   prm['conf_ln_b'][e], prm['w_out_even'][e])
            new_pool.append(sp)
            new_conf.append(sc)
        else:
            o = l // 2
            y, sg = odd_mixer(h, gconv_prev[o], prm['w_in_odd'][o], prm['gconv_w'][o], prm['w_out_odd'][o])
            new_gconv.append(sg)
        x = x + rms_norm(y, ng[NORM_MIX_POST])
        h = rms_norm(x, ng[NORM_X_PRE])
        y = mem_attend(h, mem_k[l], mem_v[l], prm['w_mem_q'][l], prm['w_mem_o'][l])
        x = x + rms_norm(y, ng[NORM_X_POST])
        h = rms_norm(x, ng[NORM_FFN_PRE])
        y, sf = conv_ffn(h, ffn_prev[l], prm['w_ffn_gate'][l], prm['w_ffn_up'][l], prm['ffn_conv_w'][l],
                         prm['ffn_conv_b'][l], prm['w_ffn_down'][l])
        new_ffn.append(sf)
        x = x + rms_norm(y, ng[NORM_FFN_POST])
    return x, jnp.stack(new_pool), jnp.stack(new_conf), jnp.stack(new_gconv), jnp.stack(new_ffn)


def setup_inputs(seed: int = 0) -> dict:
    key = jax.random.key(seed)
    ks = jax.random.split(key, 32)
    f32 = jnp.float32

    def nrm(k, shape, scale=1.0):
        return jax.random.normal(k, shape, f32) * scale

    return {
        'x_prompt': nrm(ks[0], (BATCH, SEQ, D_MODEL)),
        'x_sample': nrm(ks[1], (DEC_BATCH, DEC_SEQ, D_MODEL)),
        'state_pool': nrm(ks[2], (N_EVEN, DEC_BATCH, POOL_BUF, D_POOL)),
        'state_conf': nrm(ks[3], (N_EVEN, DEC_BATCH, CONF_WIDTH - 1, D_CONF)),
        'state_gconv': nrm(ks[4], (N_ODD, DEC_BATCH, GCONV_WIDTH - 1, D_GCONV)),
        'state_ffn': nrm(ks[5], (DEPTH, DEC_BATCH, FFN_CONV_WIDTH - 1, D_FF)),
        'cache_mem_k': nrm(ks[6], (DEPTH, DEC_BATCH, N_MEM, N_MEM_HEADS, MEM_HEAD_DIM)),
        'cache_mem_v': nrm(ks[7], (DEPTH, DEC_BATCH, N_MEM, N_MEM_HEADS, MEM_HEAD_DIM)),
        'mem_prompt': nrm(ks[8], (BATCH, N_MEM, D_MODEL)),
        'norm_gains': 1.0 + nrm(ks[9], (DEPTH, N_NORMS, D_MODEL), 0.05),
        'w_in_even': nrm(ks[10], (N_EVEN, D_MODEL, D_POOL + 2 * D_CONF), D_MODEL ** -0.5),
        'w_pool': nrm(ks[11], (N_EVEN, N_POOL_GROUPS, POOL_GROUP, POOL_GROUP), POOL_GROUP ** -0.5),
        'pool_scale': 1.0 + nrm(ks[12], (N_EVEN, D_POOL), 0.1),
        'conf_w': nrm(ks[13], (N_EVEN, CONF_WIDTH, D_CONF), CONF_WIDTH ** -0.5),
        'conf_b': nrm(ks[14], (N_EVEN, D_CONF), 0.02),
        'conf_ln_g': 1.0 + nrm(ks[15], (N_EVEN, D_CONF), 0.05),
        'conf_ln_b': nrm(ks[16], (N_EVEN, D_CONF), 0.02),
        'w_out_even': nrm(ks[17], (N_EVEN, D_POOL + D_CONF, D_MODEL), (D_POOL + D_CONF) ** -0.5),
        'w_in_odd': nrm(ks[18], (N_ODD, D_MODEL, 3 * D_GCONV), D_MODEL ** -0.5),
        'gconv_w': nrm(ks[19], (N_ODD, GCONV_WIDTH, D_GCONV), GCONV_WIDTH ** -0.5),
        'w_out_odd': nrm(ks[20], (N_ODD, D_GCONV, D_MODEL), D_GCONV ** -0.5),
        'w_mem_q': nrm(ks[21], (DEPTH, D_MODEL, D_MODEL), D_MODEL ** -0.5),
        'w_mem_k': nrm(ks[22], (DEPTH, D_MODEL, D_MODEL), D_MODEL ** -0.5),
        'w_mem_v': nrm(ks[23], (DEPTH, D_MODEL, D_MODEL), D_MODEL ** -0.5),
        'w_mem_o': nrm(ks[24], (DEPTH, D_MODEL, D_MODEL), D_MODEL ** -0.5),
        'w_ffn_gate': nrm(ks[25], (DEPTH, D_MODEL, D_FF), D_MODEL ** -0.5),
        'w_ffn_up': nrm(ks[26], (DEPTH, D_MODEL, D_FF), D_MODEL ** -0.5),
        'ffn_conv_w': nrm(ks[27], (DEPTH, FFN_CONV_WIDTH, D_FF), FFN_CONV_WIDTH ** -0.5),
        'ffn_conv_b': nrm(ks[28], (DEPTH, D_FF), 0.02),
        'w_ffn_down': nrm(ks[29], (DEPTH, D_FF, D_MODEL), D_FF ** -0.5),
    }


def reference(x_prompt, x_sample, state_pool, state_conf, state_gconv, state_ffn, cache_mem_k, cache_mem_v,
              mem_prompt, norm_gains, w_in_even, w_pool, pool_scale, conf_w, conf_b, conf_ln_g, conf_ln_b,
              w_out_even, w_in_odd, gconv_w, w_out_odd, w_mem_q, w_mem_k, w_mem_v, w_mem_o,
              w_ffn_gate, w_ffn_up, ffn_conv_w, ffn_conv_b, w_ffn_down):
    prm = dict(norm_gains=norm_gains, w_in_even=w_in_even, w_pool=w_pool, pool_scale=pool_scale, conf_w=conf_w,
               conf_b=conf_b, conf_ln_g=conf_ln_g, conf_ln_b=conf_ln_b, w_out_even=w_out_even, w_in_odd=w_in_odd,
               gconv_w=gconv_w, w_out_odd=w_out_odd, w_mem_q=w_mem_q, w_mem_o=w_mem_o, w_ffn_gate=w_ffn_gate,
               w_ffn_up=w_ffn_up, ffn_conv_w=ffn_conv_w, ffn_conv_b=ffn_conv_b, w_ffn_down=w_ffn_down)
    dt = x_prompt.dtype
    kv = [mem_kv(mem_prompt, norm_gains[l, NORM_MEM], w_mem_k[l], w_mem_v[l]) for l in range(DEPTH)]
    mem_k_p = jnp.stack([k for k, _ in kv])
    mem_v_p = jnp.stack([v for _, v in kv])
    pos_p = jnp.arange(SEQ, dtype=jnp.int32)
    y_prompt, pool_p, conf_p, gconv_p, ffn_p = run_trunk(
        x_prompt, pos_p,
        jnp.zeros((N_EVEN, BATCH, POOL_BUF, D_POOL), dt),
        jnp.zeros((N_EVEN, BATCH, CONF_WIDTH - 1, D_CONF), dt),
        jnp.zeros((N_ODD, BATCH, GCONV_WIDTH - 1, D_GCONV), dt),
        jnp.zeros((DEPTH, BATCH, FFN_CONV_WIDTH - 1, D_FF), dt),
        mem_k_p, mem_v_p, prm)
    pos_s = PAST_LEN + jnp.arange(DEC_SEQ, dtype=jnp.int32)
    y_sample, pool_s, conf_s, gconv_s, ffn_s = run_trunk(
        x_sample, pos_s, state_pool, state_conf, state_gconv, state_ffn, cache_mem_k, cache_mem_v, prm)
    return (y_prompt, y_sample, pool_p, conf_p, gconv_p, ffn_p, mem_k_p, mem_v_p, pool_s, conf_s, gconv_s, ffn_s)
```

```python
from contextlib import ExitStack

import numpy as np
import concourse.bass as bass
import concourse.mybir as mybir
from concourse.bass_utils import run_bass_kernel_spmd

F32 = mybir.dt.float32
BF16 = mybir.dt.bfloat16
ALU = mybir.AluOpType
ACTF = mybir.ActivationFunctionType

NCORES = 8
D = 1024
SEQ = 2048
DEPTH = 4
NB_S = 16
TS = 8
DFF = 2816
NMEM = 256
TCOLS = 1152
RMS_EPS = 1e-6
LN_EPS = 1e-5
POOLW = (2, 4, 8, 16)


class Buf:
    __slots__ = ("w", "r")

    def __init__(self):
        self.w = None
        self.r = {}


class Op:
    __slots__ = ("eng", "fn", "deps", "waited", "sig", "is_dma", "waits", "clock", "idx", "tag")


class Sched:
    def __init__(self, npool=12):
        self.streams = {k: [] for k in ("pe", "act", "dve", "pool", "sp")}
        self.order = []
        self.npool = npool
        self.hist = {k: [] for k in self.streams}

    def add(self, eng, fn, reads=(), writes=(), dma=False):
        op = Op()
        op.eng = eng
        op.fn = fn
        op.is_dma = dma
        op.waited = dma
        op.sig = None
        op.idx = len(self.order)
        op.tag = getattr(self, "cur_tag", "")
        deps = {}
        for b in reads:
            w = b.w
            if w is not None and (w.is_dma or w.eng != eng or eng != "pe"):
                deps[id(w)] = w
        for b in writes:
            w = b.w
            if w is not None and (w.is_dma or w.eng != eng or eng != "pe"):
                deps[id(w)] = w
            for o in b.r.values():
                if o.is_dma or o.eng != eng or eng != "pe":
                    deps[id(o)] = o
        if dma:
            hist = self.hist[eng]
            if len(hist) >= self.npool:
                prev = hist[len(hist) - self.npool]
                deps[id(prev)] = prev
            hist.append(op)
        op.deps = list(deps.values())
        for b in reads:
            b.r[id(op) if dma else eng] = op
        for b in writes:
            b.w = op
            b.r = {}
        self.streams[eng].append(op)
        self.order.append(op)
        return op

    def finalize(self, sems, dma_sems):
        for op in self.order:
            for d in op.deps:
                d.waited = True
        for eng in self.streams:
            cnt = 0
            dcnt = 0
            uses = [0] * self.npool
            for op in self.streams[eng]:
                if op.is_dma:
                    k = dcnt % self.npool
                    uses[k] += 1
                    op.sig = (("d", eng, k), 16 * uses[k])
                    dcnt += 1
                elif op.waited:
                    cnt += 1
                    op.sig = (("e", eng), cnt)
        seen = {eng: {} for eng in self.streams}
        for op in self.order:
            s = seen[op.eng]
            op.waits = []
            for d in sorted(op.deps, key=lambda d: -d.sig[1]):
                key, v = d.sig
                if s.get(key, 0) < v:
                    op.waits.append((key, v))
                    for k2, v2 in d.clock.items():
                        if s.get(k2, 0) < v2:
                            s[k2] = v2
            if op.waited:
                c = dict(s)
                c[op.sig[0]] = op.sig[1]
                op.clock = c
            else:
                op.clock = None

        def handle(key):
            if key[0] == "e":
                return sems[key[1]]
            return dma_sems[key[1]][key[2]]

        def emit(name):
            def body(e):
                for op in self.streams[name]:
                    for key, v in op.waits:
                        e.wait_ge(handle(key), v)
                    ins = op.fn(e)
                    if op.waited:
                        ins.then_inc(handle(op.sig[0]), 16 if op.is_dma else 1)
            return body
        return emit


def merged_hazards(bufs):
    out = {}
    for b in bufs:
        items = list(b.r.items())
        if b.w is not None:
            items.append((id(b.w) if b.w.is_dma else b.w.eng, b.w))
        for k, o in items:
            if k not in out or out[k].idx < o.idx:
                out[k] = o
    return out


class Tile:
    def __init__(self, n, col0, w, kind, tok0):
        self.n, self.c0, self.w, self.kind, self.tok0 = n, col0, w, kind, tok0
        self.cols = slice(col0, col0 + w)


class Builder:
    def __init__(self):
        self.nc = bass.Bass("TRN2", target_bir_lowering=False)
        self.S = Sched()
        self.es = ExitStack()
        self.out_bufs = []
        self.bank_rr = 0
        self.st_rr = 0
        self.region_bufs = {"big": [], "scr": []}
        self.cfg = dict(layers=DEPTH, passes=2, sub=("m", "a", "f"), phase0=True)

    def sb(self, name, shape, dt):
        return self.es.enter_context(self.nc.sbuf_tensor(name, shape, dt))

    def dram_in(self, name, shape):
        return self.nc.dram_tensor(name, list(shape), F32, kind="ExternalInput").ap()

    def dram_out(self, name, shape):
        return self.nc.dram_tensor(name, list(shape), F32, kind="ExternalOutput").ap()

    def rbufs(self, region, n):
        hz = merged_hazards(self.region_bufs[region])
        bs = []
        for _ in range(n):
            b = Buf()
            b.r = dict(hz)
            bs.append(b)
        self.region_bufs[region] = self.region_bufs[region] + bs
        return bs

    def region_reset(self, region, keep):
        hz = merged_hazards(self.region_bufs[region])
        carrier = Buf()
        carrier.r = hz
        self.region_bufs[region] = [carrier] + list(keep)

    def bank(self):
        i = self.bank_rr % 6
        self.bank_rr += 1
        return self.banks[i], self.bankB[i]

    def stbank(self):
        i = 6 + self.st_rr % 2
        self.st_rr += 1
        return self.banks[i], self.bankB[i]

    def add(self, *a, **k):
        return self.S.add(*a, **k)

    def build(self):
        nc = self.nc
        A = self.add
        di = self.dram_in
        self.xp = di("xp", [SEQ, D])
        self.xs = di("xs", [NB_S * TS, D])
        self.st_pool = di("st_pool", [2, NB_S, 15, 512])
        self.st_conf = di("st_conf", [2, NB_S, 30, 512])
        self.st_gconv = di("st_gconv", [2, NB_S, 2, D])
        self.st_ffn = di("st_ffn", [DEPTH, NB_S, 2, DFF])
        self.ck = di("ck", [DEPTH, NB_S, NMEM, D])
        self.cv = di("cv", [DEPTH, NB_S, NMEM, D])
        self.memp = di("memp", [NMEM, D])
        self.norm_gains = di("norm_gains", [DEPTH * 7, D])
        self.w_in_even = di("w_in_even", [2, D, 1536])
        self.w_pool = di("w_pool", [2, 4, 128, 128])
        self.pool_scale = di("pool_scale", [2, 512])
        self.conf_w = di("conf_w", [2 * 31, 512])
        self.conf_b = di("conf_b", [2, 512])
        self.conf_ln_g = di("conf_ln_g", [2, 512])
        self.conf_ln_b = di("conf_ln_b", [2, 512])
        self.w_out_even = di("w_out_even", [2, D, D])
        self.w_in_odd = di("w_in_odd", [2, D, 3 * D])
        self.gconv_w = di("gconv_w", [2 * 3, D])
        self.w_out_odd = di("w_out_odd", [2, D, D])
        self.w_mem_q = di("w_mem_q", [DEPTH, D, D])
        self.w_mem_k = di("w_mem_k", [DEPTH, D, D])
        self.w_mem_v = di("w_mem_v", [DEPTH, D, D])
        self.w_mem_o = di("w_mem_o", [DEPTH, D, D])
        self.w_ffn_gate = di("w_ffn_gate", [DEPTH, D, DFF])
        self.w_ffn_up = di("w_ffn_up", [DEPTH, D, DFF])
        self.ffn_conv_w = di("ffn_conv_w", [DEPTH * 3, DFF])
        self.ffn_conv_b = di("ffn_conv_b", [DEPTH, DFF])
        self.w_ffn_down = di("w_ffn_down", [DEPTH, DFF, D])
        do = self.dram_out
        self.yp = do("yp", [SEQ, D])
        self.ys = do("ys", [NB_S * TS, D])
        self.o_pool_p = do("o_pool_p", [2, 15, 512])
        self.o_conf_p = do("o_conf_p", [2, 30, 512])
        self.o_gconv_p = do("o_gconv_p", [2, 2, D])
        self.o_ffn_p = do("o_ffn_p", [DEPTH, 2, DFF])
        self.o_mk_p = do("o_mk_p", [DEPTH, NMEM, D])
        self.o_mv_p = do("o_mv_p", [DEPTH, NMEM, D])
        self.o_pool_s = do("o_pool_s", [2, NB_S, 15, 512])
        self.o_conf_s = do("o_conf_s", [2, NB_S, 30, 512])
        self.o_gconv_s = do("o_gconv_s", [2, NB_S, 2, D])
        self.o_ffn_s = do("o_ffn_s", [DEPTH, NB_S, 2, DFF])
        self.kT_scr = nc.dram_tensor("kT_scr", [DEPTH, 128, 8 * NMEM], BF16, kind="Internal").ap()
        self.v_scr = nc.dram_tensor("v_scr", [DEPTH, 128, 2 * D], BF16, kind="Internal").ap()

        sb = self.sb
        self.x = sb("x", [128, 8, TCOLS], F32)
        self.h = sb("h", [128, 8, TCOLS], BF16)
        self.BIGN = 25344
        self.big = sb("big", [128, self.BIGN], BF16)
        self.SCRN = 4352
        self.scr = sb("scr", [128, self.SCRN], F32)
        self.ring = [sb(f"ring{i}", [128, 4096], BF16) for i in range(4)]
        self.ringB = [Buf() for _ in range(4)]
        self.stg = [sb(f"stg{i}", [128, 1024], F32) for i in range(2)]
        self.stgB = [Buf() for _ in range(2)]
        self.stg_rr = 0
        self.c1024 = sb("c1024", [128, 8, 34], F32)
        self.c512 = sb("c512", [128, 4, 70], F32)
        self.c2816 = sb("c2816", [128, 22, 16], F32)
        self.cB = Buf()
        self.ident = sb("ident", [128, 128], F32)
        self.identb = sb("identb", [128, 128], BF16)
        self.ones1024 = sb("ones1024", [128, 128], BF16)
        self.ones512 = sb("ones512", [128, 128], BF16)
        self.ones1 = sb("ones1", [128, 128], BF16)
        self.onesf = sb("onesf", [128, 128], F32)
        self.zeros = sb("zeros", [128, 64], F32)
        self.poolfix = sb("poolfix", [128, 4, 15], F32)
        self.constB = Buf()
        self.vsb = sb("vsb", [128, 512], F32)
        self.vsbB = Buf()
        self.rstd = [sb(f"rstd{i}", [128, 512], F32) for i in range(2)]
        self.rstdB = [Buf() for _ in range(2)]
        self.rstd_rr = 0
        self.tmp = [sb(f"tmp{i}", [128, 512], F32) for i in range(2)]
        self.tmpB = [Buf() for _ in range(2)]
        self.tmp_rr = 0
        self.sq = [sb(f"sq{i}", [128, 512], BF16) for i in range(4)]
        self.sqB = [Buf() for _ in range(4)]
        self.sq_rr = 0
        self.et = [sb(f"et{i}", [128, 2, 512], BF16) for i in range(2)]
        self.etB = [Buf() for _ in range(2)]
        self.et_rr = 0
        self.sl = [sb(f"sl{i}", [128, 512], BF16) for i in range(2)]
        self.slB = [Buf() for _ in range(2)]
        self.sl_rr = 0
        self.fstate = sb("fstate", [128, 22, 32], F32)
        self.fstateB = Buf()
        self.fctx = sb("fctx", [128, 22, 32], F32)
        self.fctxB = Buf()
        self.cstate = sb("cstate", [128, 8, 128], F32)
        self.cstateB = Buf()
        self.kTp = sb("kTp", [128, 8, NMEM], BF16)
        self.kTpB = Buf()
        self.vp = sb("vp", [128, 2, D], BF16)
        self.vpB = Buf()
        self.car_a = [sb(f"car_a{e}", [128, 4, 15], BF16) for e in range(2)]
        self.car_u = [sb(f"car_u{e}", [128, 4, 30], BF16) for e in range(2)]
        self.car_g = [sb(f"car_g{o}", [128, 8, 2], BF16) for o in range(2)]
        self.car_f = [sb(f"car_f{l}", [128, 22, 2], F32) for l in range(DEPTH)]
        self.car_aB = [Buf() for _ in range(2)]
        self.car_uB = [Buf() for _ in range(2)]
        self.car_gB = [Buf() for _ in range(2)]
        self.car_fB = [Buf() for _ in range(DEPTH)]
        self.banks = [self.es.enter_context(nc.psum_tensor(f"bk{i}", [128, 512], F32)) for i in range(8)]
        self.bankB = [Buf() for _ in range(8)]
        self.xB = [[Buf() for _ in range(3)] for _ in range(8)]
        self.hB = [[Buf() for _ in range(3)] for _ in range(8)]

        sems = {k: self.es.enter_context(nc.semaphore("s_" + k)) for k in self.S.streams}
        dsems = {k: [self.es.enter_context(nc.semaphore(f"d_{k}_{i}")) for i in range(self.S.npool)]
                 for k in ("sp", "pool")}

        self.plan = []
        self.w_issued = 0
        self.w_consumed = 0
        self.make_plan()

        self.setup_consts()
        if self.cfg["phase0"]:
            self.phase0_memkv()
        passes = [
            [Tile(0, 0, 512, "P", 0), Tile(1, 512, 512, "P", 512), Tile(2, 1024, 128, "S", 0)],
            [Tile(0, 0, 512, "P", 1024), Tile(1, 512, 512, "P", 1536)],
        ]
        for pi, tiles in enumerate(passes[:self.cfg["passes"]]):
            self.pi = pi
            self.tiles = tiles
            self.load_x()
            for l in range(self.cfg["layers"]):
                self.l = l
                if "m" in self.cfg["sub"]:
                    if l % 2 == 0:
                        self.even_mixer(l // 2)
                    else:
                        self.odd_mixer(l // 2)
                if "a" in self.cfg["sub"]:
                    self.attention(l)
                if "f" in self.cfg["sub"]:
                    self.ffn(l)
            self.store_x()
        assert self.w_consumed == len(self.plan), (self.w_consumed, len(self.plan))
        A("sp", lambda e: e.nop(), reads=self.out_bufs)

        emit = self.S.finalize(sems, dsems)
        with nc.Block() as block:
            block.tensor(emit("pe"))
            block.scalar(emit("act"))
            block.vector(emit("dve"))
            block.gpsimd(emit("pool"))
            block.sync(emit("sp"))
        self.es.close()
        return nc

    def make_plan(self):
        P = self.plan

        def mat(key, w, kch, ncols, csz=512):
            for u, c0 in enumerate(range(0, ncols, csz)):
                cw = min(csz, ncols - c0)
                P.append((key + (u,), w[:, c0:c0 + cw], kch, cw))

        cfg = self.cfg
        for l in range(DEPTH if cfg["phase0"] else 0):
            mat(("wk", l), self.w_mem_k[l], 8, D)
            mat(("wv", l), self.w_mem_v[l], 8, D)
        for pi in range(cfg["passes"]):
            for l in range(cfg["layers"]):
                if "m" not in cfg["sub"]:
                    pass
                elif l % 2 == 0:
                    e = l // 2
                    wie = self.w_in_even[e]
                    P.append((("win", pi, l, 0), wie[:, 0:512], 8, 512))
                    P.append((("wpool", pi, l), self.w_pool[e], None, None))
                    P.append((("win", pi, l, 1), wie[:, 512:1024], 8, 512))
                    P.append((("win", pi, l, 2), wie[:, 1024:1536], 8, 512))
                    mat(("wout", pi, l), self.w_out_even[e], 8, D)
                else:
                    o = l // 2
                    wi = self.w_in_odd[o]
                    for part in (0, 2, 1):
                        mat(("win", pi, l, part), wi[:, part * D:(part + 1) * D], 8, D)
                    mat(("wout", pi, l), self.w_out_odd[o], 8, D)
                if "a" in cfg["sub"]:
                    mat(("wq", pi, l), self.w_mem_q[l], 8, D)
                    mat(("wo", pi, l), self.w_mem_o[l], 8, D)
                if "f" not in cfg["sub"]:
                    continue
                for u, c0 in enumerate(range(0, DFF, 512)):
                    cw = min(512, DFF - c0)
                    P.append((("wg", pi, l, u), self.w_ffn_gate[l][:, c0:c0 + cw], 8, cw))
                    P.append((("wu", pi, l, u), self.w_ffn_up[l][:, c0:c0 + cw], 8, cw))
                for ch in range(2):
                    for kg, (k0, kn) in enumerate(((0, 8), (8, 8), (16, 6))):
                        P.append((("wd", pi, l, ch, kg),
                                  self.w_ffn_down[l][k0 * 128:(k0 + kn) * 128, ch * 512:(ch + 1) * 512], kn, 512))

    def w_issue(self, i):
        key, w, kch, cw = self.plan[i]
        slot = i % 4
        ring = self.ring[slot]
        if kch is None:
            dst = ring[:, 0:512].rearrange("p (g d) -> p g d", g=4)
            src = w.rearrange("g c d -> c g d")
        else:
            dst = ring[:, 0:kch * cw].rearrange("p (c n) -> p c n", c=kch)
            src = w.rearrange("(c p) n -> p c n", p=128)
        self.add("pool", lambda e: e.dma_start(out=dst, in_=src), writes=[self.ringB[slot]], dma=True)

    def wget(self, key, cont=False):
        i = self.w_consumed
        pkey, w, kch, cw = self.plan[i]
        assert pkey == key, (pkey, key)
        lim = min(len(self.plan), i + 4)
        while (not cont) and self.w_issued < lim:
            self.w_issue(self.w_issued)
            self.w_issued += 1
        assert self.w_issued > i
        self.w_consumed += 1
        slot = i % 4
        ring = self.ring[slot]
        if kch is None:
            view = ring[:, 0:512].rearrange("p (g d) -> p g d", g=4)
        else:
            view = ring[:, 0:kch * cw].rearrange("p (c n) -> p c n", c=kch)
        return view, self.ringB[slot], cw

    def next_stg(self):
        i = self.stg_rr % 2
        self.stg_rr += 1
        return self.stg[i], self.stgB[i]

    def mm(self, out, lhsT, rhs, start, stop, reads, wbuf):
        self.add("pe", lambda e: e.matmul(out, lhsT=lhsT, rhs=rhs, start=start, stop=stop),
                 reads=reads, writes=[wbuf])

    def tr(self, out, in_, ident, reads, wbuf):
        self.add("pe", lambda e: e.transpose(out, in_, ident), reads=reads, writes=[wbuf])

    def load_rows_T(self, rows_ap, R, C, evac):
        nch = C // 128
        stg, stgB = self.next_stg()
        self.add("sp", lambda e: e.dma_start(out=stg[0:R, 0:C], in_=rows_ap), writes=[stgB], dma=True)
        done = 0
        per = max(1, 512 // R)
        while done < nch:
            k = min(per, nch - done)
            bk, bkB = self.bank()
            for j in range(k):
                self.tr(bk[:, j * R:(j + 1) * R], stg[0:R, (done + j) * 128:(done + j + 1) * 128],
                        self.ident[0:R, 0:R], [stgB, self.constB], bkB)
            evac(done, k, bk[:, 0:k * R].rearrange("p (c r) -> p c r", c=k), bkB)
            done += k

    def store_rows_T(self, srcs, R, dram_rows, evac_eng="act"):
        nch = len(srcs)
        assert nch <= 8
        stg, stgB = self.next_stg()
        done = 0
        while done < nch:
            k = min(4, nch - done)
            bk, bkB = self.bank()
            for j in range(k):
                ap, bufs = srcs[done + j]
                self.tr(bk[0:R, j * 128:(j + 1) * 128], ap, self.ident[:, :], list(bufs) + [self.constB], bkB)
            o = stg[0:R, done * 128:(done + k) * 128]
            i_ = bk[0:R, 0:k * 128]
            if evac_eng == "act":
                self.add("act", lambda e, o=o, i_=i_: e.activation(out=o, in_=i_, func=ACTF.Copy), reads=[bkB], writes=[stgB])
            else:
                self.add("dve", lambda e, o=o, i_=i_: e.tensor_copy(out=o, in_=i_), reads=[bkB], writes=[stgB])
            done += k
        ob = Buf()
        self.add("sp", lambda e: e.dma_start(out=dram_rows, in_=stg[0:R, 0:nch * 128]), reads=[stgB], writes=[ob], dma=True)
        self.out_bufs.append(ob)

    def gain(self, l, i):
        return lambda c: self.c1024[:, c, l * 7 + i:l * 7 + i + 1]

    def setup_consts(self):
        self.S.cur_tag = "consts"
        A = self.add
        ident, identb = self.ident, self.identb

        cB = self.constB
        A("pool", lambda e: e.memset(ident[:], 0.0), writes=[cB])
        A("pool", lambda e: e.affine_select(out=ident[:], in_=ident[:], compare_op=ALU.not_equal, fill=1.0, base=0,
                                            pattern=[[-1, 128]], channel_multiplier=1), reads=[cB], writes=[cB])
        A("pool", lambda e: e.tensor_copy(out=identb[:], in_=ident[:]), reads=[cB], writes=[cB])
        for tl, val in ((self.ones1024, 1.0 / 1024), (self.ones512, 1.0 / 512), (self.ones1, 1.0), (self.zeros, 0.0), (self.onesf, 1.0)):
            A("pool", lambda e, tl=tl, val=val: e.memset(tl[:], val), writes=[cB])
        for g, w in enumerate(POOLW):
            A("pool", lambda e, g=g: e.memset(self.poolfix[:, g, :], 1.0), writes=[cB])
            for t in range(w - 1):
                A("pool", lambda e, g=g, t=t, w=w: e.memset(self.poolfix[:, g, t:t + 1], float(w) / float(t + 1)), writes=[cB])
        groups = [
            (self.c1024, 8, [(self.norm_gains, 28), (self.gconv_w, 6)]),
            (self.c512, 4, [(self.pool_scale, 2), (self.conf_w, 62), (self.conf_b, 2), (self.conf_ln_g, 2),
                            (self.conf_ln_b, 2)]),
        ]
        for dst, nch, items in groups:
            r0 = 0
            for src, R in items:
                def ev(c0, k, view, bkB, dst=dst, r0=r0, R=R):
                    A("dve", lambda e: e.tensor_copy(out=dst[:, c0:c0 + k, r0:r0 + R], in_=view), reads=[bkB], writes=[self.cB])
                self.load_rows_T(src, R, nch * 128, ev)
                r0 += R
        r0 = 0
        for src, R in [(self.ffn_conv_w, 12), (self.ffn_conv_b, 4)]:
            for c0 in range(0, DFF, 1024):
                cw = min(1024, DFF - c0)

                def ev(cc, k, view, bkB, r0=r0, R=R, base=c0 // 128):
                    A("dve", lambda e: e.tensor_copy(out=self.c2816[:, base + cc:base + cc + k, r0:r0 + R], in_=view),
                      reads=[bkB], writes=[self.cB])
                self.load_rows_T(src[:, c0:c0 + cw], R, cw, ev)
            r0 += R

    def stats_rstd(self, src_fn, w, ones, eps, nchunks=8):
        A = self.add
        st, stB = self.stbank()
        for c in range(nchunks):
            ap, bufs = src_fn(c)
            i = self.sq_rr % 4
            self.sq_rr += 1
            sq, sqB = self.sq[i], self.sqB[i]
            A("act", lambda e, ap=ap, sq=sq: e.activation(out=sq[:, 0:w], in_=ap, func=ACTF.Square), reads=bufs, writes=[sqB])
            self.mm(st[:, 0:w], ones[:, :], sq[:, 0:w], c == 0, c == nchunks - 1, [sqB, self.constB], stB)
        A("act", lambda e: e.activation(out=self.vsb[0:1, 0:w], in_=st[0:1, 0:w], func=ACTF.Sqrt, bias=eps, scale=1.0),
          reads=[stB], writes=[self.vsbB])
        return self.row_pow_bcast(w)

    def row_pow_bcast(self, w, extra=None):
        A = self.add
        i = self.rstd_rr % 2
        self.rstd_rr += 1
        r, rB = self.rstd[i], self.rstdB[i]
        A("dve", lambda e: e.reciprocal(out=r[0:1, 0:w], in_=self.vsb[0:1, 0:w]), reads=[self.vsbB], writes=[rB])
        bc, bcB = self.bank()
        self.bcast_row(bc, bcB, r, rB, w)
        return bc, bcB, r, rB

    def bcast_row(self, bc, bcB, r, rB, w):
        for c0 in range(0, w, 128):
            self.mm(bc[:, c0:c0 + 128], self.onesf[0:1, :], r[0:1, c0:c0 + 128], True, True, [rB, self.constB], bcB)

    def _pre_norm(self, t, gi):
        A = self.add
        x, h = self.x, self.h
        g = self.gain(self.l, gi)
        r, rB, _, _ = self.stats_rstd(lambda c: (x[:, c, t.cols], [self.xB[c][t.n]]), t.w, self.ones1024, RMS_EPS)
        for c in range(8):
            A("dve", lambda e, c=c: e.scalar_tensor_tensor(out=h[:, c, t.cols], in0=x[:, c, t.cols], scalar=g(c),
                                                           in1=r[:, 0:t.w], op0=ALU.mult, op1=ALU.mult),
              reads=[self.xB[c][t.n], rB, self.cB], writes=[self.hB[c][t.n]])

    def _post_norm(self, t, gi):
        A = self.add
        x, y = self.x, self.h
        g = self.gain(self.l, gi)
        r, rB, _, _ = self.stats_rstd(lambda c: (y[:, c, t.cols], [self.hB[c][t.n]]), t.w, self.ones1024, RMS_EPS)
        for c in range(8):
            i = self.tmp_rr % 2
            self.tmp_rr += 1
            tm, tmB = self.tmp[i], self.tmpB[i]
            A("dve", lambda e, c=c, tm=tm: e.scalar_tensor_tensor(out=tm[:, 0:t.w], in0=y[:, c, t.cols], scalar=g(c),
                                                                  in1=r[:, 0:t.w], op0=ALU.mult, op1=ALU.mult),
              reads=[self.hB[c][t.n], rB, self.cB], writes=[tmB])
            A("dve", lambda e, c=c, tm=tm: e.tensor_tensor(out=x[:, c, t.cols], in0=x[:, c, t.cols], in1=tm[:, 0:t.w], op=ALU.add),
              reads=[tmB, self.xB[c][t.n]], writes=[self.xB[c][t.n]])

    def _tagged(self, suffix, fn, *a):
        old = self.S.cur_tag
        self.S.cur_tag = old.split(".")[0] + suffix
        try:
            return fn(*a)
        finally:
            self.S.cur_tag = old

    def pre_norm(self, t, gi):
        return self._tagged(".pre", self._pre_norm, t, gi)

    def post_norm(self, t, gi):
        return self._tagged(".post", self._post_norm, t, gi)

    def out_proj(self, *a):
        return self._tagged(".out", self._out_proj, *a)

    def _out_proj(self, keybase, src, srcB, gi_post, gi_pre_next):
        A = self.add
        pending = []

        def norms(t):
            self.post_norm(t, gi_post)
            if gi_pre_next is not None:
                self.l_next_pre(t, gi_pre_next)
        ws = [self.wget(keybase + (0,)), self.wget(keybase + (1,), cont=True)]
        for t in self.tiles:
            for m in range(8):
                wv, wB, _ = ws[m // 4]
                ml = m % 4
                bk, bkB = self.bank()
                for c in range(8):
                    self.mm(bk[:, 0:t.w], wv[:, c, ml * 128:(ml + 1) * 128], src[:, c, t.cols], c == 0, c == 7,
                            [wB, srcB[c][t.n]], bkB)
                A("act", lambda e, m=m, t=t, bk=bk: e.activation(out=self.h[:, m, t.cols], in_=bk[:, 0:t.w], func=ACTF.Copy),
                  reads=[bkB], writes=[self.hB[m][t.n]])
            pending.append(t)
            if len(pending) > 1:
                norms(pending.pop(0))
        while pending:
            norms(pending.pop(0))

    def l_next_pre(self, t, spec):
        l_save = self.l
        self.l = spec[0]
        self.pre_norm(t, spec[1])
        self.l = l_save

    def load_x(self):
        self.S.cur_tag = "load"
        A = self.add
        for t in self.tiles:
            for blk in range(t.w // 128):
                if t.kind == "P":
                    rows = self.xp[t.tok0 + blk * 128: t.tok0 + (blk + 1) * 128, :]
                else:
                    rows = self.xs[:, :]
                cs = slice(t.c0 + blk * 128, t.c0 + (blk + 1) * 128)

                def ev(c0, k, view, bkB, cs=cs, t=t):
                    A("act", lambda e: e.activation(out=self.x[:, c0:c0 + k, cs], in_=view, func=ACTF.Copy), reads=[bkB],
                      writes=[self.xB[c][t.n] for c in range(c0, c0 + k)])
                self.load_rows_T(rows, 128, D, ev)
        self.l = 0
        for t in self.tiles:
            self.pre_norm(t, 0)

    def store_x(self):
        self.S.cur_tag = "store"
        for t in self.tiles:
            for blk in range(t.w // 128):
                cs = slice(t.c0 + blk * 128, t.c0 + (blk + 1) * 128)
                srcs = [(self.x[:, c, cs], [self.xB[c][t.n]]) for c in range(8)]
                if t.kind == "P":
                    rows = self.yp[t.tok0 + blk * 128: t.tok0 + (blk + 1) * 128, :]
                else:
                    rows = self.ys[:, :]
                self.store_rows_T(srcs, 128, rows, evac_eng="act" if blk % 2 == 0 else "dve")

    def phase0_memkv(self):
        self.S.cur_tag = "p0"
        A = self.add
        big = self.big
        self.region_reset("big", [])
        mhat = big[:, 0:2048].rearrange("p (c k) -> p c k", c=8)
        mT = big[:, 2048:4096].rearrange("p (c k) -> p c k", c=8)
        vbf = big[:, 4096:6144].rearrange("p (k d) -> p k d", k=2)
        kTb = big[:, 6144:8192].rearrange("p (c k) -> p c k", c=8)
        mhatB, mTB, vbfB, kTbB = self.rbufs("big", 4)
        self.region_reset("scr", [])
        mraw = self.scr[:, 0:2048].rearrange("p (c k) -> p c k", c=8)
        mrawB = self.rbufs("scr", 1)[0]
        for kc in range(2):
            def ev(c0, k, view, bkB, kc=kc):
                A("dve", lambda e: e.tensor_copy(out=mraw[:, c0:c0 + k, kc * 128:(kc + 1) * 128], in_=view), reads=[bkB], writes=[mrawB])
            self.load_rows_T(self.memp[kc * 128:(kc + 1) * 128, :], 128, D, ev)
        rbk, rbkB, _, _ = self.stats_rstd(lambda c: (mraw[:, c, :], [mrawB]), NMEM, self.ones1024, RMS_EPS)
        r, rB = self.tmp[0], self.tmpB[0]
        A("act", lambda e: e.activation(out=r[:, 0:NMEM], in_=rbk[:, 0:NMEM], func=ACTF.Copy), reads=[rbkB], writes=[rB])
        for l in range(DEPTH):
            for c in range(8):
                A("dve", lambda e, c=c, l=l: e.scalar_tensor_tensor(out=mT[:, c, :], in0=mraw[:, c, :], scalar=self.c1024[:, c, l * 7 + 6:l * 7 + 7],
                                                                   in1=r[:, 0:NMEM], op0=ALU.mult, op1=ALU.mult),
                  reads=[mrawB, rB, self.cB], writes=[mTB])
            for which, dram_o in (("wk", self.o_mk_p), ("wv", self.o_mv_p)):
                for u in range(2):
                    wv, wB, _ = self.wget((which, l, u))
                    for kc in range(2):
                        bk, bkB = self.bank()
                        for c in range(8):
                            self.mm(bk[:, :], mT[:, c, kc * 128:(kc + 1) * 128], wv[:, c, :], c == 0, c == 7, [mTB, wB], bkB)
                        stg, stgB = self.next_stg()
                        A("act", lambda e, stg=stg, bk=bk: e.activation(out=stg[:, 0:512], in_=bk[:, :], func=ACTF.Copy),
                          reads=[bkB], writes=[stgB])
                        ob = Buf()
                        if "out" in self.cfg.get("p0", ("out", "kT", "vbf")):
                            A("sp", lambda e, stg=stg, kc=kc, u=u, dram_o=dram_o, l=l: e.dma_start(
                                out=dram_o[l, kc * 128:(kc + 1) * 128, u * 512:(u + 1) * 512], in_=stg[:, 0:512]),
                              reads=[stgB], writes=[ob], dma=True)
                            self.out_bufs.append(ob)
                        if which == "wv" and "vbf" in self.cfg.get("p0", ("out", "kT", "vbf")):
                            A("dve", lambda e, stg=stg, kc=kc, u=u: e.tensor_copy(out=vbf[:, kc, u * 512:(u + 1) * 512], in_=stg[:, 0:512]),
                              reads=[stgB], writes=[vbfB])
                    if which == "wk" and "kT" in self.cfg.get("p0", ("out", "kT", "vbf")):
                        for ml in range(4):
                            m = u * 4 + ml
                            bk, bkB = self.bank()
                            for c in range(8):
                                self.mm(bk[:, 0:256], wv[:, c, ml * 128:(ml + 1) * 128], mT[:, c, :], c == 0, c == 7, [mTB, wB], bkB)
                            A("dve", lambda e, bk=bk, m=m: e.tensor_copy(out=kTb[:, m, :], in_=bk[:, 0:256]), reads=[bkB], writes=[kTbB])
            s1, s2 = Buf(), Buf()
            if self.cfg.get("scr", True):
                A("sp", lambda e, l=l: e.dma_start(out=self.kT_scr[l], in_=big[:, 6144:8192]), reads=[kTbB], writes=[s1], dma=True)
                A("sp", lambda e, l=l: e.dma_start(out=self.v_scr[l], in_=big[:, 4096:6144]), reads=[vbfB], writes=[s2], dma=True)
            if l == 0:
                self.kscrB, self.vscrB = [], []
            self.kscrB.append(s1)
            self.vscrB.append(s2)

    def scrB0(self):
        if not hasattr(self, "_scrB0"):
            self._scrB0 = self.rbufs("scr", 1)[0]
        return self._scrB0

    def odd_mixer(self, o):
        self.S.cur_tag = "odd"
        A = self.add
        l, pi = self.l, self.pi
        big = self.big
        self.region_reset("big", [])
        self.region_reset("scr", [])
        EXT = 1186
        uext = big[:, 0:8 * EXT].rearrange("p (c t) -> p c t", c=8)
        ycv = big[:, 8 * EXT:8 * EXT + 8 * TCOLS].rearrange("p (c t) -> p c t", c=8)
        uB = [[b for b in self.rbufs("big", 3)] for _ in range(8)]
        uctxB = self.rbufs("big", 1)[0]
        ycvB = [[b for b in self.rbufs("big", 3)] for _ in range(8)]
        accs = [self.scr[:, 0:512], self.scr[:, 512:1024]]
        accB = self.rbufs("scr", 2)
        gw = lambda j, k: self.c1024[:, j, 28 + o * 3 + k:28 + o * 3 + k + 1]

        def ucols(j, t, shift=0):
            if t.kind == "P":
                return uext[:, j, t.c0 + shift:t.c0 + shift + t.w]
            return uext[:, j, 1026:1186].rearrange("p (b s) -> p b s", b=NB_S)[:, :, shift:shift + TS]

        if pi == 0:
            A("dve", lambda e: e.memset(uext[:, :, 0:2], 0.0), writes=[uctxB])
            def ev(c0, k, view, bkB):
                for j in range(c0, c0 + k):
                    dst = uext[:, j, 1026:1186].rearrange("p (b s) -> p b s", b=NB_S)[:, :, 0:2]
                    src = view[:, j - c0, :].rearrange("p (b r) -> p b r", b=NB_S)
                    A("dve", lambda e, dst=dst, src=src: e.tensor_copy(out=dst, in_=src), reads=[bkB], writes=[uctxB])
            self.load_rows_T(self.st_gconv[o].rearrange("b r d -> (b r) d"), 32, D, ev)
        else:
            A("dve", lambda e: e.tensor_copy(out=uext[:, :, 0:2], in_=self.car_g[o][:, :, :]), reads=[self.car_gB[o]], writes=[uctxB])

        def view3(ap, t):
            return ap if t.kind == "P" else ap.rearrange("p (b s) -> p b s", b=NB_S)

        for part, name in ((0, "xin"), (2, "gc"), (1, "gb")):
            for u in range(2):
                wv, wB, _ = self.wget(("win", pi, l, part, u))
                for ml in range(4):
                    j = u * 4 + ml
                    for t in self.tiles:
                        bk, bkB = self.bank()
                        for c in range(8):
                            self.mm(bk[:, 0:t.w], wv[:, c, ml * 128:(ml + 1) * 128], self.h[:, c, t.cols], c == 0, c == 7,
                                    [wB, self.hB[c][t.n]], bkB)
                        bv = view3(bk[:, 0:t.w], t)
                        if part == 0:
                            A("act", lambda e, j=j, t=t, bv=bv: e.activation(out=ucols(j, t, 2), in_=bv, func=ACTF.Copy),
                              reads=[bkB], writes=[uB[j][t.n]])
                        elif part == 2:
                            A("dve", lambda e, j=j, t=t, bv=bv: e.tensor_tensor(out=ucols(j, t, 2), in0=bv, in1=ucols(j, t, 2), op=ALU.mult),
                              reads=[bkB, uB[j][t.n]], writes=[uB[j][t.n]])
                        else:
                            i = (j * 3 + t.n) % 2
                            acc, aB = accs[i], accB[i]
                            av = view3(acc[:, 0:t.w], t)
                            rd = [uB[j][t.n], uctxB, self.cB] + ([uB[j][t.n - 1]] if (t.kind == "P" and t.n > 0) else [])
                            A("dve", lambda e, j=j, t=t, av=av: e.tensor_scalar(out=av, in0=ucols(j, t, 0), scalar1=gw(j, 0), scalar2=None,
                                                                               op0=ALU.mult), reads=rd, writes=[aB])
                            for k in (1, 2):
                                A("dve", lambda e, j=j, t=t, av=av, k=k: e.scalar_tensor_tensor(
                                    out=av, in0=ucols(j, t, k), scalar=gw(j, k), in1=av, op0=ALU.mult, op1=ALU.add),
                                  reads=rd + [aB], writes=[aB])
                            A("dve", lambda e, j=j, t=t, acc=acc, bk=bk: e.tensor_tensor(out=ycv[:, j, t.cols], in0=bk[:, 0:t.w],
                                                                                        in1=acc[:, 0:t.w], op=ALU.mult),
                              reads=[bkB, aB], writes=[ycvB[j][t.n]])
        last = self.tiles[-1]
        if pi == 0:
            A("act", lambda e: e.activation(out=self.car_g[o][:, :, :], in_=uext[:, :, 1024:1026], func=ACTF.Copy),
              reads=[uB[j][1] for j in range(8)], writes=[self.car_gB[o]])
            for j in range(8):
                src = uext[:, j, 1026:1186].rearrange("p (b s) -> p b s", b=NB_S)[:, :, 8:10].rearrange("p b r -> p r b")
                dst = self.cstate[:, j, 0:32].rearrange("p (r b) -> p r b", r=2)
                A("act", lambda e, src=src, dst=dst: e.activation(out=dst, in_=src, func=ACTF.Copy), reads=[uB[j][2]], writes=[self.cstateB])
            srcs = [(self.cstate[:, j, 0:32], [self.cstateB]) for j in range(8)]
            self.store_rows_T_multi(srcs, 32, [(r * 16, 16, self.o_gconv_s[o][:, r, :]) for r in range(2)])
        else:
            for j in range(8):
                A("act", lambda e, j=j: e.activation(out=self.cstate[:, j, 0:2], in_=uext[:, j, 1024:1026], func=ACTF.Copy),
                  reads=[uB[j][1]], writes=[self.cstateB])
            srcs = [(self.cstate[:, j, 0:2], [self.cstateB]) for j in range(8)]
            self.store_rows_T_multi(srcs, 2, [(0, 2, self.o_gconv_p[o])])
        self.out_proj(("wout", pi, l), ycv, ycvB, 1, (l, 2))

    def store_rows_T_multi(self, srcs, R, dsts):
        nch = len(srcs)
        for g0 in range(0, nch, 8):
            g = srcs[g0:g0 + 8]
            stg, stgB = self.next_stg()
            done = 0
            while done < len(g):
                k = min(4, len(g) - done)
                bk, bkB = self.bank()
                for j in range(k):
                    ap, bufs = g[done + j]
                    self.tr(bk[0:R, j * 128:(j + 1) * 128], ap, self.ident[:, :], list(bufs) + [self.constB], bkB)
                o = stg[0:R, done * 128:(done + k) * 128]
                i_ = bk[0:R, 0:k * 128]
                self.add("act", lambda e, o=o, i_=i_: e.activation(out=o, in_=i_, func=ACTF.Copy), reads=[bkB], writes=[stgB])
                done += k
            for (r0, nr, dap) in dsts:
                ob = Buf()
                self.add("sp", lambda e, r0=r0, nr=nr, dap=dap, stg=stg, g0=g0, ng=len(g): e.dma_start(
                    out=dap[:, g0 * 128:(g0 + ng) * 128], in_=stg[r0:r0 + nr, 0:ng * 128]), reads=[stgB], writes=[ob], dma=True)
                self.out_bufs.append(ob)

    def even_mixer(self, e_):
        self.S.cur_tag = "even"
        A = self.add
        l, pi = self.l, self.pi
        big = self.big
        self.region_reset("big", [])
        self.region_reset("scr", [])
        AE, UE = 1407, 1662
        aext = big[:, 0:4 * AE].rearrange("p (c t) -> p c t", c=4)
        uext = big[:, 4 * AE:4 * AE + 4 * UE].rearrange("p (c t) -> p c t", c=4)
        o0 = 4 * AE + 4 * UE
        ycat = big[:, o0:o0 + 8 * TCOLS].rearrange("p (c t) -> p c t", c=8)
        o1 = o0 + 8 * TCOLS
        cbb = big[:, o1:o1 + 2048].rearrange("p (c t) -> p c t", c=4)
        aB = [[b for b in self.rbufs("big", 3)] for _ in range(4)]
        actxB = self.rbufs("big", 1)[0]
        uB = [[b for b in self.rbufs("big", 3)] for _ in range(4)]
        uctxB = self.rbufs("big", 1)[0]
        ycatB = [[b for b in self.rbufs("big", 3)] for _ in range(8)]
        cbbB = self.rbufs("big", 4)
        scr = self.scr
        ping = [scr[:, 0:768], scr[:, 768:1536]]
        pingB = self.rbufs("scr", 2)
        lnt = [scr[:, 1536 + i * 512:1536 + (i + 1) * 512] for i in range(3)]
        lntB = self.rbufs("scr", 3)
        cst = lambda row: (lambda j: self.c512[:, j, row:row + 1])
        pscale = cst(e_)
        cw = lambda j, k: self.c512[:, j, 2 + e_ * 31 + k:2 + e_ * 31 + k + 1]
        cbias, lng, lnb = cst(64 + e_), cst(66 + e_), cst(68 + e_)

        def acols(g, t, lo, n):
            if t.kind == "P":
                return aext[:, g, 15 + t.c0 + lo:15 + t.c0 + lo + n]
            return aext[:, g, 1039:1407].rearrange("p (b s) -> p b s", b=NB_S)[:, :, 15 + lo:15 + lo + n]

        def ucols(j, t, lo, n):
            if t.kind == "P":
                return uext[:, j, 30 + t.c0 + lo:30 + t.c0 + lo + n]
            return uext[:, j, 1054:1662].rearrange("p (b s) -> p b s", b=NB_S)[:, :, 30 + lo:30 + lo + n]

        def view3(ap, t):
            return ap if t.kind == "P" else ap.rearrange("p (b s) -> p b s", b=NB_S)

        if pi == 0:
            A("dve", lambda e: e.memset(aext[:, :, 0:15], 0.0), writes=[actxB])
            A("dve", lambda e: e.memset(uext[:, :, 0:30], 0.0), writes=[uctxB])
            for (st, nr, ext, base, tot, ctxB) in ((self.st_pool[e_], 15, aext, 1039, 23, actxB), (self.st_conf[e_], 30, uext, 1054, 38, uctxB)):
                bper = 128 // nr
                for b0 in range(0, NB_S, bper):
                    nb = min(bper, NB_S - b0)
                    R = nb * nr

                    def ev(c0, k, view, bkB, ext=ext, base=base, tot=tot, nr=nr, b0=b0, nb=nb, ctxB=ctxB):
                        for j in range(c0, c0 + k):
                            dst = ext[:, j, base + b0 * tot:base + (b0 + nb) * tot].rearrange("p (b s) -> p b s", b=nb)[:, :, 0:nr]
                            src = view[:, j - c0, :].rearrange("p (b r) -> p b r", b=nb)
                            A("dve", lambda e, dst=dst, src=src: e.tensor_copy(out=dst, in_=src), reads=[bkB], writes=[ctxB])
                    self.load_rows_T(st[b0:b0 + nb].rearrange("b r d -> (b r) d"), R, 512, ev)
            for (st, ost, keep, nr) in ((self.st_pool[e_], self.o_pool_s[e_], 7, 15), (self.st_conf[e_], self.o_conf_s[e_], 22, 30)):
                ob = Buf()
                A("sp", lambda e, st=st, ost=ost, keep=keep, nr=nr: e.dma_start(
                    out=ost[:, 0:keep, :].rearrange("b r d -> b (r d)"), in_=st[:, nr - keep:nr, :].rearrange("b r d -> b (r d)")),
                  writes=[ob], dma=True)
                self.out_bufs.append(ob)
        else:
            A("dve", lambda e: e.tensor_copy(out=aext[:, :, 0:15], in_=self.car_a[e_][:, :, :]), reads=[self.car_aB[e_]], writes=[actxB])
            A("dve", lambda e: e.tensor_copy(out=uext[:, :, 0:30], in_=self.car_u[e_][:, :, :]), reads=[self.car_uB[e_]], writes=[uctxB])

        wa, waB, _ = self.wget(("win", pi, l, 0))
        for g in range(4):
            for t in self.tiles:
                bk, bkB = self.bank()
                for c in range(8):
                    self.mm(bk[:, 0:t.w], wa[:, c, g * 128:(g + 1) * 128], self.h[:, c, t.cols], c == 0, c == 7, [waB, self.hB[c][t.n]], bkB)
                A("act", lambda e, g=g, t=t, bk=bk: e.activation(out=acols(g, t, 0, TS if t.kind == "S" else t.w), in_=view3(bk[:, 0:t.w], t),
                                                                func=ACTF.Copy), reads=[bkB], writes=[aB[g][t.n]])
        wp, wpB, _ = self.wget(("wpool", pi, l))
        for g in range(4):
            wlen = POOLW[g]
            for t in self.tiles:
                n = TS if t.kind == "S" else t.w
                nseq = NB_S if t.kind == "S" else 1
                rd = [aB[g][t.n], actxB] + ([aB[g][t.n - 1]] if (t.kind == "P" and t.n > 0) else [])
                prev = lambda lo, cnt, g=g, t=t: acols(g, t, lo, cnt)
                prevB = rd
                s = 1
                k = 0
                while s < wlen:
                    lo = -(wlen - 2 * s)
                    cnt = n - lo
                    buf, bufB = ping[k % 2], pingB[k % 2]

                    def lvl(lo2, cnt2, buf=buf, lo=lo, cnt=cnt, nseq=nseq):
                        if nseq == 1:
                            return buf[:, lo2 - lo:lo2 - lo + cnt2]
                        return buf[:, 0:nseq * cnt].rearrange("p (b s) -> p b s", b=nseq)[:, :, lo2 - lo:lo2 - lo + cnt2]
                    A("dve", lambda e, lvl=lvl, prev=prev, lo=lo, cnt=cnt, s=s: e.tensor_tensor(
                        out=lvl(lo, cnt), in0=prev(lo, cnt), in1=prev(lo - s, cnt), op=ALU.add), reads=prevB, writes=[bufB])
                    prev, prevB = lvl, [bufB]
                    s *= 2
                    k += 1
                if t.kind == "P" and t.tok0 == 0:
                    A("dve", lambda e, prev=prev, g=g: e.tensor_tensor(out=prev(0, 15), in0=prev(0, 15), in1=self.poolfix[:, g, :], op=ALU.mult),
                      reads=prevB + [self.constB], writes=prevB)
                i = self.sl_rr % 2
                self.sl_rr += 1
                pl, plB = self.sl[i], self.slB[i]
                A("dve", lambda e, prev=prev, pl=pl, t=t, g=g, n=n, wlen=wlen: e.scalar_tensor_tensor(
                    out=view3(pl[:, 0:t.w], t), in0=prev(0, n), scalar=1.0 / wlen, in1=acols(g, t, 0, n), op0=ALU.mult, op1=ALU.subtract),
                  reads=prevB + [aB[g][t.n]], writes=[plB])
                bk, bkB = self.bank()
                self.mm(bk[:, 0:t.w], wp[:, g, :], pl[:, 0:t.w], True, True, [wpB, plB], bkB)
                A("act", lambda e, g=g, t=t, bk=bk: e.activation(out=ycat[:, g, t.cols], in_=bk[:, 0:t.w], func=ACTF.Copy, scale=pscale(g)),
                  reads=[bkB, self.cB], writes=[ycatB[g][t.n]])
        if pi == 0:
            A("act", lambda e: e.activation(out=self.car_a[e_][:, :, :], in_=aext[:, :, 1024:1039], func=ACTF.Copy),
              reads=[aB[g][1] for g in range(4)], writes=[self.car_aB[e_]])
            for g in range(4):
                src = aext[:, g, 1039:1407].rearrange("p (b s) -> p b s", b=NB_S)[:, :, 15:23].rearrange("p b r -> p r b")
                dst = self.cstate[:, g, :].rearrange("p (r b) -> p r b", r=TS)
                A("act", lambda e, src=src, dst=dst: e.activation(out=dst, in_=src, func=ACTF.Copy), reads=[aB[g][2]], writes=[self.cstateB])
            self.store_rows_T_multi([(self.cstate[:, g, :], [self.cstateB]) for g in range(4)], 128,
                                    [(r * 16, 16, self.o_pool_s[e_][:, 7 + r, :]) for r in range(TS)])
        else:
            for g in range(4):
                A("act", lambda e, g=g: e.activation(out=self.cstate[:, g, 0:15], in_=aext[:, g, 1024:1039], func=ACTF.Copy),
                  reads=[aB[g][1]], writes=[self.cstateB])
            self.store_rows_T_multi([(self.cstate[:, g, 0:15], [self.cstateB]) for g in range(4)], 15, [(0, 15, self.o_pool_p[e_])])

        w1, w1B, _ = self.wget(("win", pi, l, 1))
        w2, w2B, _ = self.wget(("win", pi, l, 2), cont=True)
        for j in range(4):
            for t in self.tiles:
                n = TS if t.kind == "S" else t.w
                bk2, bk2B = self.bank()
                for c in range(8):
                    self.mm(bk2[:, 0:t.w], w2[:, c, j * 128:(j + 1) * 128], self.h[:, c, t.cols], c == 0, c == 7, [w2B, self.hB[c][t.n]], bk2B)
                i = self.sl_rr % 2
                self.sl_rr += 1
                sg, sgB = self.sl[i], self.slB[i]
                A("act", lambda e, sg=sg, bk2=bk2, t=t: e.activation(out=sg[:, 0:t.w], in_=bk2[:, 0:t.w], func=ACTF.Sigmoid),
                  reads=[bk2B], writes=[sgB])
                bk1, bk1B = self.bank()
                for c in range(8):
                    self.mm(bk1[:, 0:t.w], w1[:, c, j * 128:(j + 1) * 128], self.h[:, c, t.cols], c == 0, c == 7, [w1B, self.hB[c][t.n]], bk1B)
                A("dve", lambda e, j=j, t=t, n=n, bk1=bk1, sg=sg: e.tensor_tensor(out=ucols(j, t, 0, n), in0=view3(bk1[:, 0:t.w], t),
                                                                                 in1=view3(sg[:, 0:t.w], t), op=ALU.mult),
                  reads=[bk1B, sgB], writes=[uB[j][t.n]])
        if pi == 0:
            A("act", lambda e: e.activation(out=self.car_u[e_][:, :, :], in_=uext[:, :, 1024:1054], func=ACTF.Copy),
              reads=[uB[j][1] for j in range(4)], writes=[self.car_uB[e_]])
            for j in range(4):
                src = uext[:, j, 1054:1662].rearrange("p (b s) -> p b s", b=NB_S)[:, :, 30:38].rearrange("p b r -> p r b")
                dst = self.cstate[:, 4 + j, :].rearrange("p (r b) -> p r b", r=TS)
                A("act", lambda e, src=src, dst=dst: e.activation(out=dst, in_=src, func=ACTF.Copy), reads=[uB[j][2]], writes=[self.cstateB])
            self.store_rows_T_multi([(self.cstate[:, 4 + j, :], [self.cstateB]) for j in range(4)], 128,
                                    [(r * 16, 16, self.o_conf_s[e_][:, 22 + r, :]) for r in range(TS)])
        else:
            for j in range(4):
                A("act", lambda e, j=j: e.activation(out=self.cstate[:, 4 + j, 0:30], in_=uext[:, j, 1024:1054], func=ACTF.Copy),
                  reads=[uB[j][1]], writes=[self.cstateB])
            self.store_rows_T_multi([(self.cstate[:, 4 + j, 0:30], [self.cstateB]) for j in range(4)], 30, [(0, 30, self.o_conf_p[e_])])

        self.region_reset("scr", [])
        scrb = scr[:, :].bitcast(BF16)
        cb = scrb[:, 0:4 * TCOLS].rearrange("p (c t) -> p c t", c=4)
        cbB = [[b for b in self.rbufs("scr", 3)] for _ in range(4)]
        dg0 = 4 * TCOLS
        assert dg0 + 31 * 128 <= 2 * self.SCRN
        diag = scrb[:, dg0:dg0 + 31 * 128].rearrange("p (k d) -> p k d", k=31)
        diagB = self.rbufs("scr", 1)[0]
        for j in range(4):
            for k in range(31):
                A("dve", lambda e, j=j, k=k: e.tensor_scalar(out=diag[:, k, :], in0=self.identb[:, :], scalar1=cw(j, k), scalar2=None,
                                                            op0=ALU.mult), reads=[self.constB, self.cB], writes=[diagB])
            for t in self.tiles:
                n = TS if t.kind == "S" else t.w
                bk, bkB = self.bank()
                rd = [diagB, uB[j][t.n], uctxB] + ([uB[j][t.n - 1]] if (t.kind == "P" and t.n > 0) else [])
                for k in range(31):
                    self.mm(view3(bk[:, 0:t.w], t), diag[:, k, :], ucols(j, t, k - 30, n), k == 0, k == 30, rd, bkB)
                A("act", lambda e, j=j, t=t, bk=bk: e.activation(out=cb[:, j, t.cols], in_=bk[:, 0:t.w], func=ACTF.Identity, bias=cbias(j), scale=1.0),
                  reads=[bkB, self.cB], writes=[cbB[j][t.n]])
        for t in self.tiles:
            w_ = t.w
            stA, stAB = self.stbank()
            stB_, stBB = self.stbank()
            for j in range(4):
                self.mm(stA[:, 0:w_], self.ones512[:, :], cb[:, j, t.cols], j == 0, j == 3, [cbB[j][t.n], self.constB], stAB)
                i = self.sq_rr % 4
                self.sq_rr += 1
                sq, sqB = self.sq[i], self.sqB[i]
                A("act", lambda e, j=j, t=t, sq=sq: e.activation(out=sq[:, 0:t.w], in_=cb[:, j, t.cols], func=ACTF.Square), reads=[cbB[j][t.n]], writes=[sqB])
                self.mm(stB_[:, 0:w_], self.ones512[:, :], sq[:, 0:w_], j == 0, j == 3, [sqB, self.constB], stBB)
            mean_sb, meanB = self.tmp[0], self.tmpB[0]
            m2, m2B = self.tmp[1], self.tmpB[1]
            A("act", lambda e, w_=w_, stA=stA: e.activation(out=mean_sb[0:1, 0:w_], in_=stA[0:1, 0:w_], func=ACTF.Copy), reads=[stAB], writes=[meanB])
            A("dve", lambda e, w_=w_: e.tensor_tensor(out=m2[0:1, 0:w_], in0=mean_sb[0:1, 0:w_], in1=mean_sb[0:1, 0:w_], op=ALU.mult), reads=[meanB], writes=[m2B])
            A("dve", lambda e, w_=w_, stB_=stB_: e.scalar_tensor_tensor(out=self.vsb[0:1, 0:w_], in0=stB_[0:1, 0:w_], scalar=LN_EPS, in1=m2[0:1, 0:w_],
                                                                        op0=ALU.add, op1=ALU.subtract), reads=[stBB, m2B], writes=[self.vsbB])
            A("act", lambda e, w_=w_: e.activation(out=self.vsb[0:1, 0:w_], in_=self.vsb[0:1, 0:w_], func=ACTF.Sqrt), reads=[self.vsbB], writes=[self.vsbB])
            r, rB, r1, r1B = self.row_pow_bcast(w_)
            A("dve", lambda e, w_=w_, r1=r1: e.scalar_tensor_tensor(out=m2[0:1, 0:w_], in0=mean_sb[0:1, 0:w_], scalar=-1.0, in1=r1[0:1, 0:w_],
                                                                    op0=ALU.mult, op1=ALU.mult), reads=[meanB, r1B], writes=[m2B])
            nm, nmB = self.bank()
            self.bcast_row(nm, nmB, m2, m2B, w_)
            for j in range(4):
                i2 = self.sl_rr % 2
                self.sl_rr += 1
                z, zB = self.sl[i2], self.slB[i2]
                i3 = self.tmp_rr % 2
                self.tmp_rr += 1
                A("dve", lambda e, j=j, t=t, w_=w_, z=z, r=r: e.tensor_tensor(out=z[:, 0:w_], in0=cb[:, j, t.cols], in1=r[:, 0:w_], op=ALU.mult),
                  reads=[cbB[j][t.n], rB], writes=[zB])
                A("dve", lambda e, w_=w_, z=z, nm=nm: e.tensor_tensor(out=z[:, 0:w_], in0=z[:, 0:w_], in1=nm[:, 0:w_], op=ALU.add),
                  reads=[zB, nmB], writes=[zB])
                A("act", lambda e, j=j, t=t, z=z, w_=w_: e.activation(out=ycat[:, 4 + j, t.cols], in_=z[:, 0:w_], func=ACTF.Silu,
                                                                     bias=lnb(j), scale=lng(j)), reads=[zB, self.cB], writes=[ycatB[4 + j][t.n]])
        self.out_proj(("wout", pi, l), ycat, ycatB, 1, (l, 2))

    def attention(self, l):
        self.S.cur_tag = "attn"
        A = self.add
        pi = self.pi
        big = self.big
        self.region_reset("big", [])
        qT = big[:, 0:9216].rearrange("p (c t) -> p c t", c=8)
        oT = big[:, 9216:18432].rearrange("p (c t) -> p c t", c=8)
        kst = big[:, 18432:20480].rearrange("p (k d) -> p k d", k=2)
        vst = big[:, 20480:22528].rearrange("p (k d) -> p k d", k=2)
        kT = big[:, 22528:24576].rearrange("p (c k) -> p c k", c=8)
        ets = big[:, 24576:24640]
        qB = [[b for b in self.rbufs("big", 3)] for _ in range(8)]
        oB = [[b for b in self.rbufs("big", 3)] for _ in range(8)]
        kstB, vstB, kTB, etsB = self.rbufs("big", 4)
        self.region_reset("scr", [])
        kst1 = self.scr[:, 0:1024].bitcast(BF16).rearrange("p (k d) -> p k d", k=2)
        vst1 = self.scr[:, 1024:2048].bitcast(BF16).rearrange("p (k d) -> p k d", k=2)
        kst1B, vst1B = self.rbufs("scr", 2)
        kv = [(kst, kstB, vst, vstB), (kst1, kst1B, vst1, vst1B)]
        A("sp", lambda e: e.dma_start(out=self.kTp[:, :, :].rearrange("p c k -> p (c k)"), in_=self.kT_scr[l]),
          reads=[self.kscrB[l]], writes=[self.kTpB], dma=True)
        A("sp", lambda e: e.dma_start(out=self.vp[:, :, :].rearrange("p k d -> p (k d)"), in_=self.v_scr[l]),
          reads=[self.vscrB[l]], writes=[self.vpB], dma=True)
        for u in range(2):
            wv, wB, _ = self.wget(("wq", pi, l, u))
            for ml in range(4):
                m = u * 4 + ml
                for t in self.tiles:
                    bk, bkB = self.bank()
                    for c in range(8):
                        self.mm(bk[:, 0:t.w], wv[:, c, ml * 128:(ml + 1) * 128], self.h[:, c, t.cols], c == 0, c == 7,
                                [wB, self.hB[c][t.n]], bkB)
                    A("act", lambda e, m=m, t=t, bk=bk: e.activation(out=qT[:, m, t.cols], in_=bk[:, 0:t.w], func=ACTF.Copy, scale=1.0 / 16.0),
                      reads=[bkB], writes=[qB[m][t.n]])
        for t in self.tiles:
            if t.kind == "P":
                def scores(hd, t=t):
                    i = self.et_rr % 2
                    self.et_rr += 1
                    et, etB = self.et[i], self.etB[i]
                    den, denB = self.stbank()
                    for kc in range(2):
                        bk, bkB = self.bank()
                        for dc in range(2):
                            m = 2 * hd + dc
                            self.mm(bk[:, 0:t.w], self.kTp[:, m, kc * 128:(kc + 1) * 128], qT[:, m, t.cols], dc == 0, dc == 1,
                                    [self.kTpB, qB[m][t.n]], bkB)
                        A("act", lambda e, kc=kc, et=et, bk=bk: e.activation(out=et[:, kc, 0:t.w], in_=bk[:, 0:t.w], func=ACTF.Exp),
                          reads=[bkB], writes=[etB])
                    for kc in range(2):
                        self.mm(den[:, 0:t.w], self.ones1[:, :], et[:, kc, 0:t.w], kc == 0, kc == 1, [etB, self.constB], denB)
                    return et, etB, den, denB

                def pv(hd, ctx, t=t):
                    et, etB, den, denB = ctx
                    i2 = self.rstd_rr % 2
                    self.rstd_rr += 1
                    rd_, rdB = self.rstd[i2], self.rstdB[i2]
                    A("dve", lambda e: e.reciprocal(out=rd_[:, 0:t.w], in_=den[:, 0:t.w]), reads=[denB], writes=[rdB])
                    for dc in range(2):
                        m = 2 * hd + dc
                        bk, bkB = self.bank()
                        for kc in range(2):
                            self.mm(bk[:, 0:t.w], self.vp[:, kc, m * 128:(m + 1) * 128], et[:, kc, 0:t.w], kc == 0, kc == 1,
                                    [self.vpB, etB], bkB)
                        A("dve", lambda e, m=m, bk=bk: e.tensor_tensor(out=oT[:, m, t.cols], in0=bk[:, 0:t.w], in1=rd_[:, 0:t.w],
                                                                      op=ALU.mult), reads=[bkB, rdB], writes=[oB[m][t.n]])
                ctxs = {0: scores(0)}
                for hd in range(4):
                    if hd + 1 < 4:
                        ctxs[hd + 1] = scores(hd + 1)
                    pv(hd, ctxs.pop(hd))
            else:
                for b in range(NB_S):
                    cs = slice(t.c0 + b * TS, t.c0 + (b + 1) * TS)
                    kst, kstB, vst, vstB = kv[b % 2]
                    A("pool", lambda e, b=b, kst=kst: e.dma_start(out=kst, in_=self.ck[l, b].rearrange("(k p) d -> p k d", p=128)),
                      writes=[kstB], dma=True)
                    A("pool", lambda e, b=b, vst=vst: e.dma_start(out=vst, in_=self.cv[l, b].rearrange("(k p) d -> p k d", p=128)),
                      writes=[vstB], dma=True)
                    for half in range(2):
                        bk, bkB = self.bank()
                        bkb = bk[:, :].bitcast(BF16)
                        for ml in range(4):
                            m = half * 4 + ml
                            for kc in range(2):
                                self.tr(bkb[:, ml * 256 + kc * 128: ml * 256 + (kc + 1) * 128], kst[:, kc, m * 128:(m + 1) * 128],
                                        self.identb[:, :], [kstB, self.constB], bkB)
                        A("act", lambda e, half=half, bkb=bkb: e.activation(out=kT[:, half * 4:(half + 1) * 4, :],
                                                                           in_=bkb.rearrange("p (c k) -> p c k", c=4), func=ACTF.Copy),
                          reads=[bkB], writes=[kTB])
                    sbk, sbkB = self.bank()
                    for hd in range(4):
                        for kc in range(2):
                            for dc in range(2):
                                m = 2 * hd + dc
                                self.mm(sbk[:, (hd * 2 + kc) * TS:(hd * 2 + kc + 1) * TS], kT[:, m, kc * 128:(kc + 1) * 128], qT[:, m, cs],
                                        dc == 0, dc == 1, [kTB, qB[m][t.n]], sbkB)
                    A("act", lambda e, sbk=sbk: e.activation(out=ets[:, 0:64], in_=sbk[:, 0:64], func=ACTF.Exp), reads=[sbkB], writes=[etsB])
                    den, denB = self.stbank()
                    e4 = ets[:, 0:64].rearrange("p (h k q) -> p h k q", h=4, k=2)
                    for kc in range(2):
                        self.mm(den[:, 0:32].rearrange("p (h q) -> p h q", h=4), self.ones1[:, :], e4[:, :, kc, :], kc == 0, kc == 1,
                                [etsB, self.constB], denB)
                    i2 = self.rstd_rr % 2
                    self.rstd_rr += 1
                    rd_, rdB = self.rstd[i2], self.rstdB[i2]
                    A("dve", lambda e, rd_=rd_, den=den: e.reciprocal(out=rd_[:, 0:32], in_=den[:, 0:32]), reads=[denB], writes=[rdB])
                    obk, obkB = self.bank()
                    for m in range(8):
                        hd = m // 2
                        for kc in range(2):
                            self.mm(obk[:, m * TS:(m + 1) * TS], vst[:, kc, m * 128:(m + 1) * 128],
                                    ets[:, (hd * 2 + kc) * TS:(hd * 2 + kc + 1) * TS], kc == 0, kc == 1, [vstB, etsB], obkB)
                    o4 = obk[:, 0:64].rearrange("p (h d q) -> p h d q", h=4, d=2)
                    for dc in range(2):
                        dst = oT[:, :, cs].rearrange("p (h d) q -> p h d q", d=2)[:, :, dc, :]
                        A("dve", lambda e, dst=dst, dc=dc, o4=o4, rd_=rd_: e.tensor_tensor(
                            out=dst, in0=o4[:, :, dc, :], in1=rd_[:, 0:32].rearrange("p (h q) -> p h q", h=4), op=ALU.mult),
                          reads=[obkB, rdB], writes=[oB[2 * hh + dc][t.n] for hh in range(4)])
        self.out_proj(("wo", pi, l), oT, oB, 3, (l, 4))

    def ffn(self, l):
        self.S.cur_tag = "ffn"
        A = self.add
        pi = self.pi
        big = self.big
        self.region_reset("big", [])
        self.region_reset("scr", [])
        act = big[:, 0:22 * TCOLS].rearrange("p (c t) -> p c t", c=22)
        actB = [[b for b in self.rbufs("big", 3)] for _ in range(22)]
        scr = self.scr
        GE = 1186
        gext = scr[:, 0:GE]
        gextB = self.rbufs("scr", 3)
        gctxB = self.rbufs("scr", 1)[0]
        accs = [scr[:, 1536:2048], scr[:, 2048:2560], scr[:, 2560:3072]]
        accB = self.rbufs("scr", 3)
        fw = lambda m, k: self.c2816[:, m, l * 3 + k:l * 3 + k + 1]
        fb = lambda m: self.c2816[:, m, 12 + l:12 + l + 1]

        def gcols(t, shift):
            if t.kind == "P":
                return gext[:, t.c0 + shift:t.c0 + shift + t.w]
            return gext[:, 1026:1186].rearrange("p (b s) -> p b s", b=NB_S)[:, :, shift:shift + TS]

        def view3(ap, t):
            return ap if t.kind == "P" else ap.rearrange("p (b s) -> p b s", b=NB_S)

        if pi == 0:
            for c0 in range(0, DFF, 1024):
                cw_ = min(1024, DFF - c0)

                def ev(cc, k, view, bkB, base=c0 // 128):
                    A("dve", lambda e: e.tensor_copy(out=self.fctx[:, base + cc:base + cc + k, :], in_=view), reads=[bkB], writes=[self.fctxB])
                self.load_rows_T(self.st_ffn[l].rearrange("b r d -> (b r) d")[:, c0:c0 + cw_], 32, cw_, ev)
        for u in range(6):
            wg, wgB, cwid = self.wget(("wg", pi, l, u))
            wu, wuB, _ = self.wget(("wu", pi, l, u), cont=True)
            for ml in range(cwid // 128):
                m = u * 4 + ml
                if pi == 0:
                    A("pool", lambda e: e.tensor_copy(out=gext[:, 0:2], in_=self.zeros[:, 0:2]), reads=[self.constB], writes=[gctxB])
                    A("pool", lambda e, m=m: e.tensor_copy(out=gext[:, 1026:1186].rearrange("p (b s) -> p b s", b=NB_S)[:, :, 0:2],
                                                          in_=self.fctx[:, m, :].rearrange("p (b r) -> p b r", b=NB_S)),
                      reads=[self.fctxB], writes=[gctxB])
                else:
                    A("pool", lambda e, m=m: e.tensor_copy(out=gext[:, 0:2], in_=self.car_f[l][:, m, :]), reads=[self.car_fB[l]], writes=[gctxB])
                T_ = self.tiles
                bkg_l, bku_l = [], []
                for t in T_:
                    bkg, bkgB = self.bank()
                    for c in range(8):
                        self.mm(bkg[:, 0:t.w], wg[:, c, ml * 128:(ml + 1) * 128], self.h[:, c, t.cols], c == 0, c == 7, [wgB, self.hB[c][t.n]], bkgB)
                    A("act", lambda e, t=t, bkg=bkg: e.activation(out=gcols(t, 2), in_=view3(bkg[:, 0:t.w], t), func=ACTF.Copy),
                      reads=[bkgB], writes=[gextB[t.n]])
                    acc, aB = accs[t.n], accB[t.n]
                    A("act", lambda e, m=m, t=t, bkg=bkg, acc=acc: e.activation(out=acc[:, 0:t.w], in_=bkg[:, 0:t.w], func=ACTF.Identity,
                                                                              bias=fb(m), scale=fw(m, 2)), reads=[bkgB, self.cB], writes=[aB])
                    bkg_l.append((bkg, bkgB))
                for t in T_:
                    bku, bkuB = self.bank()
                    for c in range(8):
                        self.mm(bku[:, 0:t.w], wu[:, c, ml * 128:(ml + 1) * 128], self.h[:, c, t.cols], c == 0, c == 7, [wuB, self.hB[c][t.n]], bkuB)
                    bku_l.append((bku, bkuB))
                for k in (0, 1):
                    for t in T_:
                        acc, aB = accs[t.n], accB[t.n]
                        av = view3(acc[:, 0:t.w], t)
                        rd = [gextB[t.n], gctxB, self.cB] + ([gextB[t.n - 1]] if (t.kind == "P" and t.n > 0) else [])
                        A("dve", lambda e, m=m, t=t, av=av, k=k: e.scalar_tensor_tensor(out=av, in0=gcols(t, k), scalar=fw(m, k), in1=av,
                                                                                       op0=ALU.mult, op1=ALU.add), reads=rd + [aB], writes=[aB])
                sls = []
                for t in T_:
                    acc, aB = accs[t.n], accB[t.n]
                    i2 = self.sq_rr % 4
                    self.sq_rr += 1
                    sl, slB = self.sq[i2], self.sqB[i2]
                    A("act", lambda e, t=t, acc=acc, sl=sl: e.activation(out=sl[:, 0:t.w], in_=acc[:, 0:t.w], func=ACTF.Silu), reads=[aB], writes=[slB])
                    sls.append((sl, slB))
                for ti, t in enumerate(T_):
                    bku, bkuB = bku_l[ti]
                    sl, slB = sls[ti]
                    A("dve", lambda e, m=m, t=t, bku=bku, sl=sl: e.tensor_tensor(out=act[:, m, t.cols], in0=bku[:, 0:t.w], in1=sl[:, 0:t.w], op=ALU.mult),
                      reads=[bkuB, slB], writes=[actB[m][t.n]])
                if pi == 0:
                    A("pool", lambda e, m=m: e.tensor_copy(out=self.car_f[l][:, m, :], in_=gext[:, 1024:1026]), reads=[gextB[1]], writes=[self.car_fB[l]])
                    src = gext[:, 1026:1186].rearrange("p (b s) -> p b s", b=NB_S)[:, :, 8:10].rearrange("p b r -> p r b")
                    A("pool", lambda e, m=m, src=src: e.tensor_copy(out=self.fstate[:, m, :].rearrange("p (r b) -> p r b", r=2), in_=src),
                      reads=[gextB[2]], writes=[self.fstateB])
                else:
                    A("pool", lambda e, m=m: e.tensor_copy(out=self.fstate[:, m, 0:2], in_=gext[:, 1024:1026]), reads=[gextB[1]], writes=[self.fstateB])
        if pi == 0:
            self.store_rows_T_multi([(self.fstate[:, m, :], [self.fstateB]) for m in range(22)], 32,
                                    [(r * 16, 16, self.o_ffn_s[l][:, r, :]) for r in range(2)])
        else:
            self.store_rows_T_multi([(self.fstate[:, m, 0:2], [self.fstateB]) for m in range(22)], 2, [(0, 2, self.o_ffn_p[l])])
        for ch in range(2):
            ws = [self.wget(("wd", pi, l, ch, kg), cont=(kg > 0)) for kg in range(3)]
            for ml in range(4):
                m = ch * 4 + ml
                for t in self.tiles:
                    bk, bkB = self.bank()
                    kk = 0
                    for kg, (k0, kn) in enumerate(((0, 8), (8, 8), (16, 6))):
                        wv, wB, _ = ws[kg]
                        for c in range(kn):
                            self.mm(bk[:, 0:t.w], wv[:, c, ml * 128:(ml + 1) * 128], act[:, k0 + c, t.cols], kk == 0, kk == 21,
                                    [wB, actB[k0 + c][t.n]], bkB)
                            kk += 1
                    A("act", lambda e, m=m, t=t, bk=bk: e.activation(out=self.h[:, m, t.cols], in_=bk[:, 0:t.w], func=ACTF.Copy),
                      reads=[bkB], writes=[self.hB[m][t.n]])
        nxt = (l + 1, 0) if l + 1 < DEPTH else None
        for t in self.tiles:
            self.post_norm(t, 5)
            if nxt is not None:
                self.l_next_pre(t, nxt)


_NC_CACHE = {}


def _get_nc():
    if "nc" not in _NC_CACHE:
        _NC_CACHE["nc"] = Builder().build()
    return _NC_CACHE["nc"]


def kernel(x_prompt, x_sample, state_pool, state_conf, state_gconv, state_ffn, cache_mem_k, cache_mem_v,
           mem_prompt, norm_gains, w_in_even, w_pool, pool_scale, conf_w, conf_b, conf_ln_g, conf_ln_b,
           w_out_even, w_in_odd, gconv_w, w_out_odd, w_mem_q, w_mem_k, w_mem_v, w_mem_o,
           w_ffn_gate, w_ffn_up, ffn_conv_w, ffn_conv_b, w_ffn_down):
    f = lambda a: np.ascontiguousarray(np.asarray(a, dtype=np.float32))
    shared = {
        "norm_gains": f(norm_gains).reshape(DEPTH * 7, D), "w_in_even": f(w_in_even), "w_pool": f(w_pool),
        "pool_scale": f(pool_scale), "conf_w": f(conf_w).reshape(62, 512), "conf_b": f(conf_b),
        "conf_ln_g": f(conf_ln_g), "conf_ln_b": f(conf_ln_b), "w_out_even": f(w_out_even), "w_in_odd": f(w_in_odd),
        "gconv_w": f(gconv_w).reshape(6, D), "w_out_odd": f(w_out_odd), "w_mem_q": f(w_mem_q), "w_mem_k": f(w_mem_k),
        "w_mem_v": f(w_mem_v), "w_mem_o": f(w_mem_o), "w_ffn_gate": f(w_ffn_gate), "w_ffn_up": f(w_ffn_up),
        "ffn_conv_w": f(ffn_conv_w).reshape(DEPTH * 3, DFF), "ffn_conv_b": f(ffn_conv_b), "w_ffn_down": f(w_ffn_down),
    }
    x_prompt, x_sample = f(x_prompt), f(x_sample)
    state_pool, state_conf, state_gconv, state_ffn = f(state_pool), f(state_conf), f(state_gconv), f(state_ffn)
    cache_mem_k, cache_mem_v, mem_prompt = f(cache_mem_k), f(cache_mem_v), f(mem_prompt)
    in_maps = []
    for c in range(NCORES):
        bs = slice(c * NB_S, (c + 1) * NB_S)
        m = dict(shared)
        m["xp"] = x_prompt[c]
        m["xs"] = x_sample[bs].reshape(NB_S * TS, D)
        m["st_pool"] = state_pool[:, bs]
        m["st_conf"] = state_conf[:, bs]
        m["st_gconv"] = state_gconv[:, bs]
        m["st_ffn"] = state_ffn[:, bs]
        m["ck"] = cache_mem_k[:, bs].reshape(DEPTH, NB_S, NMEM, D)
        m["cv"] = cache_mem_v[:, bs].reshape(DEPTH, NB_S, NMEM, D)
        m["memp"] = mem_prompt[c]
        in_maps.append({k: np.ascontiguousarray(v) for k, v in m.items()})
    nc = _get_nc()
    res = run_bass_kernel_spmd(nc, in_maps, core_ids=list(range(NCORES)))
    R = res.results
    cat = lambda k, ax: np.concatenate([np.asarray(r[k]) for r in R], axis=ax)
    stack = lambda k, ax: np.stack([np.asarray(r[k]) for r in R], axis=ax)
    y_prompt = stack("yp", 0)
    y_sample = cat("ys", 0).reshape(NCORES * NB_S, TS, D)
    pool_p = stack("o_pool_p", 1)
    conf_p = stack("o_conf_p", 1)
    gconv_p = stack("o_gconv_p", 1)
    ffn_p = stack("o_ffn_p", 1)
    mk_p = stack("o_mk_p", 1).reshape(DEPTH, NCORES, NMEM, 4, 256)
    mv_p = stack("o_mv_p", 1).reshape(DEPTH, NCORES, NMEM, 4, 256)
    pool_s = cat("o_pool_s", 1)
    conf_s = cat("o_conf_s", 1)
    gconv_s = cat("o_gconv_s", 1)
    ffn_s = cat("o_ffn_s", 1)
    return (y_prompt, y_sample, pool_p, conf_p, gconv_p, ffn_p, mk_p, mv_p, pool_s, conf_s, gconv_s, ffn_s)
```

```python
from contextlib import ExitStack

import numpy as np
import concourse.bass as bass
import concourse.mybir as mybir
from concourse.bass_utils import run_bass_kernel_spmd

F32 = mybir.dt.float32
BF16 = mybir.dt.bfloat16
ALU = mybir.AluOpType
ACTF = mybir.ActivationFunctionType

NCORES = 8
D = 1024
SEQ = 2048
DEPTH = 4
NB_S = 16
TS = 8
DFF = 2816
NMEM = 256
TCOLS = 1152
RMS_EPS = 1e-6
LN_EPS = 1e-5
POOLW = (2, 4, 8, 16)


class Buf:
    __slots__ = ("w", "r")

    def __init__(self):
        self.w = None
        self.r = {}


class Op:
    __slots__ = ("eng", "fn", "deps", "waited", "sig", "is_dma", "waits", "clock", "idx", "tag")


class Sched:
    def __init__(self, npool=12):
        self.streams = {k: [] for k in ("pe", "act", "dve", "pool", "sp")}
        self.order = []
        self.npool = npool
        self.hist = {k: [] for k in self.streams}

    def add(self, eng, fn, reads=(), writes=(), dma=False):
        op = Op()
        op.eng = eng
        op.fn = fn
        op.is_dma = dma
        op.waited = dma
        op.sig = None
        op.idx = len(self.order)
        op.tag = getattr(self, "cur_tag", "")
        deps = {}
        for b in reads:
            w = b.w
            if w is not None and (w.is_dma or w.eng != eng or eng != "pe"):
                deps[id(w)] = w
        for b in writes:
            w = b.w
            if w is not None and (w.is_dma or w.eng != eng or eng != "pe"):
                deps[id(w)] = w
            for o in b.r.values():
                if o.is_dma or o.eng != eng or eng != "pe":
                    deps[id(o)] = o
        if dma:
            hist = self.hist[eng]
            if len(hist) >= self.npool:
                prev = hist[len(hist) - self.npool]
                deps[id(prev)] = prev
            hist.append(op)
        op.deps = list(deps.values())
        for b in reads:
            b.r[id(op) if dma else eng] = op
        for b in writes:
            b.w = op
            b.r = {}
        self.streams[eng].append(op)
        self.order.append(op)
        return op

    def finalize(self, sems, dma_sems):
        for op in self.order:
            for d in op.deps:
                d.waited = True
        for eng in self.streams:
            cnt = 0
            dcnt = 0
            uses = [0] * self.npool
            for op in self.streams[eng]:
                if op.is_dma:
                    k = dcnt % self.npool
                    uses[k] += 1
                    op.sig = (("d", eng, k), 16 * uses[k])
                    dcnt += 1
                elif op.waited:
                    cnt += 1
                    op.sig = (("e", eng), cnt)
        seen = {eng: {} for eng in self.streams}
        for op in self.order:
            s = seen[op.eng]
            op.waits = []
            for d in sorted(op.deps, key=lambda d: -d.sig[1]):
                key, v = d.sig
                if s.get(key, 0) < v:
                    op.waits.append((key, v))
                    for k2, v2 in d.clock.items():
                        if s.get(k2, 0) < v2:
                            s[k2] = v2
            if op.waited:
                c = dict(s)
                c[op.sig[0]] = op.sig[1]
                op.clock = c
            else:
                op.clock = None

        def handle(key):
            if key[0] == "e":
                return sems[key[1]]
            return dma_sems[key[1]][key[2]]

        def emit(name):
            def body(e):
                for op in self.streams[name]:
                    for key, v in op.waits:
                        e.wait_ge(handle(key), v)
                    ins = op.fn(e)
                    if op.waited:
                        ins.then_inc(handle(op.sig[0]), 16 if op.is_dma else 1)
            return body
        return emit


def merged_hazards(bufs):
    out = {}
    for b in bufs:
        items = list(b.r.items())
        if b.w is not None:
            items.append((id(b.w) if b.w.is_dma else b.w.eng, b.w))
        for k, o in items:
            if k not in out or out[k].idx < o.idx:
                out[k] = o
    return out


class Tile:
    def __init__(self, n, col0, w, kind, tok0):
        self.n, self.c0, self.w, self.kind, self.tok0 = n, col0, w, kind, tok0
        self.cols = slice(col0, col0 + w)


class Builder:
    def __init__(self):
        self.nc = bass.Bass("TRN2", target_bir_lowering=False)
        self.S = Sched()
        self.es = ExitStack()
        self.out_bufs = []
        self.bank_rr = 0
        self.st_rr = 0
        self.region_bufs = {"big": [], "scr": []}
        self.cfg = dict(layers=DEPTH, passes=2, sub=("m", "a", "f"), phase0=True)

    def sb(self, name, shape, dt):
        return self.es.enter_context(self.nc.sbuf_tensor(name, shape, dt))

    def dram_in(self, name, shape):
        return self.nc.dram_tensor(name, list(shape), F32, kind="ExternalInput").ap()

    def dram_out(self, name, shape):
        return self.nc.dram_tensor(name, list(shape), F32, kind="ExternalOutput").ap()

    def rbufs(self, region, n):
        hz = merged_hazards(self.region_bufs[region])
        bs = []
        for _ in range(n):
            b = Buf()
            b.r = dict(hz)
            bs.append(b)
        self.region_bufs[region] = self.region_bufs[region] + bs
        return bs

    def region_reset(self, region, keep):
        hz = merged_hazards(self.region_bufs[region])
        carrier = Buf()
        carrier.r = hz
        self.region_bufs[region] = [carrier] + list(keep)

    def bank(self):
        i = self.bank_rr % 6
        self.bank_rr += 1
        return self.banks[i], self.bankB[i]

    def stbank(self):
        i = 6 + self.st_rr % 2
        self.st_rr += 1
        return self.banks[i], self.bankB[i]

    def add(self, *a, **k):
        return self.S.add(*a, **k)

    def build(self):
        nc = self.nc
        A = self.add
        di = self.dram_in
        self.xp = di("xp", [SEQ, D])
        self.xs = di("xs", [NB_S * TS, D])
        self.st_pool = di("st_pool", [2, NB_S, 15, 512])
        self.st_conf = di("st_conf", [2, NB_S, 30, 512])
        self.st_gconv = di("st_gconv", [2, NB_S, 2, D])
        self.st_ffn = di("st_ffn", [DEPTH, NB_S, 2, DFF])
        self.ck = di("ck", [DEPTH, NB_S, NMEM, D])
        self.cv = di("cv", [DEPTH, NB_S, NMEM, D])
        self.memp = di("memp", [NMEM, D])
        self.norm_gains = di("norm_gains", [DEPTH * 7, D])
        self.w_in_even = di("w_in_even", [2, D, 1536])
        self.w_pool = di("w_pool", [2, 4, 128, 128])
        self.pool_scale = di("pool_scale", [2, 512])
        self.conf_w = di("conf_w", [2 * 31, 512])
        self.conf_b = di("conf_b", [2, 512])
        self.conf_ln_g = di("conf_ln_g", [2, 512])
        self.conf_ln_b = di("conf_ln_b", [2, 512])
        self.w_out_even = di("w_out_even", [2, D, D])
        self.w_in_odd = di("w_in_odd", [2, D, 3 * D])
        self.gconv_w = di("gconv_w", [2 * 3, D])
        self.w_out_odd = di("w_out_odd", [2, D, D])
        self.w_mem_q = di("w_mem_q", [DEPTH, D, D])
        self.w_mem_k = di("w_mem_k", [DEPTH, D, D])
        self.w_mem_v = di("w_mem_v", [DEPTH, D, D])
        self.w_mem_o = di("w_mem_o", [DEPTH, D, D])
        self.w_ffn_gate = di("w_ffn_gate", [DEPTH, D, DFF])
        self.w_ffn_up = di("w_ffn_up", [DEPTH, D, DFF])
        self.ffn_conv_w = di("ffn_conv_w", [DEPTH * 3, DFF])
        self.ffn_conv_b = di("ffn_conv_b", [DEPTH, DFF])
        self.w_ffn_down = di("w_ffn_down", [DEPTH, DFF, D])
        do = self.dram_out
        self.yp = do("yp", [SEQ, D])
        self.ys = do("ys", [NB_S * TS, D])
        self.o_pool_p = do("o_pool_p", [2, 15, 512])
        self.o_conf_p = do("o_conf_p", [2, 30, 512])
        self.o_gconv_p = do("o_gconv_p", [2, 2, D])
        self.o_ffn_p = do("o_ffn_p", [DEPTH, 2, DFF])
        self.o_mk_p = do("o_mk_p", [DEPTH, NMEM, D])
        self.o_mv_p = do("o_mv_p", [DEPTH, NMEM, D])
        self.o_pool_s = do("o_pool_s", [2, NB_S, 15, 512])
        self.o_conf_s = do("o_conf_s", [2, NB_S, 30, 512])
        self.o_gconv_s = do("o_gconv_s", [2, NB_S, 2, D])
        self.o_ffn_s = do("o_ffn_s", [DEPTH, NB_S, 2, DFF])
        self.kT_scr = nc.dram_tensor("kT_scr", [DEPTH, 128, 8 * NMEM], BF16, kind="Internal").ap()
        self.v_scr = nc.dram_tensor("v_scr", [DEPTH, 128, 2 * D], BF16, kind="Internal").ap()

        sb = self.sb
        self.x = sb("x", [128, 8, TCOLS], F32)
        self.h = sb("h", [128, 8, TCOLS], BF16)
        self.BIGN = 25344
        self.big = sb("big", [128, self.BIGN], BF16)
        self.SCRN = 4352
        self.scr = sb("scr", [128, self.SCRN], F32)
        self.ring = [sb(f"ring{i}", [128, 4096], BF16) for i in range(4)]
        self.ringB = [Buf() for _ in range(4)]
        self.stg = [sb(f"stg{i}", [128, 1024], F32) for i in range(2)]
        self.stgB = [Buf() for _ in range(2)]
        self.stg_rr = 0
        self.c1024 = sb("c1024", [128, 8, 34], F32)
        self.c512 = sb("c512", [128, 4, 70], F32)
        self.c2816 = sb("c2816", [128, 22, 16], F32)
        self.cB = Buf()
        self.ident = sb("ident", [128, 128], F32)
        self.identb = sb("identb", [128, 128], BF16)
        self.ones1024 = sb("ones1024", [128, 128], BF16)
        self.ones512 = sb("ones512", [128, 128], BF16)
        self.ones1 = sb("ones1", [128, 128], BF16)
        self.onesf = sb("onesf", [128, 128], F32)
        self.zeros = sb("zeros", [128, 64], F32)
        self.poolfix = sb("poolfix", [128, 4, 15], F32)
        self.constB = Buf()
        self.vsb = sb("vsb", [128, 512], F32)
        self.vsbB = Buf()
        self.rstd = [sb(f"rstd{i}", [128, 512], F32) for i in range(2)]
        self.rstdB = [Buf() for _ in range(2)]
        self.rstd_rr = 0
        self.tmp = [sb(f"tmp{i}", [128, 512], F32) for i in range(2)]
        self.tmpB = [Buf() for _ in range(2)]
        self.tmp_rr = 0
        self.sq = [sb(f"sq{i}", [128, 512], BF16) for i in range(4)]
        self.sqB = [Buf() for _ in range(4)]
        self.sq_rr = 0
        self.et = [sb(f"et{i}", [128, 2, 512], BF16) for i in range(2)]
        self.etB = [Buf() for _ in range(2)]
        self.et_rr = 0
        self.sl = [sb(f"sl{i}", [128, 512], BF16) for i in range(2)]
        self.slB = [Buf() for _ in range(2)]
        self.sl_rr = 0
        self.fstate = sb("fstate", [128, 22, 32], F32)
        self.fstateB = Buf()
        self.fctx = sb("fctx", [128, 22, 32], F32)
        self.fctxB = Buf()
        self.cstate = sb("cstate", [128, 8, 128], F32)
        self.cstateB = Buf()
        self.kTp = sb("kTp", [128, 8, NMEM], BF16)
        self.kTpB = Buf()
        self.vp = sb("vp", [128, 2, D], BF16)
        self.vpB = Buf()
        self.car_a = [sb(f"car_a{e}", [128, 4, 15], BF16) for e in range(2)]
        self.car_u = [sb(f"car_u{e}", [128, 4, 30], BF16) for e in range(2)]
        self.car_g = [sb(f"car_g{o}", [128, 8, 2], BF16) for o in range(2)]
        self.car_f = [sb(f"car_f{l}", [128, 22, 2], F32) for l in range(DEPTH)]
        self.car_aB = [Buf() for _ in range(2)]
        self.car_uB = [Buf() for _ in range(2)]
        self.car_gB = [Buf() for _ in range(2)]
        self.car_fB = [Buf() for _ in range(DEPTH)]
        self.banks = [self.es.enter_context(nc.psum_tensor(f"bk{i}", [128, 512], F32)) for i in range(8)]
        self.bankB = [Buf() for _ in range(8)]
        self.xB = [[Buf() for _ in range(3)] for _ in range(8)]
        self.hB = [[Buf() for _ in range(3)] for _ in range(8)]

        sems = {k: self.es.enter_context(nc.semaphore("s_" + k)) for k in self.S.streams}
        dsems = {k: [self.es.enter_context(nc.semaphore(f"d_{k}_{i}")) for i in range(self.S.npool)]
                 for k in ("sp", "pool")}

        self.plan = []
        self.w_issued = 0
        self.w_consumed = 0
        self.make_plan()

        self.setup_consts()
        if self.cfg["phase0"]:
            self.phase0_memkv()
        passes = [
            [Tile(0, 0, 512, "P", 0), Tile(1, 512, 512, "P", 512), Tile(2, 1024, 128, "S", 0)],
            [Tile(0, 0, 512, "P", 1024), Tile(1, 512, 512, "P", 1536)],
        ]
        for pi, tiles in enumerate(passes[:self.cfg["passes"]]):
            self.pi = pi
            self.tiles = tiles
            self.load_x()
            for l in range(self.cfg["layers"]):
                self.l = l
                if "m" in self.cfg["sub"]:
                    if l % 2 == 0:
                        self.even_mixer(l // 2)
                    else:
                        self.odd_mixer(l // 2)
                if "a" in self.cfg["sub"]:
                    self.attention(l)
                if "f" in self.cfg["sub"]:
                    self.ffn(l)
            self.store_x()
        assert self.w_consumed == len(self.plan), (self.w_consumed, len(self.plan))
        A("sp", lambda e: e.nop(), reads=self.out_bufs)

        emit = self.S.finalize(sems, dsems)
        with nc.Block() as block:
            block.tensor(emit("pe"))
            block.scalar(emit("act"))
            block.vector(emit("dve"))
            block.gpsimd(emit("pool"))
            block.sync(emit("sp"))
        self.es.close()
        return nc

    def make_plan(self):
        P = self.plan

        def mat(key, w, kch, ncols, csz=512):
            for u, c0 in enumerate(range(0, ncols, csz)):
                cw = min(csz, ncols - c0)
                P.append((key + (u,), w[:, c0:c0 + cw], kch, cw))

        cfg = self.cfg
        for l in range(DEPTH if cfg["phase0"] else 0):
            mat(("wk", l), self.w_mem_k[l], 8, D)
            mat(("wv", l), self.w_mem_v[l], 8, D)
        for pi in range(cfg["passes"]):
            for l in range(cfg["layers"]):
                if "m" not in cfg["sub"]:
                    pass
                elif l % 2 == 0:
                    e = l // 2
                    wie = self.w_in_even[e]
                    P.append((("win", pi, l, 0), wie[:, 0:512], 8, 512))
                    P.append((("wpool", pi, l), self.w_pool[e], None, None))
                    P.append((("win", pi, l, 1), wie[:, 512:1024], 8, 512))
                    P.append((("win", pi, l, 2), wie[:, 1024:1536], 8, 512))
                    mat(("wout", pi, l), self.w_out_even[e], 8, D)
                else:
                    o = l // 2
                    wi = self.w_in_odd[o]
                    for part in (0, 2, 1):
                        mat(("win", pi, l, part), wi[:, part * D:(part + 1) * D], 8, D)
                    mat(("wout", pi, l), self.w_out_odd[o], 8, D)
                if "a" in cfg["sub"]:
                    mat(("wq", pi, l), self.w_mem_q[l], 8, D)
                    mat(("wo", pi, l), self.w_mem_o[l], 8, D)
                if "f" not in cfg["sub"]:
                    continue
                for u, c0 in enumerate(range(0, DFF, 512)):
                    cw = min(512, DFF - c0)
                    P.append((("wg", pi, l, u), self.w_ffn_gate[l][:, c0:c0 + cw], 8, cw))
                    P.append((("wu", pi, l, u), self.w_ffn_up[l][:, c0:c0 + cw], 8, cw))
                for ch in range(2):
                    for kg, (k0, kn) in enumerate(((0, 8), (8, 8), (16, 6))):
                        P.append((("wd", pi, l, ch, kg),
                                  self.w_ffn_down[l][k0 * 128:(k0 + kn) * 128, ch * 512:(ch + 1) * 512], kn, 512))

    def w_issue(self, i):
        key, w, kch, cw = self.plan[i]
        slot = i % 4
        ring = self.ring[slot]
        if kch is None:
            dst = ring[:, 0:512].rearrange("p (g d) -> p g d", g=4)
            src = w.rearrange("g c d -> c g d")
        else:
            dst = ring[:, 0:kch * cw].rearrange("p (c n) -> p c n", c=kch)
            src = w.rearrange("(c p) n -> p c n", p=128)
        self.add("pool", lambda e: e.dma_start(out=dst, in_=src), writes=[self.ringB[slot]], dma=True)

    def wget(self, key, cont=False):
        i = self.w_consumed
        pkey, w, kch, cw = self.plan[i]
        assert pkey == key, (pkey, key)
        lim = min(len(self.plan), i + 4)
        while (not cont) and self.w_issued < lim:
            self.w_issue(self.w_issued)
            self.w_issued += 1
        assert self.w_issued > i
        self.w_consumed += 1
        slot = i % 4
        ring = self.ring[slot]
        if kch is None:
            view = ring[:, 0:512].rearrange("p (g d) -> p g d", g=4)
        else:
            view = ring[:, 0:kch * cw].rearrange("p (c n) -> p c n", c=kch)
        return view, self.ringB[slot], cw

    def next_stg(self):
        i = self.stg_rr % 2
        self.stg_rr += 1
        return self.stg[i], self.stgB[i]

    def mm(self, out, lhsT, rhs, start, stop, reads, wbuf):
        self.add("pe", lambda e: e.matmul(out, lhsT=lhsT, rhs=rhs, start=start, stop=stop),
                 reads=reads, writes=[wbuf])

    def tr(self, out, in_, ident, reads, wbuf):
        self.add("pe", lambda e: e.transpose(out, in_, ident), reads=reads, writes=[wbuf])

    def load_rows_T(self, rows_ap, R, C, evac):
        nch = C // 128
        stg, stgB = self.next_stg()
        self.add("sp", lambda e: e.dma_start(out=stg[0:R, 0:C], in_=rows_ap), writes=[stgB], dma=True)
        done = 0
        per = max(1, 512 // R)
        while done < nch:
            k = min(per, nch - done)
            bk, bkB = self.bank()
            for j in range(k):
                self.tr(bk[:, j * R:(j + 1) * R], stg[0:R, (done + j) * 128:(done + j + 1) * 128],
                        self.ident[0:R, 0:R], [stgB, self.constB], bkB)
            evac(done, k, bk[:, 0:k * R].rearrange("p (c r) -> p c r", c=k), bkB)
            done += k

    def store_rows_T(self, srcs, R, dram_rows, evac_eng="act"):
        nch = len(srcs)
        assert nch <= 8
        stg, stgB = self.next_stg()
        done = 0
        while done < nch:
            k = min(4, nch - done)
            bk, bkB = self.bank()
            for j in range(k):
                ap, bufs = srcs[done + j]
                self.tr(bk[0:R, j * 128:(j + 1) * 128], ap, self.ident[:, :], list(bufs) + [self.constB], bkB)
            o = stg[0:R, done * 128:(done + k) * 128]
            i_ = bk[0:R, 0:k * 128]
            if evac_eng == "act":
                self.add("act", lambda e, o=o, i_=i_: e.activation(out=o, in_=i_, func=ACTF.Copy), reads=[bkB], writes=[stgB])
            else:
                self.add("dve", lambda e, o=o, i_=i_: e.tensor_copy(out=o, in_=i_), reads=[bkB], writes=[stgB])
            done += k
        ob = Buf()
        self.add("sp", lambda e: e.dma_start(out=dram_rows, in_=stg[0:R, 0:nch * 128]), reads=[stgB], writes=[ob], dma=True)
        self.out_bufs.append(ob)

    def gain(self, l, i):
        return lambda c: self.c1024[:, c, l * 7 + i:l * 7 + i + 1]

    def setup_consts(self):
        self.S.cur_tag = "consts"
        A = self.add
        ident, identb = self.ident, self.identb

        cB = self.constB
        A("pool", lambda e: e.memset(ident[:], 0.0), writes=[cB])
        A("pool", lambda e: e.affine_select(out=ident[:], in_=ident[:], compare_op=ALU.not_equal, fill=1.0, base=0,
                                            pattern=[[-1, 128]], channel_multiplier=1), reads=[cB], writes=[cB])
        A("pool", lambda e: e.tensor_copy(out=identb[:], in_=ident[:]), reads=[cB], writes=[cB])
        for tl, val in ((self.ones1024, 1.0 / 1024), (self.ones512, 1.0 / 512), (self.ones1, 1.0), (self.zeros, 0.0), (self.onesf, 1.0)):
            A("pool", lambda e, tl=tl, val=val: e.memset(tl[:], val), writes=[cB])
        for g, w in enumerate(POOLW):
            A("pool", lambda e, g=g: e.memset(self.poolfix[:, g, :], 1.0), writes=[cB])
            for t in range(w - 1):
                A("pool", lambda e, g=g, t=t, w=w: e.memset(self.poolfix[:, g, t:t + 1], float(w) / float(t + 1)), writes=[cB])
        groups = [
            (self.c1024, 8, [(self.norm_gains, 28), (self.gconv_w, 6)]),
            (self.c512, 4, [(self.pool_scale, 2), (self.conf_w, 62), (self.conf_b, 2), (self.conf_ln_g, 2),
                            (self.conf_ln_b, 2)]),
        ]
        for dst, nch, items in groups:
            r0 = 0
            for src, R in items:
                def ev(c0, k, view, bkB, dst=dst, r0=r0, R=R):
                    A("dve", lambda e: e.tensor_copy(out=dst[:, c0:c0 + k, r0:r0 + R], in_=view), reads=[bkB], writes=[self.cB])
                self.load_rows_T(src, R, nch * 128, ev)
                r0 += R
        r0 = 0
        for src, R in [(self.ffn_conv_w, 12), (self.ffn_conv_b, 4)]:
            for c0 in range(0, DFF, 1024):
                cw = min(1024, DFF - c0)

                def ev(cc, k, view, bkB, r0=r0, R=R, base=c0 // 128):
                    A("dve", lambda e: e.tensor_copy(out=self.c2816[:, base + cc:base + cc + k, r0:r0 + R], in_=view),
                      reads=[bkB], writes=[self.cB])
                self.load_rows_T(src[:, c0:c0 + cw], R, cw, ev)
            r0 += R

    def stats_rstd(self, src_fn, w, ones, eps, nchunks=8):
        A = self.add
        st, stB = self.stbank()
        for c in range(nchunks):
            ap, bufs = src_fn(c)
            i = self.sq_rr % 4
            self.sq_rr += 1
            sq, sqB = self.sq[i], self.sqB[i]
            A("act", lambda e, ap=ap, sq=sq: e.activation(out=sq[:, 0:w], in_=ap, func=ACTF.Square), reads=bufs, writes=[sqB])
            self.mm(st[:, 0:w], ones[:, :], sq[:, 0:w], c == 0, c == nchunks - 1, [sqB, self.constB], stB)
        A("act", lambda e: e.activation(out=self.vsb[0:1, 0:w], in_=st[0:1, 0:w], func=ACTF.Sqrt, bias=eps, scale=1.0),
          reads=[stB], writes=[self.vsbB])
        return self.row_pow_bcast(w)

    def row_pow_bcast(self, w, extra=None):
        A = self.add
        i = self.rstd_rr % 2
        self.rstd_rr += 1
        r, rB = self.rstd[i], self.rstdB[i]
        A("dve", lambda e: e.reciprocal(out=r[0:1, 0:w], in_=self.vsb[0:1, 0:w]), reads=[self.vsbB], writes=[rB])
        bc, bcB = self.bank()
        self.bcast_row(bc, bcB, r, rB, w)
        return bc, bcB, r, rB

    def bcast_row(self, bc, bcB, r, rB, w):
        for c0 in range(0, w, 128):
            self.mm(bc[:, c0:c0 + 128], self.onesf[0:1, :], r[0:1, c0:c0 + 128], True, True, [rB, self.constB], bcB)

    def _pre_norm(self, t, gi):
        A = self.add
        x, h = self.x, self.h
        g = self.gain(self.l, gi)
        r, rB, _, _ = self.stats_rstd(lambda c: (x[:, c, t.cols], [self.xB[c][t.n]]), t.w, self.ones1024, RMS_EPS)
        for c in range(8):
            A("dve", lambda e, c=c: e.scalar_tensor_tensor(out=h[:, c, t.cols], in0=x[:, c, t.cols], scalar=g(c),
                                                           in1=r[:, 0:t.w], op0=ALU.mult, op1=ALU.mult),
              reads=[self.xB[c][t.n], rB, self.cB], writes=[self.hB[c][t.n]])

    def _post_norm(self, t, gi):
        A = self.add
        x, y = self.x, self.h
        g = self.gain(self.l, gi)
        r, rB, _, _ = self.stats_rstd(lambda c: (y[:, c, t.cols], [self.hB[c][t.n]]), t.w, self.ones1024, RMS_EPS)
        for c in range(8):
            i = self.tmp_rr % 2
            self.tmp_rr += 1
            tm, tmB = self.tmp[i], self.tmpB[i]
            A("dve", lambda e, c=c, tm=tm: e.scalar_tensor_tensor(out=tm[:, 0:t.w], in0=y[:, c, t.cols], scalar=g(c),
                                                                  in1=r[:, 0:t.w], op0=ALU.mult, op1=ALU.mult),
              reads=[self.hB[c][t.n], rB, self.cB], writes=[tmB])
            A("dve", lambda e, c=c, tm=tm: e.tensor_tensor(out=x[:, c, t.cols], in0=x[:, c, t.cols], in1=tm[:, 0:t.w], op=ALU.add),
              reads=[tmB, self.xB[c][t.n]], writes=[self.xB[c][t.n]])

    def _tagged(self, suffix, fn, *a):
        old = self.S.cur_tag
        self.S.cur_tag = old.split(".")[0] + suffix
        try:
            return fn(*a)
        finally:
            self.S.cur_tag = old

    def pre_norm(self, t, gi):
        return self._tagged(".pre", self._pre_norm, t, gi)

    def post_norm(self, t, gi):
        return self._tagged(".post", self._post_norm, t, gi)

    def out_proj(self, *a):
        return self._tagged(".out", self._out_proj, *a)

    def _out_proj(self, keybase, src, srcB, gi_post, gi_pre_next):
        A = self.add
        pending = []

        def norms(t):
            self.post_norm(t, gi_post)
            if gi_pre_next is not None:
                self.l_next_pre(t, gi_pre_next)
        ws = [self.wget(keybase + (0,)), self.wget(keybase + (1,), cont=True)]
        for t in self.tiles:
            for m in range(8):
                wv, wB, _ = ws[m // 4]
                ml = m % 4
                bk, bkB = self.bank()
                for c in range(8):
                    self.mm(bk[:, 0:t.w], wv[:, c, ml * 128:(ml + 1) * 128], src[:, c, t.cols], c == 0, c == 7,
                            [wB, srcB[c][t.n]], bkB)
                A("act", lambda e, m=m, t=t, bk=bk: e.activation(out=self.h[:, m, t.cols], in_=bk[:, 0:t.w], func=ACTF.Copy),
                  reads=[bkB], writes=[self.hB[m][t.n]])
            pending.append(t)
            if len(pending) > 1:
                norms(pending.pop(0))
        while pending:
            norms(pending.pop(0))

    def l_next_pre(self, t, spec):
        l_save = self.l
        self.l = spec[0]
        self.pre_norm(t, spec[1])
        self.l = l_save

    def load_x(self):
        self.S.cur_tag = "load"
        A = self.add
        for t in self.tiles:
            for blk in range(t.w // 128):
                if t.kind == "P":
                    rows = self.xp[t.tok0 + blk * 128: t.tok0 + (blk + 1) * 128, :]
                else:
                    rows = self.xs[:, :]
                cs = slice(t.c0 + blk * 128, t.c0 + (blk + 1) * 128)

                def ev(c0, k, view, bkB, cs=cs, t=t):
                    A("act", lambda e: e.activation(out=self.x[:, c0:c0 + k, cs], in_=view, func=ACTF.Copy), reads=[bkB],
                      writes=[self.xB[c][t.n] for c in range(c0, c0 + k)])
                self.load_rows_T(rows, 128, D, ev)
        self.l = 0
        for t in self.tiles:
            self.pre_norm(t, 0)

    def store_x(self):
        self.S.cur_tag = "store"
        for t in self.tiles:
            for blk in range(t.w // 128):
                cs = slice(t.c0 + blk * 128, t.c0 + (blk + 1) * 128)
                srcs = [(self.x[:, c, cs], [self.xB[c][t.n]]) for c in range(8)]
                if t.kind == "P":
                    rows = self.yp[t.tok0 + blk * 128: t.tok0 + (blk + 1) * 128, :]
                else:
                    rows = self.ys[:, :]
                self.store_rows_T(srcs, 128, rows, evac_eng="act" if blk % 2 == 0 else "dve")

    def phase0_memkv(self):
        self.S.cur_tag = "p0"
        A = self.add
        big = self.big
        self.region_reset("big", [])
        mhat = big[:, 0:2048].rearrange("p (c k) -> p c k", c=8)
        mT = big[:, 2048:4096].rearrange("p (c k) -> p c k", c=8)
        vbf = big[:, 4096:6144].rearrange("p (k d) -> p k d", k=2)
        kTb = big[:, 6144:8192].rearrange("p (c k) -> p c k", c=8)
        mhatB, mTB, vbfB, kTbB = self.rbufs("big", 4)
        self.region_reset("scr", [])
        mraw = self.scr[:, 0:2048].rearrange("p (c k) -> p c k", c=8)
        mrawB = self.rbufs("scr", 1)[0]
        for kc in range(2):
            def ev(c0, k, view, bkB, kc=kc):
                A("dve", lambda e: e.tensor_copy(out=mraw[:, c0:c0 + k, kc * 128:(kc + 1) * 128], in_=view), reads=[bkB], writes=[mrawB])
            self.load_rows_T(self.memp[kc * 128:(kc + 1) * 128, :], 128, D, ev)
        rbk, rbkB, _, _ = self.stats_rstd(lambda c: (mraw[:, c, :], [mrawB]), NMEM, self.ones1024, RMS_EPS)
        r, rB = self.tmp[0], self.tmpB[0]
        A("act", lambda e: e.activation(out=r[:, 0:NMEM], in_=rbk[:, 0:NMEM], func=ACTF.Copy), reads=[rbkB], writes=[rB])
        for l in range(DEPTH):
            for c in range(8):
                A("dve", lambda e, c=c, l=l: e.scalar_tensor_tensor(out=mT[:, c, :], in0=mraw[:, c, :], scalar=self.c1024[:, c, l * 7 + 6:l * 7 + 7],
                                                                   in1=r[:, 0:NMEM], op0=ALU.mult, op1=ALU.mult),
                  reads=[mrawB, rB, self.cB], writes=[mTB])
            for which, dram_o in (("wk", self.o_mk_p), ("wv", self.o_mv_p)):
                for u in range(2):
                    wv, wB, _ = self.wget((which, l, u))
                    for kc in range(2):
                        bk, bkB = self.bank()
                        for c in range(8):
                            self.mm(bk[:, :], mT[:, c, kc * 128:(kc + 1) * 128], wv[:, c, :], c == 0, c == 7, [mTB, wB], bkB)
                        stg, stgB = self.next_stg()
                        A("act", lambda e, stg=stg, bk=bk: e.activation(out=stg[:, 0:512], in_=bk[:, :], func=ACTF.Copy),
                          reads=[bkB], writes=[stgB])
                        ob = Buf()
                        if "out" in self.cfg.get("p0", ("out", "kT", "vbf")):
                            A("sp", lambda e, stg=stg, kc=kc, u=u, dram_o=dram_o, l=l: e.dma_start(
                                out=dram_o[l, kc * 128:(kc + 1) * 128, u * 512:(u + 1) * 512], in_=stg[:, 0:512]),
                              reads=[stgB], writes=[ob], dma=True)
                            self.out_bufs.append(ob)
                        if which == "wv" and "vbf" in self.cfg.get("p0", ("out", "kT", "vbf")):
                            A("dve", lambda e, stg=stg, kc=kc, u=u: e.tensor_copy(out=vbf[:, kc, u * 512:(u + 1) * 512], in_=stg[:, 0:512]),
                              reads=[stgB], writes=[vbfB])
                    if which == "wk" and "kT" in self.cfg.get("p0", ("out", "kT", "vbf")):
                        for ml in range(4):
                            m = u * 4 + ml
                            bk, bkB = self.bank()
                            for c in range(8):
                                self.mm(bk[:, 0:256], wv[:, c, ml * 128:(ml + 1) * 128], mT[:, c, :], c == 0, c == 7, [mTB, wB], bkB)
                            A("dve", lambda e, bk=bk, m=m: e.tensor_copy(out=kTb[:, m, :], in_=bk[:, 0:256]), reads=[bkB], writes=[kTbB])
            s1, s2 = Buf(), Buf()
            if self.cfg.get("scr", True):
                A("sp", lambda e, l=l: e.dma_start(out=self.kT_scr[l], in_=big[:, 6144:8192]), reads=[kTbB], writes=[s1], dma=True)
                A("sp", lambda e, l=l: e.dma_start(out=self.v_scr[l], in_=big[:, 4096:6144]), reads=[vbfB], writes=[s2], dma=True)
            if l == 0:
                self.kscrB, self.vscrB = [], []
            self.kscrB.append(s1)
            self.vscrB.append(s2)

    def scrB0(self):
        if not hasattr(self, "_scrB0"):
            self._scrB0 = self.rbufs("scr", 1)[0]
        return self._scrB0

    def odd_mixer(self, o):
        self.S.cur_tag = "odd"
        A = self.add
        l, pi = self.l, self.pi
        big = self.big
        self.region_reset("big", [])
        self.region_reset("scr", [])
        EXT = 1186
        uext = big[:, 0:8 * EXT].rearrange("p (c t) -> p c t", c=8)
        ycv = big[:, 8 * EXT:8 * EXT + 8 * TCOLS].rearrange("p (c t) -> p c t", c=8)
        uB = [[b for b in self.rbufs("big", 3)] for _ in range(8)]
        uctxB = self.rbufs("big", 1)[0]
        ycvB = [[b for b in self.rbufs("big", 3)] for _ in range(8)]
        accs = [self.scr[:, 0:512], self.scr[:, 512:1024]]
        accB = self.rbufs("scr", 2)
        gw = lambda j, k: self.c1024[:, j, 28 + o * 3 + k:28 + o * 3 + k + 1]

        def ucols(j, t, shift=0):
            if t.kind == "P":
                return uext[:, j, t.c0 + shift:t.c0 + shift + t.w]
            return uext[:, j, 1026:1186].rearrange("p (b s) -> p b s", b=NB_S)[:, :, shift:shift + TS]

        if pi == 0:
            A("dve", lambda e: e.memset(uext[:, :, 0:2], 0.0), writes=[uctxB])
            def ev(c0, k, view, bkB):
                for j in range(c0, c0 + k):
                    dst = uext[:, j, 1026:1186].rearrange("p (b s) -> p b s", b=NB_S)[:, :, 0:2]
                    src = view[:, j - c0, :].rearrange("p (b r) -> p b r", b=NB_S)
                    A("dve", lambda e, dst=dst, src=src: e.tensor_copy(out=dst, in_=src), reads=[bkB], writes=[uctxB])
            self.load_rows_T(self.st_gconv[o].rearrange("b r d -> (b r) d"), 32, D, ev)
        else:
            A("dve", lambda e: e.tensor_copy(out=uext[:, :, 0:2], in_=self.car_g[o][:, :, :]), reads=[self.car_gB[o]], writes=[uctxB])

        def view3(ap, t):
            return ap if t.kind == "P" else ap.rearrange("p (b s) -> p b s", b=NB_S)

        for part, name in ((0, "xin"), (2, "gc"), (1, "gb")):
            for u in range(2):
                wv, wB, _ = self.wget(("win", pi, l, part, u))
                for ml in range(4):
                    j = u * 4 + ml
                    for t in self.tiles:
                        bk, bkB = self.bank()
                        for c in range(8):
                            self.mm(bk[:, 0:t.w], wv[:, c, ml * 128:(ml + 1) * 128], self.h[:, c, t.cols], c == 0, c == 7,
                                    [wB, self.hB[c][t.n]], bkB)
                        bv = view3(bk[:, 0:t.w], t)
                        if part == 0:
                            A("act", lambda e, j=j, t=t, bv=bv: e.activation(out=ucols(j, t, 2), in_=bv, func=ACTF.Copy),
                              reads=[bkB], writes=[uB[j][t.n]])
                        elif part == 2:
                            A("dve", lambda e, j=j, t=t, bv=bv: e.tensor_tensor(out=ucols(j, t, 2), in0=bv, in1=ucols(j, t, 2), op=ALU.mult),
                              reads=[bkB, uB[j][t.n]], writes=[uB[j][t.n]])
                        else:
                            i = (j * 3 + t.n) % 2
                            acc, aB = accs[i], accB[i]
                            av = view3(acc[:, 0:t.w], t)
                            rd = [uB[j][t.n], uctxB, self.cB] + ([uB[j][t.n - 1]] if (t.kind == "P" and t.n > 0) else [])
                            A("dve", lambda e, j=j, t=t, av=av: e.tensor_scalar(out=av, in0=ucols(j, t, 0), scalar1=gw(j, 0), scalar2=None,
                                                                               op0=ALU.mult), reads=rd, writes=[aB])
                            for k in (1, 2):
                                A("dve", lambda e, j=j, t=t, av=av, k=k: e.scalar_tensor_tensor(
                                    out=av, in0=ucols(j, t, k), scalar=gw(j, k), in1=av, op0=ALU.mult, op1=ALU.add),
                                  reads=rd + [aB], writes=[aB])
                            A("dve", lambda e, j=j, t=t, acc=acc, bk=bk: e.tensor_tensor(out=ycv[:, j, t.cols], in0=bk[:, 0:t.w],
                                                                                        in1=acc[:, 0:t.w], op=ALU.mult),
                              reads=[bkB, aB], writes=[ycvB[j][t.n]])
        last = self.tiles[-1]
        if pi == 0:
            A("act", lambda e: e.activation(out=self.car_g[o][:, :, :], in_=uext[:, :, 1024:1026], func=ACTF.Copy),
              reads=[uB[j][1] for j in range(8)], writes=[self.car_gB[o]])
            for j in range(8):
                src = uext[:, j, 1026:1186].rearrange("p (b s) -> p b s", b=NB_S)[:, :, 8:10].rearrange("p b r -> p r b")
                dst = self.cstate[:, j, 0:32].rearrange("p (r b) -> p r b", r=2)
                A("act", lambda e, src=src, dst=dst: e.activation(out=dst, in_=src, func=ACTF.Copy), reads=[uB[j][2]], writes=[self.cstateB])
            srcs = [(self.cstate[:, j, 0:32], [self.cstateB]) for j in range(8)]
            self.store_rows_T_multi(srcs, 32, [(r * 16, 16, self.o_gconv_s[o][:, r, :]) for r in range(2)])
        else:
            for j in range(8):
                A("act", lambda e, j=j: e.activation(out=self.cstate[:, j, 0:2], in_=uext[:, j, 1024:1026], func=ACTF.Copy),
                  reads=[uB[j][1]], writes=[self.cstateB])
            srcs = [(self.cstate[:, j, 0:2], [self.cstateB]) for j in range(8)]
            self.store_rows_T_multi(srcs, 2, [(0, 2, self.o_gconv_p[o])])
        self.out_proj(("wout", pi, l), ycv, ycvB, 1, (l, 2))

    def store_rows_T_multi(self, srcs, R, dsts):
        nch = len(srcs)
        for g0 in range(0, nch, 8):
            g = srcs[g0:g0 + 8]
            stg, stgB = self.next_stg()
            done = 0
            while done < len(g):
                k = min(4, len(g) - done)
                bk, bkB = self.bank()
                for j in range(k):
                    ap, bufs = g[done + j]
                    self.tr(bk[0:R, j * 128:(j + 1) * 128], ap, self.ident[:, :], list(bufs) + [self.constB], bkB)
                o = stg[0:R, done * 128:(done + k) * 128]
                i_ = bk[0:R, 0:k * 128]
                self.add("act", lambda e, o=o, i_=i_: e.activation(out=o, in_=i_, func=ACTF.Copy), reads=[bkB], writes=[stgB])
                done += k
            for (r0, nr, dap) in dsts:
                ob = Buf()
                self.add("sp", lambda e, r0=r0, nr=nr, dap=dap, stg=stg, g0=g0, ng=len(g): e.dma_start(
                    out=dap[:, g0 * 128:(g0 + ng) * 128], in_=stg[r0:r0 + nr, 0:ng * 128]), reads=[stgB], writes=[ob], dma=True)
                self.out_bufs.append(ob)

    def even_mixer(self, e_):
        self.S.cur_tag = "even"
        A = self.add
        l, pi = self.l, self.pi
        big = self.big
        self.region_reset("big", [])
        self.region_reset("scr", [])
        AE, UE = 1407, 1662
        aext = big[:, 0:4 * AE].rearrange("p (c t) -> p c t", c=4)
        uext = big[:, 4 * AE:4 * AE + 4 * UE].rearrange("p (c t) -> p c t", c=4)
        o0 = 4 * AE + 4 * UE
        ycat = big[:, o0:o0 + 8 * TCOLS].rearrange("p (c t) -> p c t", c=8)
        o1 = o0 + 8 * TCOLS
        cbb = big[:, o1:o1 + 2048].rearrange("p (c t) -> p c t", c=4)
        aB = [[b for b in self.rbufs("big", 3)] for _ in range(4)]
        actxB = self.rbufs("big", 1)[0]
        uB = [[b for b in self.rbufs("big", 3)] for _ in range(4)]
        uctxB = self.rbufs("big", 1)[0]
        ycatB = [[b for b in self.rbufs("big", 3)] for _ in range(8)]
        cbbB = self.rbufs("big", 4)
        scr = self.scr
        ping = [scr[:, 0:768], scr[:, 768:1536]]
        pingB = self.rbufs("scr", 2)
        lnt = [scr[:, 1536 + i * 512:1536 + (i + 1) * 512] for i in range(3)]
        lntB = self.rbufs("scr", 3)
        cst = lambda row: (lambda j: self.c512[:, j, row:row + 1])
        pscale = cst(e_)
        cw = lambda j, k: self.c512[:, j, 2 + e_ * 31 + k:2 + e_ * 31 + k + 1]
        cbias, lng, lnb = cst(64 + e_), cst(66 + e_), cst(68 + e_)

        def acols(g, t, lo, n):
            if t.kind == "P":
                return aext[:, g, 15 + t.c0 + lo:15 + t.c0 + lo + n]
            return aext[:, g, 1039:1407].rearrange("p (b s) -> p b s", b=NB_S)[:, :, 15 + lo:15 + lo + n]

        def ucols(j, t, lo, n):
            if t.kind == "P":
                return uext[:, j, 30 + t.c0 + lo:30 + t.c0 + lo + n]
            return uext[:, j, 1054:1662].rearrange("p (b s) -> p b s", b=NB_S)[:, :, 30 + lo:30 + lo + n]

        def view3(ap, t):
            return ap if t.kind == "P" else ap.rearrange("p (b s) -> p b s", b=NB_S)

        if pi == 0:
            A("dve", lambda e: e.memset(aext[:, :, 0:15], 0.0), writes=[actxB])
            A("dve", lambda e: e.memset(uext[:, :, 0:30], 0.0), writes=[uctxB])
            for (st, nr, ext, base, tot, ctxB) in ((self.st_pool[e_], 15, aext, 1039, 23, actxB), (self.st_conf[e_], 30, uext, 1054, 38, uctxB)):
                bper = 128 // nr
                for b0 in range(0, NB_S, bper):
                    nb = min(bper, NB_S - b0)
                    R = nb * nr

                    def ev(c0, k, view, bkB, ext=ext, base=base, tot=tot, nr=nr, b0=b0, nb=nb, ctxB=ctxB):
                        for j in range(c0, c0 + k):
                            dst = ext[:, j, base + b0 * tot:base + (b0 + nb) * tot].rearrange("p (b s) -> p b s", b=nb)[:, :, 0:nr]
                            src = view[:, j - c0, :].rearrange("p (b r) -> p b r", b=nb)
                            A("dve", lambda e, dst=dst, src=src: e.tensor_copy(out=dst, in_=src), reads=[bkB], writes=[ctxB])
                    self.load_rows_T(st[b0:b0 + nb].rearrange("b r d -> (b r) d"), R, 512, ev)
            for (st, ost, keep, nr) in ((self.st_pool[e_], self.o_pool_s[e_], 7, 15), (self.st_conf[e_], self.o_conf_s[e_], 22, 30)):
                ob = Buf()
                A("sp", lambda e, st=st, ost=ost, keep=keep, nr=nr: e.dma_start(
                    out=ost[:, 0:keep, :].rearrange("b r d -> b (r d)"), in_=st[:, nr - keep:nr, :].rearrange("b r d -> b (r d)")),
                  writes=[ob], dma=True)
                self.out_bufs.append(ob)
        else:
            A("dve", lambda e: e.tensor_copy(out=aext[:, :, 0:15], in_=self.car_a[e_][:, :, :]), reads=[self.car_aB[e_]], writes=[actxB])
            A("dve", lambda e: e.tensor_copy(out=uext[:, :, 0:30], in_=self.car_u[e_][:, :, :]), reads=[self.car_uB[e_]], writes=[uctxB])

        wa, waB, _ = self.wget(("win", pi, l, 0))
        for g in range(4):
            for t in self.tiles:
                bk, bkB = self.bank()
                for c in range(8):
                    self.mm(bk[:, 0:t.w], wa[:, c, g * 128:(g + 1) * 128], self.h[:, c, t.cols], c == 0, c == 7, [waB, self.hB[c][t.n]], bkB)
                A("act", lambda e, g=g, t=t, bk=bk: e.activation(out=acols(g, t, 0, TS if t.kind == "S" else t.w), in_=view3(bk[:, 0:t.w], t),
                                                                func=ACTF.Copy), reads=[bkB], writes=[aB[g][t.n]])
        wp, wpB, _ = self.wget(("wpool", pi, l))
        for g in range(4):
            wlen = POOLW[g]
            for t in self.tiles:
                n = TS if t.kind == "S" else t.w
                nseq = NB_S if t.kind == "S" else 1
                rd = [aB[g][t.n], actxB] + ([aB[g][t.n - 1]] if (t.kind == "P" and t.n > 0) else [])
                prev = lambda lo, cnt, g=g, t=t: acols(g, t, lo, cnt)
                prevB = rd
                s = 1
                k = 0
                while s < wlen:
                    lo = -(wlen - 2 * s)
                    cnt = n - lo
                    buf, bufB = ping[k % 2], pingB[k % 2]

                    def lvl(lo2, cnt2, buf=buf, lo=lo, cnt=cnt, nseq=nseq):
                        if nseq == 1:
                            return buf[:, lo2 - lo:lo2 - lo + cnt2]
                        return buf[:, 0:nseq * cnt].rearrange("p (b s) -> p b s", b=nseq)[:, :, lo2 - lo:lo2 - lo + cnt2]
                    A("dve", lambda e, lvl=lvl, prev=prev, lo=lo, cnt=cnt, s=s: e.tensor_tensor(
                        out=lvl(lo, cnt), in0=prev(lo, cnt), in1=prev(lo - s, cnt), op=ALU.add), reads=prevB, writes=[bufB])
                    prev, prevB = lvl, [bufB]
                    s *= 2
                    k += 1
                if t.kind == "P" and t.tok0 == 0:
                    A("dve", lambda e, prev=prev, g=g: e.tensor_tensor(out=prev(0, 15), in0=prev(0, 15), in1=self.poolfix[:, g, :], op=ALU.mult),
                      reads=prevB + [self.constB], writes=prevB)
                i = self.sl_rr % 2
                self.sl_rr += 1
                pl, plB = self.sl[i], self.slB[i]
                A("dve", lambda e, prev=prev, pl=pl, t=t, g=g, n=n, wlen=wlen: e.scalar_tensor_tensor(
                    out=view3(pl[:, 0:t.w], t), in0=prev(0, n), scalar=1.0 / wlen, in1=acols(g, t, 0, n), op0=ALU.mult, op1=ALU.subtract),
                  reads=prevB + [aB[g][t.n]], writes=[plB])
                bk, bkB = self.bank()
                self.mm(bk[:, 0:t.w], wp[:, g, :], pl[:, 0:t.w], True, True, [wpB, plB], bkB)
                A("act", lambda e, g=g, t=t, bk=bk: e.activation(out=ycat[:, g, t.cols], in_=bk[:, 0:t.w], func=ACTF.Copy, scale=pscale(g)),
                  reads=[bkB, self.cB], writes=[ycatB[g][t.n]])
        if pi == 0:
            A("act", lambda e: e.activation(out=self.car_a[e_][:, :, :], in_=aext[:, :, 1024:1039], func=ACTF.Copy),
              reads=[aB[g][1] for g in range(4)], writes=[self.car_aB[e_]])
            for g in range(4):
                src = aext[:, g, 1039:1407].rearrange("p (b s) -> p b s", b=NB_S)[:, :, 15:23].rearrange("p b r -> p r b")
                dst = self.cstate[:, g, :].rearrange("p (r b) -> p r b", r=TS)
                A("act", lambda e, src=src, dst=dst: e.activation(out=dst, in_=src, func=ACTF.Copy), reads=[aB[g][2]], writes=[self.cstateB])
            self.store_rows_T_multi([(self.cstate[:, g, :], [self.cstateB]) for g in range(4)], 128,
                                    [(r * 16, 16, self.o_pool_s[e_][:, 7 + r, :]) for r in range(TS)])
        else:
            for g in range(4):
                A("act", lambda e, g=g: e.activation(out=self.cstate[:, g, 0:15], in_=aext[:, g, 1024:1039], func=ACTF.Copy),
                  reads=[aB[g][1]], writes=[self.cstateB])
            self.store_rows_T_multi([(self.cstate[:, g, 0:15], [self.cstateB]) for g in range(4)], 15, [(0, 15, self.o_pool_p[e_])])

        w1, w1B, _ = self.wget(("win", pi, l, 1))
        w2, w2B, _ = self.wget(("win", pi, l, 2), cont=True)
        for j in range(4):
            for t in self.tiles:
                n = TS if t.kind == "S" else t.w
                bk2, bk2B = self.bank()
                for c in range(8):
                    self.mm(bk2[:, 0:t.w], w2[:, c, j * 128:(j + 1) * 128], self.h[:, c, t.cols], c == 0, c == 7, [w2B, self.hB[c][t.n]], bk2B)
                i = self.sl_rr % 2
                self.sl_rr += 1
                sg, sgB = self.sl[i], self.slB[i]
                A("act", lambda e, sg=sg, bk2=bk2, t=t: e.activation(out=sg[:, 0:t.w], in_=bk2[:, 0:t.w], func=ACTF.Sigmoid),
                  reads=[bk2B], writes=[sgB])
                bk1, bk1B = self.bank()
                for c in range(8):
                    self.mm(bk1[:, 0:t.w], w1[:, c, j * 128:(j + 1) * 128], self.h[:, c, t.cols], c == 0, c == 7, [w1B, self.hB[c][t.n]], bk1B)
                A("dve", lambda e, j=j, t=t, n=n, bk1=bk1, sg=sg: e.tensor_tensor(out=ucols(j, t, 0, n), in0=view3(bk1[:, 0:t.w], t),
                                                                                 in1=view3(sg[:, 0:t.w], t), op=ALU.mult),
                  reads=[bk1B, sgB], writes=[uB[j][t.n]])
        if pi == 0:
            A("act", lambda e: e.activation(out=self.car_u[e_][:, :, :], in_=uext[:, :, 1024:1054], func=ACTF.Copy),
              reads=[uB[j][1] for j in range(4)], writes=[self.car_uB[e_]])
            for j in range(4):
                src = uext[:, j, 1054:1662].rearrange("p (b s) -> p b s", b=NB_S)[:, :, 30:38].rearrange("p b r -> p r b")
                dst = self.cstate[:, 4 + j, :].rearrange("p (r b) -> p r b", r=TS)
                A("act", lambda e, src=src, dst=dst: e.activation(out=dst, in_=src, func=ACTF.Copy), reads=[uB[j][2]], writes=[self.cstateB])
            self.store_rows_T_multi([(self.cstate[:, 4 + j, :], [self.cstateB]) for j in range(4)], 128,
                                    [(r * 16, 16, self.o_conf_s[e_][:, 22 + r, :]) for r in range(TS)])
        else:
            for j in range(4):
                A("act", lambda e, j=j: e.activation(out=self.cstate[:, 4 + j, 0:30], in_=uext[:, j, 1024:1054], func=ACTF.Copy),
                  reads=[uB[j][1]], writes=[self.cstateB])
            self.store_rows_T_multi([(self.cstate[:, 4 + j, 0:30], [self.cstateB]) for j in range(4)], 30, [(0, 30, self.o_conf_p[e_])])

        self.region_reset("scr", [])
        scrb = scr[:, :].bitcast(BF16)
        cb = scrb[:, 0:4 * TCOLS].rearrange("p (c t) -> p c t", c=4)
        cbB = [[b for b in self.rbufs("scr", 3)] for _ in range(4)]
        dg0 = 4 * TCOLS
        assert dg0 + 31 * 128 <= 2 * self.SCRN
        diag = scrb[:, dg0:dg0 + 31 * 128].rearrange("p (k d) -> p k d", k=31)
        diagB = self.rbufs("scr", 1)[0]
        for j in range(4):
            for k in range(31):
                A("dve", lambda e, j=j, k=k: e.tensor_scalar(out=diag[:, k, :], in0=self.identb[:, :], scalar1=cw(j, k), scalar2=None,
                                                            op0=ALU.mult), reads=[self.constB, self.cB], writes=[diagB])
            for t in self.tiles:
                n = TS if t.kind == "S" else t.w
                bk, bkB = self.bank()
                rd = [diagB, uB[j][t.n], uctxB] + ([uB[j][t.n - 1]] if (t.kind == "P" and t.n > 0) else [])
                for k in range(31):
                    self.mm(view3(bk[:, 0:t.w], t), diag[:, k, :], ucols(j, t, k - 30, n), k == 0, k == 30, rd, bkB)
                A("act", lambda e, j=j, t=t, bk=bk: e.activation(out=cb[:, j, t.cols], in_=bk[:, 0:t.w], func=ACTF.Identity, bias=cbias(j), scale=1.0),
                  reads=[bkB, self.cB], writes=[cbB[j][t.n]])
        for t in self.tiles:
            w_ = t.w
            stA, stAB = self.stbank()
            stB_, stBB = self.stbank()
            for j in range(4):
                self.mm(stA[:, 0:w_], self.ones512[:, :], cb[:, j, t.cols], j == 0, j == 3, [cbB[j][t.n], self.constB], stAB)
                i = self.sq_rr % 4
                self.sq_rr += 1
                sq, sqB = self.sq[i], self.sqB[i]
                A("act", lambda e, j=j, t=t, sq=sq: e.activation(out=sq[:, 0:t.w], in_=cb[:, j, t.cols], func=ACTF.Square), reads=[cbB[j][t.n]], writes=[sqB])
                self.mm(stB_[:, 0:w_], self.ones512[:, :], sq[:, 0:w_], j == 0, j == 3, [sqB, self.constB], stBB)
            mean_sb, meanB = self.tmp[0], self.tmpB[0]
            m2, m2B = self.tmp[1], self.tmpB[1]
            A("act", lambda e, w_=w_, stA=stA: e.activation(out=mean_sb[0:1, 0:w_], in_=stA[0:1, 0:w_], func=ACTF.Copy), reads=[stAB], writes=[meanB])
            A("dve", lambda e, w_=w_: e.tensor_tensor(out=m2[0:1, 0:w_], in0=mean_sb[0:1, 0:w_], in1=mean_sb[0:1, 0:w_], op=ALU.mult), reads=[meanB], writes=[m2B])
            A("dve", lambda e, w_=w_, stB_=stB_: e.scalar_tensor_tensor(out=self.vsb[0:1, 0:w_], in0=stB_[0:1, 0:w_], scalar=LN_EPS, in1=m2[0:1, 0:w_],
                                                                        op0=ALU.add, op1=ALU.subtract), reads=[stBB, m2B], writes=[self.vsbB])
            A("act", lambda e, w_=w_: e.activation(out=self.vsb[0:1, 0:w_], in_=self.vsb[0:1, 0:w_], func=ACTF.Sqrt), reads=[self.vsbB], writes=[self.vsbB])
            r, rB, r1, r1B = self.row_pow_bcast(w_)
            A("dve", lambda e, w_=w_, r1=r1: e.scalar_tensor_tensor(out=m2[0:1, 0:w_], in0=mean_sb[0:1, 0:w_], scalar=-1.0, in1=r1[0:1, 0:w_],
                                                                    op0=ALU.mult, op1=ALU.mult), reads=[meanB, r1B], writes=[m2B])
            nm, nmB = self.bank()
            self.bcast_row(nm, nmB, m2, m2B, w_)
            for j in range(4):
                i2 = self.sl_rr % 2
                self.sl_rr += 1
                z, zB = self.sl[i2], self.slB[i2]
                i3 = self.tmp_rr % 2
                self.tmp_rr += 1
                A("dve", lambda e, j=j, t=t, w_=w_, z=z, r=r: e.tensor_tensor(out=z[:, 0:w_], in0=cb[:, j, t.cols], in1=r[:, 0:w_], op=ALU.mult),
                  reads=[cbB[j][t.n], rB], writes=[zB])
                A("dve", lambda e, w_=w_, z=z, nm=nm: e.tensor_tensor(out=z[:, 0:w_], in0=z[:, 0:w_], in1=nm[:, 0:w_], op=ALU.add),
                  reads=[zB, nmB], writes=[zB])
                A("act", lambda e, j=j, t=t, z=z, w_=w_: e.activation(out=ycat[:, 4 + j, t.cols], in_=z[:, 0:w_], func=ACTF.Silu,
                                                                     bias=lnb(j), scale=lng(j)), reads=[zB, self.cB], writes=[ycatB[4 + j][t.n]])
        self.out_proj(("wout", pi, l), ycat, ycatB, 1, (l, 2))

    def attention(self, l):
        self.S.cur_tag = "attn"
        A = self.add
        pi = self.pi
        big = self.big
        self.region_reset("big", [])
        qT = big[:, 0:9216].rearrange("p (c t) -> p c t", c=8)
        oT = big[:, 9216:18432].rearrange("p (c t) -> p c t", c=8)
        kst = big[:, 18432:20480].rearrange("p (k d) -> p k d", k=2)
        vst = big[:, 20480:22528].rearrange("p (k d) -> p k d", k=2)
        kT = big[:, 22528:24576].rearrange("p (c k) -> p c k", c=8)
        ets = big[:, 24576:24640]
        qB = [[b for b in self.rbufs("big", 3)] for _ in range(8)]
        oB = [[b for b in self.rbufs("big", 3)] for _ in range(8)]
        kstB, vstB, kTB, etsB = self.rbufs("big", 4)
        self.region_reset("scr", [])
        kst1 = self.scr[:, 0:1024].bitcast(BF16).rearrange("p (k d) -> p k d", k=2)
        vst1 = self.scr[:, 1024:2048].bitcast(BF16).rearrange("p (k d) -> p k d", k=2)
        kst1B, vst1B = self.rbufs("scr", 2)
        kv = [(kst, kstB, vst, vstB), (kst1, kst1B, vst1, vst1B)]
        A("sp", lambda e: e.dma_start(out=self.kTp[:, :, :].rearrange("p c k -> p (c k)"), in_=self.kT_scr[l]),
          reads=[self.kscrB[l]], writes=[self.kTpB], dma=True)
        A("sp", lambda e: e.dma_start(out=self.vp[:, :, :].rearrange("p k d -> p (k d)"), in_=self.v_scr[l]),
          reads=[self.vscrB[l]], writes=[self.vpB], dma=True)
        for u in range(2):
            wv, wB, _ = self.wget(("wq", pi, l, u))
            for ml in range(4):
                m = u * 4 + ml
                for t in self.tiles:
                    bk, bkB = self.bank()
                    for c in range(8):
                        self.mm(bk[:, 0:t.w], wv[:, c, ml * 128:(ml + 1) * 128], self.h[:, c, t.cols], c == 0, c == 7,
                                [wB, self.hB[c][t.n]], bkB)
                    A("act", lambda e, m=m, t=t, bk=bk: e.activation(out=qT[:, m, t.cols], in_=bk[:, 0:t.w], func=ACTF.Copy, scale=1.0 / 16.0),
                      reads=[bkB], writes=[qB[m][t.n]])
        for t in self.tiles:
            if t.kind == "P":
                def scores(hd, t=t):
                    i = self.et_rr % 2
                    self.et_rr += 1
                    et, etB = self.et[i], self.etB[i]
                    den, denB = self.stbank()
                    for kc in range(2):
                        bk, bkB = self.bank()
                        for dc in range(2):
                            m = 2 * hd + dc
                            self.mm(bk[:, 0:t.w], self.kTp[:, m, kc * 128:(kc + 1) * 128], qT[:, m, t.cols], dc == 0, dc == 1,
                                    [self.kTpB, qB[m][t.n]], bkB)
                        A("act", lambda e, kc=kc, et=et, bk=bk: e.activation(out=et[:, kc, 0:t.w], in_=bk[:, 0:t.w], func=ACTF.Exp),
                          reads=[bkB], writes=[etB])
                    for kc in range(2):
                        self.mm(den[:, 0:t.w], self.ones1[:, :], et[:, kc, 0:t.w], kc == 0, kc == 1, [etB, self.constB], denB)
                    return et, etB, den, denB

                def pv(hd, ctx, t=t):
                    et, etB, den, denB = ctx
                    i2 = self.rstd_rr % 2
                    self.rstd_rr += 1
                    rd_, rdB = self.rstd[i2], self.rstdB[i2]
                    A("dve", lambda e: e.reciprocal(out=rd_[:, 0:t.w], in_=den[:, 0:t.w]), reads=[denB], writes=[rdB])
                    for dc in range(2):
                        m = 2 * hd + dc
                        bk, bkB = self.bank()
                        for kc in range(2):
                            self.mm(bk[:, 0:t.w], self.vp[:, kc, m * 128:(m + 1) * 128], et[:, kc, 0:t.w], kc == 0, kc == 1,
                                    [self.vpB, etB], bkB)
                        A("dve", lambda e, m=m, bk=bk: e.tensor_tensor(out=oT[:, m, t.cols], in0=bk[:, 0:t.w], in1=rd_[:, 0:t.w],
                                                                      op=ALU.mult), reads=[bkB, rdB], writes=[oB[m][t.n]])
                ctxs = {0: scores(0)}
                for hd in range(4):
                    if hd + 1 < 4:
                        ctxs[hd + 1] = scores(hd + 1)
                    pv(hd, ctxs.pop(hd))
            else:
                kT1 = self.scr[:, 2048:3072].bitcast(BF16).rearrange("p (c k) -> p c k", c=8)
                kT1B, ets1B = self.rbufs("scr", 2)
                ets1 = self.scr[:, 3072:3104].bitcast(BF16)
                kTs = [(kT, kTB), (kT1, kT1B)]
                etss = [(ets, etsB), (ets1, ets1B)]

                def dmaK(b):
                    kst, kstB, _, _ = kv[b % 2]
                    A("pool", lambda e: e.dma_start(out=kst, in_=self.ck[l, b].rearrange("(k p) d -> p k d", p=128)), writes=[kstB], dma=True)

                def dmaV(b):
                    _, _, vst, vstB = kv[b % 2]
                    A("pool", lambda e: e.dma_start(out=vst, in_=self.cv[l, b].rearrange("(k p) d -> p k d", p=128)), writes=[vstB], dma=True)

                def T(b):
                    kst, kstB, _, _ = kv[b % 2]
                    kT_, kT_B = kTs[b % 2]
                    for half in range(2):
                        bk, bkB = self.bank()
                        bkb = bk[:, :].bitcast(BF16)
                        for ml in range(4):
                            m = half * 4 + ml
                            for kc in range(2):
                                self.tr(bkb[:, ml * 256 + kc * 128: ml * 256 + (kc + 1) * 128], kst[:, kc, m * 128:(m + 1) * 128],
                                        self.identb[:, :], [kstB, self.constB], bkB)
                        A("act", lambda e, half=half, bkb=bkb: e.activation(out=kT_[:, half * 4:(half + 1) * 4, :],
                                                                           in_=bkb.rearrange("p (c k) -> p c k", c=4), func=ACTF.Copy),
                          reads=[bkB], writes=[kT_B])

                def S(b):
                    cs = slice(t.c0 + b * TS, t.c0 + (b + 1) * TS)
                    kT_, kT_B = kTs[b % 2]
                    ets_, ets_B = etss[b % 2]
                    sbk, sbkB = self.bank()
                    for hd in range(4):
                        for kc in range(2):
                            for dc in range(2):
                                m = 2 * hd + dc
                                self.mm(sbk[:, (hd * 2 + kc) * TS:(hd * 2 + kc + 1) * TS], kT_[:, m, kc * 128:(kc + 1) * 128], qT[:, m, cs],
                                        dc == 0, dc == 1, [kT_B, qB[m][t.n]], sbkB)
                    A("act", lambda e: e.activation(out=ets_[:, 0:64], in_=sbk[:, 0:64], func=ACTF.Exp), reads=[sbkB], writes=[ets_B])

                def PV(b):
                    cs = slice(t.c0 + b * TS, t.c0 + (b + 1) * TS)
                    _, _, vst, vstB = kv[b % 2]
                    ets_, ets_B = etss[b % 2]
                    den, denB = self.stbank()
                    e4 = ets_[:, 0:64].rearrange("p (h k q) -> p h k q", h=4, k=2)
                    for kc in range(2):
                        self.mm(den[:, 0:32].rearrange("p (h q) -> p h q", h=4), self.ones1[:, :], e4[:, :, kc, :], kc == 0, kc == 1,
                                [ets_B, self.constB], denB)
                    i2 = self.rstd_rr % 2
                    self.rstd_rr += 1
                    rd_, rdB = self.rstd[i2], self.rstdB[i2]
                    A("dve", lambda e: e.reciprocal(out=rd_[:, 0:32], in_=den[:, 0:32]), reads=[denB], writes=[rdB])
                    obk, obkB = self.bank()
                    for m in range(8):
                        hd = m // 2
                        for kc in range(2):
                            self.mm(obk[:, m * TS:(m + 1) * TS], vst[:, kc, m * 128:(m + 1) * 128],
                                    ets_[:, (hd * 2 + kc) * TS:(hd * 2 + kc + 1) * TS], kc == 0, kc == 1, [vstB, ets_B], obkB)
                    o4 = obk[:, 0:64].rearrange("p (h d q) -> p h d q", h=4, d=2)
                    for dc in range(2):
                        dst = oT[:, :, cs].rearrange("p (h d) q -> p h d q", d=2)[:, :, dc, :]
                        A("dve", lambda e, dst=dst, dc=dc: e.tensor_tensor(
                            out=dst, in0=o4[:, :, dc, :], in1=rd_[:, 0:32].rearrange("p (h q) -> p h q", h=4), op=ALU.mult),
                          reads=[obkB, rdB], writes=[oB[2 * hh + dc][t.n] for hh in range(4)])

                dmaK(0); dmaV(0); dmaK(1); dmaV(1)
                T(0); dmaK(2)
                T(1); dmaK(3)
                S(0)
                for b in range(NB_S):
                    if b + 1 < NB_S:
                        S(b + 1)
                    PV(b)
                    if b + 2 < NB_S:
                        dmaV(b + 2)
                        T(b + 2)
                        if b + 4 < NB_S:
                            dmaK(b + 4)
        self.out_proj(("wo", pi, l), oT, oB, 3, (l, 4))

    def ffn(self, l):
        self.S.cur_tag = "ffn"
        A = self.add
        pi = self.pi
        big = self.big
        self.region_reset("big", [])
        self.region_reset("scr", [])
        act = big[:, 0:22 * TCOLS].rearrange("p (c t) -> p c t", c=22)
        actB = [[b for b in self.rbufs("big", 3)] for _ in range(22)]
        scr = self.scr
        GE = 1186
        gext = scr[:, 0:GE]
        gextB = self.rbufs("scr", 3)
        gctxB = self.rbufs("scr", 1)[0]
        accs = [scr[:, 1536:2048], scr[:, 2048:2560], scr[:, 2560:3072]]
        accB = self.rbufs("scr", 3)
        fw = lambda m, k: self.c2816[:, m, l * 3 + k:l * 3 + k + 1]
        fb = lambda m: self.c2816[:, m, 12 + l:12 + l + 1]

        def gcols(t, shift):
            if t.kind == "P":
                return gext[:, t.c0 + shift:t.c0 + shift + t.w]
            return gext[:, 1026:1186].rearrange("p (b s) -> p b s", b=NB_S)[:, :, shift:shift + TS]

        def view3(ap, t):
            return ap if t.kind == "P" else ap.rearrange("p (b s) -> p b s", b=NB_S)

        if pi == 0:
            for c0 in range(0, DFF, 1024):
                cw_ = min(1024, DFF - c0)

                def ev(cc, k, view, bkB, base=c0 // 128):
                    A("dve", lambda e: e.tensor_copy(out=self.fctx[:, base + cc:base + cc + k, :], in_=view), reads=[bkB], writes=[self.fctxB])
                self.load_rows_T(self.st_ffn[l].rearrange("b r d -> (b r) d")[:, c0:c0 + cw_], 32, cw_, ev)
        for u in range(6):
            wg, wgB, cwid = self.wget(("wg", pi, l, u))
            wu, wuB, _ = self.wget(("wu", pi, l, u), cont=True)
            for ml in range(cwid // 128):
                m = u * 4 + ml
                if pi == 0:
                    A("pool", lambda e: e.tensor_copy(out=gext[:, 0:2], in_=self.zeros[:, 0:2]), reads=[self.constB], writes=[gctxB])
                    A("pool", lambda e, m=m: e.tensor_copy(out=gext[:, 1026:1186].rearrange("p (b s) -> p b s", b=NB_S)[:, :, 0:2],
                                                          in_=self.fctx[:, m, :].rearrange("p (b r) -> p b r", b=NB_S)),
                      reads=[self.fctxB], writes=[gctxB])
                else:
                    A("pool", lambda e, m=m: e.tensor_copy(out=gext[:, 0:2], in_=self.car_f[l][:, m, :]), reads=[self.car_fB[l]], writes=[gctxB])
                T_ = self.tiles
                bkg_l, bku_l = [], []
                for t in T_:
                    bkg, bkgB = self.bank()
                    for c in range(8):
                        self.mm(bkg[:, 0:t.w], wg[:, c, ml * 128:(ml + 1) * 128], self.h[:, c, t.cols], c == 0, c == 7, [wgB, self.hB[c][t.n]], bkgB)
                    A("act", lambda e, t=t, bkg=bkg: e.activation(out=gcols(t, 2), in_=view3(bkg[:, 0:t.w], t), func=ACTF.Copy),
                      reads=[bkgB], writes=[gextB[t.n]])
                    acc, aB = accs[t.n], accB[t.n]
                    A("act", lambda e, m=m, t=t, bkg=bkg, acc=acc: e.activation(out=acc[:, 0:t.w], in_=bkg[:, 0:t.w], func=ACTF.Identity,
                                                                              bias=fb(m), scale=fw(m, 2)), reads=[bkgB, self.cB], writes=[aB])
                    bkg_l.append((bkg, bkgB))
                for t in T_:
                    bku, bkuB = self.bank()
                    for c in range(8):
                        self.mm(bku[:, 0:t.w], wu[:, c, ml * 128:(ml + 1) * 128], self.h[:, c, t.cols], c == 0, c == 7, [wuB, self.hB[c][t.n]], bkuB)
                    bku_l.append((bku, bkuB))
                for k in (0, 1):
                    for t in T_:
                        acc, aB = accs[t.n], accB[t.n]
                        av = view3(acc[:, 0:t.w], t)
                        rd = [gextB[t.n], gctxB, self.cB] + ([gextB[t.n - 1]] if (t.kind == "P" and t.n > 0) else [])
                        A("dve", lambda e, m=m, t=t, av=av, k=k: e.scalar_tensor_tensor(out=av, in0=gcols(t, k), scalar=fw(m, k), in1=av,
                                                                                       op0=ALU.mult, op1=ALU.add), reads=rd + [aB], writes=[aB])
                sls = []
                for t in T_:
                    acc, aB = accs[t.n], accB[t.n]
                    i2 = self.sq_rr % 4
                    self.sq_rr += 1
                    sl, slB = self.sq[i2], self.sqB[i2]
                    A("act", lambda e, t=t, acc=acc, sl=sl: e.activation(out=sl[:, 0:t.w], in_=acc[:, 0:t.w], func=ACTF.Silu), reads=[aB], writes=[slB])
                    sls.append((sl, slB))
                for ti, t in enumerate(T_):
                    bku, bkuB = bku_l[ti]
                    sl, slB = sls[ti]
                    A("dve", lambda e, m=m, t=t, bku=bku, sl=sl: e.tensor_tensor(out=act[:, m, t.cols], in0=bku[:, 0:t.w], in1=sl[:, 0:t.w], op=ALU.mult),
                      reads=[bkuB, slB], writes=[actB[m][t.n]])
                if pi == 0:
                    A("pool", lambda e, m=m: e.tensor_copy(out=self.car_f[l][:, m, :], in_=gext[:, 1024:1026]), reads=[gextB[1]], writes=[self.car_fB[l]])
                    src = gext[:, 1026:1186].rearrange("p (b s) -> p b s", b=NB_S)[:, :, 8:10].rearrange("p b r -> p r b")
                    A("pool", lambda e, m=m, src=src: e.tensor_copy(out=self.fstate[:, m, :].rearrange("p (r b) -> p r b", r=2), in_=src),
                      reads=[gextB[2]], writes=[self.fstateB])
                else:
                    A("pool", lambda e, m=m: e.tensor_copy(out=self.fstate[:, m, 0:2], in_=gext[:, 1024:1026]), reads=[gextB[1]], writes=[self.fstateB])
        if pi == 0:
            self.store_rows_T_multi([(self.fstate[:, m, :], [self.fstateB]) for m in range(22)], 32,
                                    [(r * 16, 16, self.o_ffn_s[l][:, r, :]) for r in range(2)])
        else:
            self.store_rows_T_multi([(self.fstate[:, m, 0:2], [self.fstateB]) for m in range(22)], 2, [(0, 2, self.o_ffn_p[l])])
        for ch in range(2):
            ws = [self.wget(("wd", pi, l, ch, kg), cont=(kg > 0)) for kg in range(3)]
            for ml in range(4):
                m = ch * 4 + ml
                for t in self.tiles:
                    bk, bkB = self.bank()
                    kk = 0
                    for kg, (k0, kn) in enumerate(((0, 8), (8, 8), (16, 6))):
                        wv, wB, _ = ws[kg]
                        for c in range(kn):
                            self.mm(bk[:, 0:t.w], wv[:, c, ml * 128:(ml + 1) * 128], act[:, k0 + c, t.cols], kk == 0, kk == 21,
                                    [wB, actB[k0 + c][t.n]], bkB)
                            kk += 1
                    A("act", lambda e, m=m, t=t, bk=bk: e.activation(out=self.h[:, m, t.cols], in_=bk[:, 0:t.w], func=ACTF.Copy),
                      reads=[bkB], writes=[self.hB[m][t.n]])
        nxt = (l + 1, 0) if l + 1 < DEPTH else None
        for t in self.tiles:
            self.post_norm(t, 5)
            if nxt is not None:
                self.l_next_pre(t, nxt)


_NC_CACHE = {}


def _get_nc():
    if "nc" not in _NC_CACHE:
        _NC_CACHE["nc"] = Builder().build()
    return _NC_CACHE["nc"]


def kernel(x_prompt, x_sample, state_pool, state_conf, state_gconv, state_ffn, cache_mem_k, cache_mem_v,
           mem_prompt, norm_gains, w_in_even, w_pool, pool_scale, conf_w, conf_b, conf_ln_g, conf_ln_b,
           w_out_even, w_in_odd, gconv_w, w_out_odd, w_mem_q, w_mem_k, w_mem_v, w_mem_o,
           w_ffn_gate, w_ffn_up, ffn_conv_w, ffn_conv_b, w_ffn_down):
    f = lambda a: np.ascontiguousarray(np.asarray(a, dtype=np.float32))
    shared = {
        "norm_gains": f(norm_gains).reshape(DEPTH * 7, D), "w_in_even": f(w_in_even), "w_pool": f(w_pool),
        "pool_scale": f(pool_scale), "conf_w": f(conf_w).reshape(62, 512), "conf_b": f(conf_b),
        "conf_ln_g": f(conf_ln_g), "conf_ln_b": f(conf_ln_b), "w_out_even": f(w_out_even), "w_in_odd": f(w_in_odd),
        "gconv_w": f(gconv_w).reshape(6, D), "w_out_odd": f(w_out_odd), "w_mem_q": f(w_mem_q), "w_mem_k": f(w_mem_k),
        "w_mem_v": f(w_mem_v), "w_mem_o": f(w_mem_o), "w_ffn_gate": f(w_ffn_gate), "w_ffn_up": f(w_ffn_up),
        "ffn_conv_w": f(ffn_conv_w).reshape(DEPTH * 3, DFF), "ffn_conv_b": f(ffn_conv_b), "w_ffn_down": f(w_ffn_down),
    }
    x_prompt, x_sample = f(x_prompt), f(x_sample)
    state_pool, state_conf, state_gconv, state_ffn = f(state_pool), f(state_conf), f(state_gconv), f(state_ffn)
    cache_mem_k, cache_mem_v, mem_prompt = f(cache_mem_k), f(cache_mem_v), f(mem_prompt)
    in_maps = []
    for c in range(NCORES):
        bs = slice(c * NB_S, (c + 1) * NB_S)
        m = dict(shared)
        m["xp"] = x_prompt[c]
        m["xs"] = x_sample[bs].reshape(NB_S * TS, D)
        m["st_pool"] = state_pool[:, bs]
        m["st_conf"] = state_conf[:, bs]
        m["st_gconv"] = state_gconv[:, bs]
        m["st_ffn"] = state_ffn[:, bs]
        m["ck"] = cache_mem_k[:, bs].reshape(DEPTH, NB_S, NMEM, D)
        m["cv"] = cache_mem_v[:, bs].reshape(DEPTH, NB_S, NMEM, D)
        m["memp"] = mem_prompt[c]
        in_maps.append({k: np.ascontiguousarray(v) for k, v in m.items()})
    nc = _get_nc()
    res = run_bass_kernel_spmd(nc, in_maps, core_ids=list(range(NCORES)))
    R = res.results
    cat = lambda k, ax: np.concatenate([np.asarray(r[k]) for r in R], axis=ax)
    stack = lambda k, ax: np.stack([np.asarray(r[k]) for r in R], axis=ax)
    y_prompt = stack("yp", 0)
    y_sample = cat("ys", 0).reshape(NCORES * NB_S, TS, D)
    pool_p = stack("o_pool_p", 1)
    conf_p = stack("o_conf_p", 1)
    gconv_p = stack("o_gconv_p", 1)
    ffn_p = stack("o_ffn_p", 1)
    mk_p = stack("o_mk_p", 1).reshape(DEPTH, NCORES, NMEM, 4, 256)
    mv_p = stack("o_mv_p", 1).reshape(DEPTH, NCORES, NMEM, 4, 256)
    pool_s = cat("o_pool_s", 1)
    conf_s = cat("o_conf_s", 1)
    gconv_s = cat("o_gconv_s", 1)
    ffn_s = cat("o_ffn_s", 1)
    return (y_prompt, y_sample, pool_p, conf_p, gconv_p, ffn_p, mk_p, mv_p, pool_s, conf_s, gconv_s, ffn_s)
```

```python
from contextlib import ExitStack

import numpy as np
import concourse.bass as bass
import concourse.mybir as mybir
from concourse.bass_utils import run_bass_kernel_spmd

F32 = mybir.dt.float32
BF16 = mybir.dt.bfloat16
ALU = mybir.AluOpType
ACTF = mybir.ActivationFunctionType

NCORES = 8
D = 1024
SEQ = 2048
DEPTH = 4
NB_S = 16
TS = 8
DFF = 2816
NMEM = 256
TCOLS = 1152
RMS_EPS = 1e-6
LN_EPS = 1e-5
POOLW = (2, 4, 8, 16)


class Buf:
    __slots__ = ("w", "r")

    def __init__(self):
        self.w = None
        self.r = {}


class Op:
    __slots__ = ("eng", "fn", "deps", "waited", "sig", "is_dma", "waits", "clock", "idx", "tag")


class Sched:
    def __init__(self, npool=12):
        self.streams = {k: [] for k in ("pe", "act", "dve", "pool", "sp")}
        self.order = []
        self.npool = npool
        self.hist = {k: [] for k in self.streams}

    def add(self, eng, fn, reads=(), writes=(), dma=False):
        op = Op()
        op.eng = eng
        op.fn = fn
        op.is_dma = dma
        op.waited = dma
        op.sig = None
        op.idx = len(self.order)
        op.tag = getattr(self, "cur_tag", "")
        deps = {}
        for b in reads:
            w = b.w
            if w is not None and (w.is_dma or w.eng != eng or eng != "pe"):
                deps[id(w)] = w
        for b in writes:
            w = b.w
            if w is not None and (w.is_dma or w.eng != eng or eng != "pe"):
                deps[id(w)] = w
            for o in b.r.values():
                if o.is_dma or o.eng != eng or eng != "pe":
                    deps[id(o)] = o
        if dma:
            hist = self.hist[eng]
            if len(hist) >= self.npool:
                prev = hist[len(hist) - self.npool]
                deps[id(prev)] = prev
            hist.append(op)
        op.deps = list(deps.values())
        for b in reads:
            b.r[id(op) if dma else eng] = op
        for b in writes:
            b.w = op
            b.r = {}
        self.streams[eng].append(op)
        self.order.append(op)
        return op

    def finalize(self, sems, dma_sems):
        for op in self.order:
            for d in op.deps:
                d.waited = True
        for eng in self.streams:
            cnt = 0
            dcnt = 0
            uses = [0] * self.npool
            for op in self.streams[eng]:
                if op.is_dma:
                    k = dcnt % self.npool
                    uses[k] += 1
                    op.sig = (("d", eng, k), 16 * uses[k])
                    dcnt += 1
                elif op.waited:
                    cnt += 1
                    op.sig = (("e", eng), cnt)
        seen = {eng: {} for eng in self.streams}
        for op in self.order:
            s = seen[op.eng]
            op.waits = []
            for d in sorted(op.deps, key=lambda d: -d.sig[1]):
                key, v = d.sig
                if s.get(key, 0) < v:
                    op.waits.append((key, v))
                    for k2, v2 in d.clock.items():
                        if s.get(k2, 0) < v2:
                            s[k2] = v2
            if op.waited:
                c = dict(s)
                c[op.sig[0]] = op.sig[1]
                op.clock = c
            else:
                op.clock = None

        def handle(key):
            if key[0] == "e":
                return sems[key[1]]
            return dma_sems[key[1]][key[2]]

        def emit(name):
            def body(e):
                for op in self.streams[name]:
                    for key, v in op.waits:
                        e.wait_ge(handle(key), v)
                    ins = op.fn(e)
                    if op.waited:
                        ins.then_inc(handle(op.sig[0]), 16 if op.is_dma else 1)
            return body
        return emit


def merged_hazards(bufs):
    out = {}
    for b in bufs:
        items = list(b.r.items())
        if b.w is not None:
            items.append((id(b.w) if b.w.is_dma else b.w.eng, b.w))
        for k, o in items:
            if k not in out or out[k].idx < o.idx:
                out[k] = o
    return out


class Tile:
    def __init__(self, n, col0, w, kind, tok0):
        self.n, self.c0, self.w, self.kind, self.tok0 = n, col0, w, kind, tok0
        self.cols = slice(col0, col0 + w)


class Builder:
    def __init__(self):
        self.nc = bass.Bass("TRN2", target_bir_lowering=False)
        self.S = Sched()
        self.es = ExitStack()
        self.out_bufs = []
        self.bank_rr = 0
        self.st_rr = 0
        self.region_bufs = {"big": [], "scr": []}
        self.cfg = dict(layers=DEPTH, passes=2, sub=("m", "a", "f"), phase0=True)

    def sb(self, name, shape, dt):
        return self.es.enter_context(self.nc.sbuf_tensor(name, shape, dt))

    def dram_in(self, name, shape):
        return self.nc.dram_tensor(name, list(shape), F32, kind="ExternalInput").ap()

    def dram_out(self, name, shape):
        return self.nc.dram_tensor(name, list(shape), F32, kind="ExternalOutput").ap()

    def rbufs(self, region, n):
        hz = merged_hazards(self.region_bufs[region])
        bs = []
        for _ in range(n):
            b = Buf()
            b.r = dict(hz)
            bs.append(b)
        self.region_bufs[region] = self.region_bufs[region] + bs
        return bs

    def region_reset(self, region, keep):
        hz = merged_hazards(self.region_bufs[region])
        carrier = Buf()
        carrier.r = hz
        self.region_bufs[region] = [carrier] + list(keep)

    def bank(self):
        i = self.bank_rr % 6
        self.bank_rr += 1
        return self.banks[i], self.bankB[i]

    def stbank(self):
        i = 6 + self.st_rr % 2
        self.st_rr += 1
        return self.banks[i], self.bankB[i]

    def add(self, *a, **k):
        return self.S.add(*a, **k)

    def build(self):
        nc = self.nc
        A = self.add
        di = self.dram_in
        self.xp = di("xp", [SEQ, D])
        self.xs = di("xs", [NB_S * TS, D])
        self.st_pool = di("st_pool", [2, NB_S, 15, 512])
        self.st_conf = di("st_conf", [2, NB_S, 30, 512])
        self.st_gconv = di("st_gconv", [2, NB_S, 2, D])
        self.st_ffn = di("st_ffn", [DEPTH, NB_S, 2, DFF])
        self.ck = di("ck", [DEPTH, NB_S, NMEM, D])
        self.cv = di("cv", [DEPTH, NB_S, NMEM, D])
        self.memp = di("memp", [NMEM, D])
        self.norm_gains = di("norm_gains", [DEPTH * 7, D])
        self.w_in_even = di("w_in_even", [2, D, 1536])
        self.w_pool = di("w_pool", [2, 4, 128, 128])
        self.pool_scale = di("pool_scale", [2, 512])
        self.conf_w = di("conf_w", [2 * 31, 512])
        self.conf_b = di("conf_b", [2, 512])
        self.conf_ln_g = di("conf_ln_g", [2, 512])
        self.conf_ln_b = di("conf_ln_b", [2, 512])
        self.w_out_even = di("w_out_even", [2, D, D])
        self.w_in_odd = di("w_in_odd", [2, D, 3 * D])
        self.gconv_w = di("gconv_w", [2 * 3, D])
        self.w_out_odd = di("w_out_odd", [2, D, D])
        self.w_mem_q = di("w_mem_q", [DEPTH, D, D])
        self.w_mem_k = di("w_mem_k", [DEPTH, D, D])
        self.w_mem_v = di("w_mem_v", [DEPTH, D, D])
        self.w_mem_o = di("w_mem_o", [DEPTH, D, D])
        self.w_ffn_gate = di("w_ffn_gate", [DEPTH, D, DFF])
        self.w_ffn_up = di("w_ffn_up", [DEPTH, D, DFF])
        self.ffn_conv_w = di("ffn_conv_w", [DEPTH * 3, DFF])
        self.ffn_conv_b = di("ffn_conv_b", [DEPTH, DFF])
        self.w_ffn_down = di("w_ffn_down", [DEPTH, DFF, D])
        do = self.dram_out
        self.yp = do("yp", [SEQ, D])
        self.ys = do("ys", [NB_S * TS, D])
        self.o_pool_p = do("o_pool_p", [2, 15, 512])
        self.o_conf_p = do("o_conf_p", [2, 30, 512])
        self.o_gconv_p = do("o_gconv_p", [2, 2, D])
        self.o_ffn_p = do("o_ffn_p", [DEPTH, 2, DFF])
        self.o_mk_p = do("o_mk_p", [DEPTH, NMEM, D])
        self.o_mv_p = do("o_mv_p", [DEPTH, NMEM, D])
        self.o_pool_s = do("o_pool_s", [2, NB_S, 15, 512])
        self.o_conf_s = do("o_conf_s", [2, NB_S, 30, 512])
        self.o_gconv_s = do("o_gconv_s", [2, NB_S, 2, D])
        self.o_ffn_s = do("o_ffn_s", [DEPTH, NB_S, 2, DFF])
        self.kT_scr = nc.dram_tensor("kT_scr", [DEPTH, 128, 8 * NMEM], BF16, kind="Internal").ap()
        self.v_scr = nc.dram_tensor("v_scr", [DEPTH, 128, 2 * D], BF16, kind="Internal").ap()

        sb = self.sb
        self.x = sb("x", [128, 8, TCOLS], F32)
        self.h = sb("h", [128, 8, TCOLS], BF16)
        self.BIGN = 25344
        self.big = sb("big", [128, self.BIGN], BF16)
        self.SCRN = 4352
        self.scr = sb("scr", [128, self.SCRN], F32)
        self.ring = [sb(f"ring{i}", [128, 4096], BF16) for i in range(4)]
        self.ringB = [Buf() for _ in range(4)]
        self.stg = [sb(f"stg{i}", [128, 1024], F32) for i in range(2)]
        self.stgB = [Buf() for _ in range(2)]
        self.stg_rr = 0
        self.c1024 = sb("c1024", [128, 8, 34], F32)
        self.c512 = sb("c512", [128, 4, 70], F32)
        self.c2816 = sb("c2816", [128, 22, 16], F32)
        self.cB = Buf()
        self.ident = sb("ident", [128, 128], F32)
        self.identb = sb("identb", [128, 128], BF16)
        self.ones1024 = sb("ones1024", [128, 128], BF16)
        self.ones512 = sb("ones512", [128, 128], BF16)
        self.ones1 = sb("ones1", [128, 128], BF16)
        self.onesf = sb("onesf", [128, 128], F32)
        self.zeros = sb("zeros", [128, 64], F32)
        self.poolfix = sb("poolfix", [128, 4, 15], F32)
        self.constB = Buf()
        self.vsb = sb("vsb", [128, 512], F32)
        self.vsbB = Buf()
        self.rstd = [sb(f"rstd{i}", [128, 512], F32) for i in range(2)]
        self.rstdB = [Buf() for _ in range(2)]
        self.rstd_rr = 0
        self.tmp = [sb(f"tmp{i}", [128, 512], F32) for i in range(2)]
        self.tmpB = [Buf() for _ in range(2)]
        self.tmp_rr = 0
        self.sq = [sb(f"sq{i}", [128, 512], BF16) for i in range(4)]
        self.sqB = [Buf() for _ in range(4)]
        self.sq_rr = 0
        self.et = [sb(f"et{i}", [128, 2, 512], BF16) for i in range(2)]
        self.etB = [Buf() for _ in range(2)]
        self.et_rr = 0
        self.sl = [sb(f"sl{i}", [128, 512], BF16) for i in range(2)]
        self.slB = [Buf() for _ in range(2)]
        self.sl_rr = 0
        self.fstate = sb("fstate", [128, 22, 32], F32)
        self.fstateB = Buf()
        self.fctx = sb("fctx", [128, 22, 32], F32)
        self.fctxB = Buf()
        self.cstate = sb("cstate", [128, 8, 128], F32)
        self.cstateB = Buf()
        self.kTp = sb("kTp", [128, 8, NMEM], BF16)
        self.kTpB = Buf()
        self.vp = sb("vp", [128, 2, D], BF16)
        self.vpB = Buf()
        self.car_a = [sb(f"car_a{e}", [128, 4, 15], BF16) for e in range(2)]
        self.car_u = [sb(f"car_u{e}", [128, 4, 30], BF16) for e in range(2)]
        self.car_g = [sb(f"car_g{o}", [128, 8, 2], BF16) for o in range(2)]
        self.car_f = [sb(f"car_f{l}", [128, 22, 2], F32) for l in range(DEPTH)]
        self.car_aB = [Buf() for _ in range(2)]
        self.car_uB = [Buf() for _ in range(2)]
        self.car_gB = [Buf() for _ in range(2)]
        self.car_fB = [Buf() for _ in range(DEPTH)]
        self.banks = [self.es.enter_context(nc.psum_tensor(f"bk{i}", [128, 512], F32)) for i in range(8)]
        self.bankB = [Buf() for _ in range(8)]
        self.xB = [[Buf() for _ in range(3)] for _ in range(8)]
        self.hB = [[Buf() for _ in range(3)] for _ in range(8)]

        sems = {k: self.es.enter_context(nc.semaphore("s_" + k)) for k in self.S.streams}
        dsems = {k: [self.es.enter_context(nc.semaphore(f"d_{k}_{i}")) for i in range(self.S.npool)]
                 for k in ("sp", "pool")}

        self.plan = []
        self.w_issued = 0
        self.w_consumed = 0
        self.make_plan()

        self.setup_consts()
        if self.cfg["phase0"]:
            self.phase0_memkv()
        passes = [
            [Tile(0, 0, 512, "P", 0), Tile(1, 512, 512, "P", 512), Tile(2, 1024, 128, "S", 0)],
            [Tile(0, 0, 512, "P", 1024), Tile(1, 512, 512, "P", 1536)],
        ]
        for pi, tiles in enumerate(passes[:self.cfg["passes"]]):
            self.pi = pi
            self.tiles = tiles
            self.load_x()
            for l in range(self.cfg["layers"]):
                self.l = l
                if "m" in self.cfg["sub"]:
                    if l % 2 == 0:
                        self.even_mixer(l // 2)
                    else:
                        self.odd_mixer(l // 2)
                if "a" in self.cfg["sub"]:
                    self.attention(l)
                if "f" in self.cfg["sub"]:
                    self.ffn(l)
            self.store_x()
        assert self.w_consumed == len(self.plan), (self.w_consumed, len(self.plan))
        A("sp", lambda e: e.nop(), reads=self.out_bufs)

        emit = self.S.finalize(sems, dsems)
        with nc.Block() as block:
            block.tensor(emit("pe"))
            block.scalar(emit("act"))
            block.vector(emit("dve"))
            block.gpsimd(emit("pool"))
            block.sync(emit("sp"))
        self.es.close()
        return nc

    def make_plan(self):
        P = self.plan

        def mat(key, w, kch, ncols, csz=512):
            for u, c0 in enumerate(range(0, ncols, csz)):
                cw = min(csz, ncols - c0)
                P.append((key + (u,), w[:, c0:c0 + cw], kch, cw))

        cfg = self.cfg
        for l in range(DEPTH if cfg["phase0"] else 0):
            mat(("wk", l), self.w_mem_k[l], 8, D)
            mat(("wv", l), self.w_mem_v[l], 8, D)
        for pi in range(cfg["passes"]):
            for l in range(cfg["layers"]):
                if "m" not in cfg["sub"]:
                    pass
                elif l % 2 == 0:
                    e = l // 2
                    wie = self.w_in_even[e]
                    P.append((("win", pi, l, 0), wie[:, 0:512], 8, 512))
                    P.append((("wpool", pi, l), self.w_pool[e], None, None))
                    P.append((("win", pi, l, 1), wie[:, 512:1024], 8, 512))
                    P.append((("win", pi, l, 2), wie[:, 1024:1536], 8, 512))
                    mat(("wout", pi, l), self.w_out_even[e], 8, D)
                else:
                    o = l // 2
                    wi = self.w_in_odd[o]
                    for part in (0, 2, 1):
                        mat(("win", pi, l, part), wi[:, part * D:(part + 1) * D], 8, D)
                    mat(("wout", pi, l), self.w_out_odd[o], 8, D)
                if "a" in cfg["sub"]:
                    mat(("wq", pi, l), self.w_mem_q[l], 8, D)
                    mat(("wo", pi, l), self.w_mem_o[l], 8, D)
                if "f" not in cfg["sub"]:
                    continue
                for u, c0 in enumerate(range(0, DFF, 512)):
                    cw = min(512, DFF - c0)
                    P.append((("wg", pi, l, u), self.w_ffn_gate[l][:, c0:c0 + cw], 8, cw))
                    P.append((("wu", pi, l, u), self.w_ffn_up[l][:, c0:c0 + cw], 8, cw))
                for ch in range(2):
                    for kg, (k0, kn) in enumerate(((0, 8), (8, 8), (16, 6))):
                        P.append((("wd", pi, l, ch, kg),
                                  self.w_ffn_down[l][k0 * 128:(k0 + kn) * 128, ch * 512:(ch + 1) * 512], kn, 512))

    def w_issue(self, i):
        key, w, kch, cw = self.plan[i]
        slot = i % 4
        ring = self.ring[slot]
        if kch is None:
            dst = ring[:, 0:512].rearrange("p (g d) -> p g d", g=4)
            src = w.rearrange("g c d -> c g d")
        else:
            dst = ring[:, 0:kch * cw].rearrange("p (c n) -> p c n", c=kch)
            src = w.rearrange("(c p) n -> p c n", p=128)
        self.add("pool", lambda e: e.dma_start(out=dst, in_=src), writes=[self.ringB[slot]], dma=True)

    def wget(self, key, cont=False):
        i = self.w_consumed
        pkey, w, kch, cw = self.plan[i]
        assert pkey == key, (pkey, key)
        lim = min(len(self.plan), i + 4)
        while (not cont) and self.w_issued < lim:
            self.w_issue(self.w_issued)
            self.w_issued += 1
        assert self.w_issued > i
        self.w_consumed += 1
        slot = i % 4
        ring = self.ring[slot]
        if kch is None:
            view = ring[:, 0:512].rearrange("p (g d) -> p g d", g=4)
        else:
            view = ring[:, 0:kch * cw].rearrange("p (c n) -> p c n", c=kch)
        return view, self.ringB[slot], cw

    def next_stg(self):
        i = self.stg_rr % 2
        self.stg_rr += 1
        return self.stg[i], self.stgB[i]

    def mm(self, out, lhsT, rhs, start, stop, reads, wbuf):
        self.add("pe", lambda e: e.matmul(out, lhsT=lhsT, rhs=rhs, start=start, stop=stop),
                 reads=reads, writes=[wbuf])

    def tr(self, out, in_, ident, reads, wbuf):
        self.add("pe", lambda e: e.transpose(out, in_, ident), reads=reads, writes=[wbuf])

    def load_rows_T(self, rows_ap, R, C, evac):
        nch = C // 128
        stg, stgB = self.next_stg()
        self.add("sp", lambda e: e.dma_start(out=stg[0:R, 0:C], in_=rows_ap), writes=[stgB], dma=True)
        done = 0
        per = max(1, 512 // R)
        while done < nch:
            k = min(per, nch - done)
            bk, bkB = self.bank()
            for j in range(k):
                self.tr(bk[:, j * R:(j + 1) * R], stg[0:R, (done + j) * 128:(done + j + 1) * 128],
                        self.ident[0:R, 0:R], [stgB, self.constB], bkB)
            evac(done, k, bk[:, 0:k * R].rearrange("p (c r) -> p c r", c=k), bkB)
            done += k

    def store_rows_T(self, srcs, R, dram_rows, evac_eng="act"):
        nch = len(srcs)
        assert nch <= 8
        stg, stgB = self.next_stg()
        done = 0
        while done < nch:
            k = min(4, nch - done)
            bk, bkB = self.bank()
            for j in range(k):
                ap, bufs = srcs[done + j]
                self.tr(bk[0:R, j * 128:(j + 1) * 128], ap, self.ident[:, :], list(bufs) + [self.constB], bkB)
            o = stg[0:R, done * 128:(done + k) * 128]
            i_ = bk[0:R, 0:k * 128]
            if evac_eng == "act":
                self.add("act", lambda e, o=o, i_=i_: e.activation(out=o, in_=i_, func=ACTF.Copy), reads=[bkB], writes=[stgB])
            else:
                self.add("dve", lambda e, o=o, i_=i_: e.tensor_copy(out=o, in_=i_), reads=[bkB], writes=[stgB])
            done += k
        ob = Buf()
        self.add("sp", lambda e: e.dma_start(out=dram_rows, in_=stg[0:R, 0:nch * 128]), reads=[stgB], writes=[ob], dma=True)
        self.out_bufs.append(ob)

    def gain(self, l, i):
        return lambda c: self.c1024[:, c, l * 7 + i:l * 7 + i + 1]

    def setup_consts(self):
        self.S.cur_tag = "consts"
        A = self.add
        ident, identb = self.ident, self.identb

        cB = self.constB
        A("pool", lambda e: e.memset(ident[:], 0.0), writes=[cB])
        A("pool", lambda e: e.affine_select(out=ident[:], in_=ident[:], compare_op=ALU.not_equal, fill=1.0, base=0,
                                            pattern=[[-1, 128]], channel_multiplier=1), reads=[cB], writes=[cB])
        A("pool", lambda e: e.tensor_copy(out=identb[:], in_=ident[:]), reads=[cB], writes=[cB])
        for tl, val in ((self.ones1024, 1.0 / 1024), (self.ones512, 1.0 / 512), (self.ones1, 1.0), (self.zeros, 0.0), (self.onesf, 1.0)):
            A("pool", lambda e, tl=tl, val=val: e.memset(tl[:], val), writes=[cB])
        for g, w in enumerate(POOLW):
            A("pool", lambda e, g=g: e.memset(self.poolfix[:, g, :], 1.0), writes=[cB])
            for t in range(w - 1):
                A("pool", lambda e, g=g, t=t, w=w: e.memset(self.poolfix[:, g, t:t + 1], float(w) / float(t + 1)), writes=[cB])
        groups = [
            (self.c1024, 8, [(self.norm_gains, 28), (self.gconv_w, 6)]),
            (self.c512, 4, [(self.pool_scale, 2), (self.conf_w, 62), (self.conf_b, 2), (self.conf_ln_g, 2),
                            (self.conf_ln_b, 2)]),
        ]
        for dst, nch, items in groups:
            r0 = 0
            for src, R in items:
                def ev(c0, k, view, bkB, dst=dst, r0=r0, R=R):
                    A("dve", lambda e: e.tensor_copy(out=dst[:, c0:c0 + k, r0:r0 + R], in_=view), reads=[bkB], writes=[self.cB])
                self.load_rows_T(src, R, nch * 128, ev)
                r0 += R
        r0 = 0
        for src, R in [(self.ffn_conv_w, 12), (self.ffn_conv_b, 4)]:
            for c0 in range(0, DFF, 1024):
                cw = min(1024, DFF - c0)

                def ev(cc, k, view, bkB, r0=r0, R=R, base=c0 // 128):
                    A("dve", lambda e: e.tensor_copy(out=self.c2816[:, base + cc:base + cc + k, r0:r0 + R], in_=view),
                      reads=[bkB], writes=[self.cB])
                self.load_rows_T(src[:, c0:c0 + cw], R, cw, ev)
            r0 += R

    def stats_rstd(self, src_fn, w, ones, eps, nchunks=8):
        A = self.add
        st, stB = self.stbank()
        for c in range(nchunks):
            ap, bufs = src_fn(c)
            i = self.sq_rr % 4
            self.sq_rr += 1
            sq, sqB = self.sq[i], self.sqB[i]
            A("act", lambda e, ap=ap, sq=sq: e.activation(out=sq[:, 0:w], in_=ap, func=ACTF.Square), reads=bufs, writes=[sqB])
            self.mm(st[:, 0:w], ones[:, :], sq[:, 0:w], c == 0, c == nchunks - 1, [sqB, self.constB], stB)
        A("act", lambda e: e.activation(out=self.vsb[0:1, 0:w], in_=st[0:1, 0:w], func=ACTF.Sqrt, bias=eps, scale=1.0),
          reads=[stB], writes=[self.vsbB])
        return self.row_pow_bcast(w)

    def row_pow_bcast(self, w, extra=None):
        A = self.add
        i = self.rstd_rr % 2
        self.rstd_rr += 1
        r, rB = self.rstd[i], self.rstdB[i]
        A("dve", lambda e: e.reciprocal(out=r[0:1, 0:w], in_=self.vsb[0:1, 0:w]), reads=[self.vsbB], writes=[rB])
        bc, bcB = self.bank()
        self.bcast_row(bc, bcB, r, rB, w)
        return bc, bcB, r, rB

    def bcast_row(self, bc, bcB, r, rB, w):
        for c0 in range(0, w, 128):
            self.mm(bc[:, c0:c0 + 128], self.onesf[0:1, :], r[0:1, c0:c0 + 128], True, True, [rB, self.constB], bcB)

    def _pre_norm(self, t, gi):
        A = self.add
        x, h = self.x, self.h
        g = self.gain(self.l, gi)
        r, rB, _, _ = self.stats_rstd(lambda c: (x[:, c, t.cols], [self.xB[c][t.n]]), t.w, self.ones1024, RMS_EPS)
        for c in range(8):
            A("dve", lambda e, c=c: e.scalar_tensor_tensor(out=h[:, c, t.cols], in0=x[:, c, t.cols], scalar=g(c),
                                                           in1=r[:, 0:t.w], op0=ALU.mult, op1=ALU.mult),
              reads=[self.xB[c][t.n], rB, self.cB], writes=[self.hB[c][t.n]])

    def _post_norm(self, t, gi):
        A = self.add
        x, y = self.x, self.h
        g = self.gain(self.l, gi)
        r, rB, _, _ = self.stats_rstd(lambda c: (y[:, c, t.cols], [self.hB[c][t.n]]), t.w, self.ones1024, RMS_EPS)
        for c in range(8):
            i = self.tmp_rr % 2
            self.tmp_rr += 1
            tm, tmB = self.tmp[i], self.tmpB[i]
            A("dve", lambda e, c=c, tm=tm: e.scalar_tensor_tensor(out=tm[:, 0:t.w], in0=y[:, c, t.cols], scalar=g(c),
                                                                  in1=r[:, 0:t.w], op0=ALU.mult, op1=ALU.mult),
              reads=[self.hB[c][t.n], rB, self.cB], writes=[tmB])
            A("pool", lambda e, c=c, tm=tm: e.tensor_tensor(out=x[:, c, t.cols], in0=x[:, c, t.cols], in1=tm[:, 0:t.w], op=ALU.add),
              reads=[tmB, self.xB[c][t.n]], writes=[self.xB[c][t.n]])

    def _tagged(self, suffix, fn, *a):
        old = self.S.cur_tag
        self.S.cur_tag = old.split(".")[0] + suffix
        try:
            return fn(*a)
        finally:
            self.S.cur_tag = old

    def pre_norm(self, t, gi):
        return self._tagged(".pre", self._pre_norm, t, gi)

    def post_norm(self, t, gi):
        return self._tagged(".post", self._post_norm, t, gi)

    def out_proj(self, *a):
        return self._tagged(".out", self._out_proj, *a)

    def _out_proj(self, keybase, src, srcB, gi_post, gi_pre_next):
        A = self.add
        pending = []

        def norms(t):
            self.post_norm(t, gi_post)
            if gi_pre_next is not None:
                self.l_next_pre(t, gi_pre_next)
        ws = [self.wget(keybase + (0,)), self.wget(keybase + (1,), cont=True)]
        for t in self.tiles:
            for m in range(8):
                wv, wB, _ = ws[m // 4]
                ml = m % 4
                bk, bkB = self.bank()
                for c in range(8):
                    self.mm(bk[:, 0:t.w], wv[:, c, ml * 128:(ml + 1) * 128], src[:, c, t.cols], c == 0, c == 7,
                            [wB, srcB[c][t.n]], bkB)
                A("act", lambda e, m=m, t=t, bk=bk: e.activation(out=self.h[:, m, t.cols], in_=bk[:, 0:t.w], func=ACTF.Copy),
                  reads=[bkB], writes=[self.hB[m][t.n]])
            pending.append(t)
            if len(pending) > 1:
                norms(pending.pop(0))
        while pending:
            norms(pending.pop(0))

    def l_next_pre(self, t, spec):
        l_save = self.l
        self.l = spec[0]
        self.pre_norm(t, spec[1])
        self.l = l_save

    def load_x(self):
        self.S.cur_tag = "load"
        A = self.add
        for t in self.tiles:
            for blk in range(t.w // 128):
                if t.kind == "P":
                    rows = self.xp[t.tok0 + blk * 128: t.tok0 + (blk + 1) * 128, :]
                else:
                    rows = self.xs[:, :]
                cs = slice(t.c0 + blk * 128, t.c0 + (blk + 1) * 128)

                def ev(c0, k, view, bkB, cs=cs, t=t):
                    A("act", lambda e: e.activation(out=self.x[:, c0:c0 + k, cs], in_=view, func=ACTF.Copy), reads=[bkB],
                      writes=[self.xB[c][t.n] for c in range(c0, c0 + k)])
                self.load_rows_T(rows, 128, D, ev)
        self.l = 0
        for t in self.tiles:
            self.pre_norm(t, 0)

    def store_x(self):
        self.S.cur_tag = "store"
        for t in self.tiles:
            for blk in range(t.w // 128):
                cs = slice(t.c0 + blk * 128, t.c0 + (blk + 1) * 128)
                srcs = [(self.x[:, c, cs], [self.xB[c][t.n]]) for c in range(8)]
                if t.kind == "P":
                    rows = self.yp[t.tok0 + blk * 128: t.tok0 + (blk + 1) * 128, :]
                else:
                    rows = self.ys[:, :]
                self.store_rows_T(srcs, 128, rows, evac_eng="act" if blk % 2 == 0 else "dve")

    def phase0_memkv(self):
        self.S.cur_tag = "p0"
        A = self.add
        big = self.big
        self.region_reset("big", [])
        mhat = big[:, 0:2048].rearrange("p (c k) -> p c k", c=8)
        mT = big[:, 2048:4096].rearrange("p (c k) -> p c k", c=8)
        vbf = big[:, 4096:6144].rearrange("p (k d) -> p k d", k=2)
        kTb = big[:, 6144:8192].rearrange("p (c k) -> p c k", c=8)
        mhatB, mTB, vbfB, kTbB = self.rbufs("big", 4)
        self.region_reset("scr", [])
        mraw = self.scr[:, 0:2048].rearrange("p (c k) -> p c k", c=8)
        mrawB = self.rbufs("scr", 1)[0]
        for kc in range(2):
            def ev(c0, k, view, bkB, kc=kc):
                A("dve", lambda e: e.tensor_copy(out=mraw[:, c0:c0 + k, kc * 128:(kc + 1) * 128], in_=view), reads=[bkB], writes=[mrawB])
            self.load_rows_T(self.memp[kc * 128:(kc + 1) * 128, :], 128, D, ev)
        rbk, rbkB, _, _ = self.stats_rstd(lambda c: (mraw[:, c, :], [mrawB]), NMEM, self.ones1024, RMS_EPS)
        r, rB = self.tmp[0], self.tmpB[0]
        A("act", lambda e: e.activation(out=r[:, 0:NMEM], in_=rbk[:, 0:NMEM], func=ACTF.Copy), reads=[rbkB], writes=[rB])
        for l in range(DEPTH):
            for c in range(8):
                A("dve", lambda e, c=c, l=l: e.scalar_tensor_tensor(out=mT[:, c, :], in0=mraw[:, c, :], scalar=self.c1024[:, c, l * 7 + 6:l * 7 + 7],
                                                                   in1=r[:, 0:NMEM], op0=ALU.mult, op1=ALU.mult),
                  reads=[mrawB, rB, self.cB], writes=[mTB])
            for which, dram_o in (("wk", self.o_mk_p), ("wv", self.o_mv_p)):
                for u in range(2):
                    wv, wB, _ = self.wget((which, l, u))
                    for kc in range(2):
                        bk, bkB = self.bank()
                        for c in range(8):
                            self.mm(bk[:, :], mT[:, c, kc * 128:(kc + 1) * 128], wv[:, c, :], c == 0, c == 7, [mTB, wB], bkB)
                        stg, stgB = self.next_stg()
                        A("act", lambda e, stg=stg, bk=bk: e.activation(out=stg[:, 0:512], in_=bk[:, :], func=ACTF.Copy),
                          reads=[bkB], writes=[stgB])
                        ob = Buf()
                        if "out" in self.cfg.get("p0", ("out", "kT", "vbf")):
                            A("sp", lambda e, stg=stg, kc=kc, u=u, dram_o=dram_o, l=l: e.dma_start(
                                out=dram_o[l, kc * 128:(kc + 1) * 128, u * 512:(u + 1) * 512], in_=stg[:, 0:512]),
                              reads=[stgB], writes=[ob], dma=True)
                            self.out_bufs.append(ob)
                        if which == "wv" and "vbf" in self.cfg.get("p0", ("out", "kT", "vbf")):
                            A("dve", lambda e, stg=stg, kc=kc, u=u: e.tensor_copy(out=vbf[:, kc, u * 512:(u + 1) * 512], in_=stg[:, 0:512]),
                              reads=[stgB], writes=[vbfB])
                    if which == "wk" and "kT" in self.cfg.get("p0", ("out", "kT", "vbf")):
                        for ml in range(4):
                            m = u * 4 + ml
                            bk, bkB = self.bank()
                            for c in range(8):
                                self.mm(bk[:, 0:256], wv[:, c, ml * 128:(ml + 1) * 128], mT[:, c, :], c == 0, c == 7, [mTB, wB], bkB)
                            A("dve", lambda e, bk=bk, m=m: e.tensor_copy(out=kTb[:, m, :], in_=bk[:, 0:256]), reads=[bkB], writes=[kTbB])
            s1, s2 = Buf(), Buf()
            if self.cfg.get("scr", True):
                A("sp", lambda e, l=l: e.dma_start(out=self.kT_scr[l], in_=big[:, 6144:8192]), reads=[kTbB], writes=[s1], dma=True)
                A("sp", lambda e, l=l: e.dma_start(out=self.v_scr[l], in_=big[:, 4096:6144]), reads=[vbfB], writes=[s2], dma=True)
            if l == 0:
                self.kscrB, self.vscrB = [], []
            self.kscrB.append(s1)
            self.vscrB.append(s2)

    def scrB0(self):
        if not hasattr(self, "_scrB0"):
            self._scrB0 = self.rbufs("scr", 1)[0]
        return self._scrB0

    def odd_mixer(self, o):
        self.S.cur_tag = "odd"
        A = self.add
        l, pi = self.l, self.pi
        big = self.big
        self.region_reset("big", [])
        self.region_reset("scr", [])
        EXT = 1186
        uext = big[:, 0:8 * EXT].rearrange("p (c t) -> p c t", c=8)
        ycv = big[:, 8 * EXT:8 * EXT + 8 * TCOLS].rearrange("p (c t) -> p c t", c=8)
        uB = [[b for b in self.rbufs("big", 3)] for _ in range(8)]
        uctxB = self.rbufs("big", 1)[0]
        ycvB = [[b for b in self.rbufs("big", 3)] for _ in range(8)]
        accs = [self.scr[:, 0:512], self.scr[:, 512:1024]]
        accB = self.rbufs("scr", 2)
        gw = lambda j, k: self.c1024[:, j, 28 + o * 3 + k:28 + o * 3 + k + 1]

        def ucols(j, t, shift=0):
            if t.kind == "P":
                return uext[:, j, t.c0 + shift:t.c0 + shift + t.w]
            return uext[:, j, 1026:1186].rearrange("p (b s) -> p b s", b=NB_S)[:, :, shift:shift + TS]

        if pi == 0:
            A("dve", lambda e: e.memset(uext[:, :, 0:2], 0.0), writes=[uctxB])
            def ev(c0, k, view, bkB):
                for j in range(c0, c0 + k):
                    dst = uext[:, j, 1026:1186].rearrange("p (b s) -> p b s", b=NB_S)[:, :, 0:2]
                    src = view[:, j - c0, :].rearrange("p (b r) -> p b r", b=NB_S)
                    A("dve", lambda e, dst=dst, src=src: e.tensor_copy(out=dst, in_=src), reads=[bkB], writes=[uctxB])
            self.load_rows_T(self.st_gconv[o].rearrange("b r d -> (b r) d"), 32, D, ev)
        else:
            A("dve", lambda e: e.tensor_copy(out=uext[:, :, 0:2], in_=self.car_g[o][:, :, :]), reads=[self.car_gB[o]], writes=[uctxB])

        def view3(ap, t):
            return ap if t.kind == "P" else ap.rearrange("p (b s) -> p b s", b=NB_S)

        for part, name in ((0, "xin"), (2, "gc"), (1, "gb")):
            for u in range(2):
                wv, wB, _ = self.wget(("win", pi, l, part, u))
                for ml in range(4):
                    j = u * 4 + ml
                    for t in self.tiles:
                        bk, bkB = self.bank()
                        for c in range(8):
                            self.mm(bk[:, 0:t.w], wv[:, c, ml * 128:(ml + 1) * 128], self.h[:, c, t.cols], c == 0, c == 7,
                                    [wB, self.hB[c][t.n]], bkB)
                        bv = view3(bk[:, 0:t.w], t)
                        if part == 0:
                            A("act", lambda e, j=j, t=t, bv=bv: e.activation(out=ucols(j, t, 2), in_=bv, func=ACTF.Copy),
                              reads=[bkB], writes=[uB[j][t.n]])
                        elif part == 2:
                            A("dve", lambda e, j=j, t=t, bv=bv: e.tensor_tensor(out=ucols(j, t, 2), in0=bv, in1=ucols(j, t, 2), op=ALU.mult),
                              reads=[bkB, uB[j][t.n]], writes=[uB[j][t.n]])
                        else:
                            i = (j * 3 + t.n) % 2
                            acc, aB = accs[i], accB[i]
                            av = view3(acc[:, 0:t.w], t)
                            rd = [uB[j][t.n], uctxB, self.cB] + ([uB[j][t.n - 1]] if (t.kind == "P" and t.n > 0) else [])
                            A("dve", lambda e, j=j, t=t, av=av: e.tensor_scalar(out=av, in0=ucols(j, t, 0), scalar1=gw(j, 0), scalar2=None,
                                                                               op0=ALU.mult), reads=rd, writes=[aB])
                            for k in (1, 2):
                                A("dve", lambda e, j=j, t=t, av=av, k=k: e.scalar_tensor_tensor(
                                    out=av, in0=ucols(j, t, k), scalar=gw(j, k), in1=av, op0=ALU.mult, op1=ALU.add),
                                  reads=rd + [aB], writes=[aB])
                            A("dve", lambda e, j=j, t=t, acc=acc, bk=bk: e.tensor_tensor(out=ycv[:, j, t.cols], in0=bk[:, 0:t.w],
                                                                                        in1=acc[:, 0:t.w], op=ALU.mult),
                              reads=[bkB, aB], writes=[ycvB[j][t.n]])
        last = self.tiles[-1]
        if pi == 0:
            A("act", lambda e: e.activation(out=self.car_g[o][:, :, :], in_=uext[:, :, 1024:1026], func=ACTF.Copy),
              reads=[uB[j][1] for j in range(8)], writes=[self.car_gB[o]])
            for j in range(8):
                src = uext[:, j, 1026:1186].rearrange("p (b s) -> p b s", b=NB_S)[:, :, 8:10].rearrange("p b r -> p r b")
                dst = self.cstate[:, j, 0:32].rearrange("p (r b) -> p r b", r=2)
                A("act", lambda e, src=src, dst=dst: e.activation(out=dst, in_=src, func=ACTF.Copy), reads=[uB[j][2]], writes=[self.cstateB])
            srcs = [(self.cstate[:, j, 0:32], [self.cstateB]) for j in range(8)]
            self.store_rows_T_multi(srcs, 32, [(r * 16, 16, self.o_gconv_s[o][:, r, :]) for r in range(2)])
        else:
            for j in range(8):
                A("act", lambda e, j=j: e.activation(out=self.cstate[:, j, 0:2], in_=uext[:, j, 1024:1026], func=ACTF.Copy),
                  reads=[uB[j][1]], writes=[self.cstateB])
            srcs = [(self.cstate[:, j, 0:2], [self.cstateB]) for j in range(8)]
            self.store_rows_T_multi(srcs, 2, [(0, 2, self.o_gconv_p[o])])
        self.out_proj(("wout", pi, l), ycv, ycvB, 1, (l, 2))

    def store_rows_T_multi(self, srcs, R, dsts):
        nch = len(srcs)
        for g0 in range(0, nch, 8):
            g = srcs[g0:g0 + 8]
            stg, stgB = self.next_stg()
            done = 0
            while done < len(g):
                k = min(4, len(g) - done)
                bk, bkB = self.bank()
                for j in range(k):
                    ap, bufs = g[done + j]
                    self.tr(bk[0:R, j * 128:(j + 1) * 128], ap, self.ident[:, :], list(bufs) + [self.constB], bkB)
                o = stg[0:R, done * 128:(done + k) * 128]
                i_ = bk[0:R, 0:k * 128]
                self.add("act", lambda e, o=o, i_=i_: e.activation(out=o, in_=i_, func=ACTF.Copy), reads=[bkB], writes=[stgB])
                done += k
            for (r0, nr, dap) in dsts:
                ob = Buf()
                self.add("sp", lambda e, r0=r0, nr=nr, dap=dap, stg=stg, g0=g0, ng=len(g): e.dma_start(
                    out=dap[:, g0 * 128:(g0 + ng) * 128], in_=stg[r0:r0 + nr, 0:ng * 128]), reads=[stgB], writes=[ob], dma=True)
                self.out_bufs.append(ob)

    def even_mixer(self, e_):
        self.S.cur_tag = "even"
        A = self.add
        l, pi = self.l, self.pi
        big = self.big
        self.region_reset("big", [])
        self.region_reset("scr", [])
        AE, UE = 1407, 1662
        aext = big[:, 0:4 * AE].rearrange("p (c t) -> p c t", c=4)
        uext = big[:, 4 * AE:4 * AE + 4 * UE].rearrange("p (c t) -> p c t", c=4)
        o0 = 4 * AE + 4 * UE
        ycat = big[:, o0:o0 + 8 * TCOLS].rearrange("p (c t) -> p c t", c=8)
        o1 = o0 + 8 * TCOLS
        cbb = big[:, o1:o1 + 2048].rearrange("p (c t) -> p c t", c=4)
        aB = [[b for b in self.rbufs("big", 3)] for _ in range(4)]
        actxB = self.rbufs("big", 1)[0]
        uB = [[b for b in self.rbufs("big", 3)] for _ in range(4)]
        uctxB = self.rbufs("big", 1)[0]
        ycatB = [[b for b in self.rbufs("big", 3)] for _ in range(8)]
        cbbB = self.rbufs("big", 4)
        scr = self.scr
        ping = [scr[:, 0:768], scr[:, 768:1536]]
        pingB = self.rbufs("scr", 2)
        lnt = [scr[:, 1536 + i * 512:1536 + (i + 1) * 512] for i in range(3)]
        lntB = self.rbufs("scr", 3)
        cst = lambda row: (lambda j: self.c512[:, j, row:row + 1])
        pscale = cst(e_)
        cw = lambda j, k: self.c512[:, j, 2 + e_ * 31 + k:2 + e_ * 31 + k + 1]
        cbias, lng, lnb = cst(64 + e_), cst(66 + e_), cst(68 + e_)

        def acols(g, t, lo, n):
            if t.kind == "P":
                return aext[:, g, 15 + t.c0 + lo:15 + t.c0 + lo + n]
            return aext[:, g, 1039:1407].rearrange("p (b s) -> p b s", b=NB_S)[:, :, 15 + lo:15 + lo + n]

        def ucols(j, t, lo, n):
            if t.kind == "P":
                return uext[:, j, 30 + t.c0 + lo:30 + t.c0 + lo + n]
            return uext[:, j, 1054:1662].rearrange("p (b s) -> p b s", b=NB_S)[:, :, 30 + lo:30 + lo + n]

        def view3(ap, t):
            return ap if t.kind == "P" else ap.rearrange("p (b s) -> p b s", b=NB_S)

        if pi == 0:
            A("dve", lambda e: e.memset(aext[:, :, 0:15], 0.0), writes=[actxB])
            A("dve", lambda e: e.memset(uext[:, :, 0:30], 0.0), writes=[uctxB])
            for (st, nr, ext, base, tot, ctxB) in ((self.st_pool[e_], 15, aext, 1039, 23, actxB), (self.st_conf[e_], 30, uext, 1054, 38, uctxB)):
                bper = 128 // nr
                for b0 in range(0, NB_S, bper):
                    nb = min(bper, NB_S - b0)
                    R = nb * nr

                    def ev(c0, k, view, bkB, ext=ext, base=base, tot=tot, nr=nr, b0=b0, nb=nb, ctxB=ctxB):
                        for j in range(c0, c0 + k):
                            dst = ext[:, j, base + b0 * tot:base + (b0 + nb) * tot].rearrange("p (b s) -> p b s", b=nb)[:, :, 0:nr]
                            src = view[:, j - c0, :].rearrange("p (b r) -> p b r", b=nb)
                            A("dve", lambda e, dst=dst, src=src: e.tensor_copy(out=dst, in_=src), reads=[bkB], writes=[ctxB])
                    self.load_rows_T(st[b0:b0 + nb].rearrange("b r d -> (b r) d"), R, 512, ev)
            for (st, ost, keep, nr) in ((self.st_pool[e_], self.o_pool_s[e_], 7, 15), (self.st_conf[e_], self.o_conf_s[e_], 22, 30)):
                ob = Buf()
                A("sp", lambda e, st=st, ost=ost, keep=keep, nr=nr: e.dma_start(
                    out=ost[:, 0:keep, :].rearrange("b r d -> b (r d)"), in_=st[:, nr - keep:nr, :].rearrange("b r d -> b (r d)")),
                  writes=[ob], dma=True)
                self.out_bufs.append(ob)
        else:
            A("dve", lambda e: e.tensor_copy(out=aext[:, :, 0:15], in_=self.car_a[e_][:, :, :]), reads=[self.car_aB[e_]], writes=[actxB])
            A("dve", lambda e: e.tensor_copy(out=uext[:, :, 0:30], in_=self.car_u[e_][:, :, :]), reads=[self.car_uB[e_]], writes=[uctxB])

        wa, waB, _ = self.wget(("win", pi, l, 0))
        for g in range(4):
            for t in self.tiles:
                bk, bkB = self.bank()
                for c in range(8):
                    self.mm(bk[:, 0:t.w], wa[:, c, g * 128:(g + 1) * 128], self.h[:, c, t.cols], c == 0, c == 7, [waB, self.hB[c][t.n]], bkB)
                A("act", lambda e, g=g, t=t, bk=bk: e.activation(out=acols(g, t, 0, TS if t.kind == "S" else t.w), in_=view3(bk[:, 0:t.w], t),
                                                                func=ACTF.Copy), reads=[bkB], writes=[aB[g][t.n]])
        wp, wpB, _ = self.wget(("wpool", pi, l))
        for g in range(4):
            wlen = POOLW[g]
            for t in self.tiles:
                n = TS if t.kind == "S" else t.w
                nseq = NB_S if t.kind == "S" else 1
                rd = [aB[g][t.n], actxB] + ([aB[g][t.n - 1]] if (t.kind == "P" and t.n > 0) else [])
                prev = lambda lo, cnt, g=g, t=t: acols(g, t, lo, cnt)
                prevB = rd
                s = 1
                k = 0
                while s < wlen:
                    lo = -(wlen - 2 * s)
                    cnt = n - lo
                    buf, bufB = ping[k % 2], pingB[k % 2]

                    def lvl(lo2, cnt2, buf=buf, lo=lo, cnt=cnt, nseq=nseq):
                        if nseq == 1:
                            return buf[:, lo2 - lo:lo2 - lo + cnt2]
                        return buf[:, 0:nseq * cnt].rearrange("p (b s) -> p b s", b=nseq)[:, :, lo2 - lo:lo2 - lo + cnt2]
                    A("dve", lambda e, lvl=lvl, prev=prev, lo=lo, cnt=cnt, s=s: e.tensor_tensor(
                        out=lvl(lo, cnt), in0=prev(lo, cnt), in1=prev(lo - s, cnt), op=ALU.add), reads=prevB, writes=[bufB])
                    prev, prevB = lvl, [bufB]
                    s *= 2
                    k += 1
                if t.kind == "P" and t.tok0 == 0:
                    A("dve", lambda e, prev=prev, g=g: e.tensor_tensor(out=prev(0, 15), in0=prev(0, 15), in1=self.poolfix[:, g, :], op=ALU.mult),
                      reads=prevB + [self.constB], writes=prevB)
                i = self.sl_rr % 2
                self.sl_rr += 1
                pl, plB = self.sl[i], self.slB[i]
                A("dve", lambda e, prev=prev, pl=pl, t=t, g=g, n=n, wlen=wlen: e.scalar_tensor_tensor(
                    out=view3(pl[:, 0:t.w], t), in0=prev(0, n), scalar=1.0 / wlen, in1=acols(g, t, 0, n), op0=ALU.mult, op1=ALU.subtract),
                  reads=prevB + [aB[g][t.n]], writes=[plB])
                bk, bkB = self.bank()
                self.mm(bk[:, 0:t.w], wp[:, g, :], pl[:, 0:t.w], True, True, [wpB, plB], bkB)
                A("act", lambda e, g=g, t=t, bk=bk: e.activation(out=ycat[:, g, t.cols], in_=bk[:, 0:t.w], func=ACTF.Copy, scale=pscale(g)),
                  reads=[bkB, self.cB], writes=[ycatB[g][t.n]])
        if pi == 0:
            A("act", lambda e: e.activation(out=self.car_a[e_][:, :, :], in_=aext[:, :, 1024:1039], func=ACTF.Copy),
              reads=[aB[g][1] for g in range(4)], writes=[self.car_aB[e_]])
            for g in range(4):
                src = aext[:, g, 1039:1407].rearrange("p (b s) -> p b s", b=NB_S)[:, :, 15:23].rearrange("p b r -> p r b")
                dst = self.cstate[:, g, :].rearrange("p (r b) -> p r b", r=TS)
                A("act", lambda e, src=src, dst=dst: e.activation(out=dst, in_=src, func=ACTF.Copy), reads=[aB[g][2]], writes=[self.cstateB])
            self.store_rows_T_multi([(self.cstate[:, g, :], [self.cstateB]) for g in range(4)], 128,
                                    [(r * 16, 16, self.o_pool_s[e_][:, 7 + r, :]) for r in range(TS)])
        else:
            for g in range(4):
                A("act", lambda e, g=g: e.activation(out=self.cstate[:, g, 0:15], in_=aext[:, g, 1024:1039], func=ACTF.Copy),
                  reads=[aB[g][1]], writes=[self.cstateB])
            self.store_rows_T_multi([(self.cstate[:, g, 0:15], [self.cstateB]) for g in range(4)], 15, [(0, 15, self.o_pool_p[e_])])

        w1, w1B, _ = self.wget(("win", pi, l, 1))
        w2, w2B, _ = self.wget(("win", pi, l, 2), cont=True)
        for j in range(4):
            for t in self.tiles:
                n = TS if t.kind == "S" else t.w
                bk2, bk2B = self.bank()
                for c in range(8):
                    self.mm(bk2[:, 0:t.w], w2[:, c, j * 128:(j + 1) * 128], self.h[:, c, t.cols], c == 0, c == 7, [w2B, self.hB[c][t.n]], bk2B)
                i = self.sl_rr % 2
                self.sl_rr += 1
                sg, sgB = self.sl[i], self.slB[i]
                A("act", lambda e, sg=sg, bk2=bk2, t=t: e.activation(out=sg[:, 0:t.w], in_=bk2[:, 0:t.w], func=ACTF.Sigmoid),
                  reads=[bk2B], writes=[sgB])
                bk1, bk1B = self.bank()
                for c in range(8):
                    self.mm(bk1[:, 0:t.w], w1[:, c, j * 128:(j + 1) * 128], self.h[:, c, t.cols], c == 0, c == 7, [w1B, self.hB[c][t.n]], bk1B)
                A("dve", lambda e, j=j, t=t, n=n, bk1=bk1, sg=sg: e.tensor_tensor(out=ucols(j, t, 0, n), in0=view3(bk1[:, 0:t.w], t),
                                                                                 in1=view3(sg[:, 0:t.w], t), op=ALU.mult),
                  reads=[bk1B, sgB], writes=[uB[j][t.n]])
        if pi == 0:
            A("act", lambda e: e.activation(out=self.car_u[e_][:, :, :], in_=uext[:, :, 1024:1054], func=ACTF.Copy),
              reads=[uB[j][1] for j in range(4)], writes=[self.car_uB[e_]])
            for j in range(4):
                src = uext[:, j, 1054:1662].rearrange("p (b s) -> p b s", b=NB_S)[:, :, 30:38].rearrange("p b r -> p r b")
                dst = self.cstate[:, 4 + j, :].rearrange("p (r b) -> p r b", r=TS)
                A("act", lambda e, src=src, dst=dst: e.activation(out=dst, in_=src, func=ACTF.Copy), reads=[uB[j][2]], writes=[self.cstateB])
            self.store_rows_T_multi([(self.cstate[:, 4 + j, :], [self.cstateB]) for j in range(4)], 128,
                                    [(r * 16, 16, self.o_conf_s[e_][:, 22 + r, :]) for r in range(TS)])
        else:
            for j in range(4):
                A("act", lambda e, j=j: e.activation(out=self.cstate[:, 4 + j, 0:30], in_=uext[:, j, 1024:1054], func=ACTF.Copy),
                  reads=[uB[j][1]], writes=[self.cstateB])
            self.store_rows_T_multi([(self.cstate[:, 4 + j, 0:30], [self.cstateB]) for j in range(4)], 30, [(0, 30, self.o_conf_p[e_])])

        self.region_reset("scr", [])
        scrb = scr[:, :].bitcast(BF16)
        cb = scrb[:, 0:4 * TCOLS].rearrange("p (c t) -> p c t", c=4)
        cbB = [[b for b in self.rbufs("scr", 3)] for _ in range(4)]
        dg0 = 4 * TCOLS
        assert dg0 + 31 * 128 <= 2 * self.SCRN
        diag = scrb[:, dg0:dg0 + 31 * 128].rearrange("p (k d) -> p k d", k=31)
        diagB = self.rbufs("scr", 1)[0]
        for j in range(4):
            for k in range(31):
                A("dve", lambda e, j=j, k=k: e.tensor_scalar(out=diag[:, k, :], in0=self.identb[:, :], scalar1=cw(j, k), scalar2=None,
                                                            op0=ALU.mult), reads=[self.constB, self.cB], writes=[diagB])
            for t in self.tiles:
                n = TS if t.kind == "S" else t.w
                bk, bkB = self.bank()
                rd = [diagB, uB[j][t.n], uctxB] + ([uB[j][t.n - 1]] if (t.kind == "P" and t.n > 0) else [])
                for k in range(31):
                    self.mm(view3(bk[:, 0:t.w], t), diag[:, k, :], ucols(j, t, k - 30, n), k == 0, k == 30, rd, bkB)
                A("act", lambda e, j=j, t=t, bk=bk: e.activation(out=cb[:, j, t.cols], in_=bk[:, 0:t.w], func=ACTF.Identity, bias=cbias(j), scale=1.0),
                  reads=[bkB, self.cB], writes=[cbB[j][t.n]])
        for t in self.tiles:
            w_ = t.w
            stA, stAB = self.stbank()
            stB_, stBB = self.stbank()
            for j in range(4):
                self.mm(stA[:, 0:w_], self.ones512[:, :], cb[:, j, t.cols], j == 0, j == 3, [cbB[j][t.n], self.constB], stAB)
                i = self.sq_rr % 4
                self.sq_rr += 1
                sq, sqB = self.sq[i], self.sqB[i]
                A("act", lambda e, j=j, t=t, sq=sq: e.activation(out=sq[:, 0:t.w], in_=cb[:, j, t.cols], func=ACTF.Square), reads=[cbB[j][t.n]], writes=[sqB])
                self.mm(stB_[:, 0:w_], self.ones512[:, :], sq[:, 0:w_], j == 0, j == 3, [sqB, self.constB], stBB)
            mean_sb, meanB = self.tmp[0], self.tmpB[0]
            m2, m2B = self.tmp[1], self.tmpB[1]
            A("act", lambda e, w_=w_, stA=stA: e.activation(out=mean_sb[0:1, 0:w_], in_=stA[0:1, 0:w_], func=ACTF.Copy), reads=[stAB], writes=[meanB])
            A("dve", lambda e, w_=w_: e.tensor_tensor(out=m2[0:1, 0:w_], in0=mean_sb[0:1, 0:w_], in1=mean_sb[0:1, 0:w_], op=ALU.mult), reads=[meanB], writes=[m2B])
            A("dve", lambda e, w_=w_, stB_=stB_: e.scalar_tensor_tensor(out=self.vsb[0:1, 0:w_], in0=stB_[0:1, 0:w_], scalar=LN_EPS, in1=m2[0:1, 0:w_],
                                                                        op0=ALU.add, op1=ALU.subtract), reads=[stBB, m2B], writes=[self.vsbB])
            A("act", lambda e, w_=w_: e.activation(out=self.vsb[0:1, 0:w_], in_=self.vsb[0:1, 0:w_], func=ACTF.Sqrt), reads=[self.vsbB], writes=[self.vsbB])
            r, rB, r1, r1B = self.row_pow_bcast(w_)
            A("dve", lambda e, w_=w_, r1=r1: e.scalar_tensor_tensor(out=m2[0:1, 0:w_], in0=mean_sb[0:1, 0:w_], scalar=-1.0, in1=r1[0:1, 0:w_],
                                                                    op0=ALU.mult, op1=ALU.mult), reads=[meanB, r1B], writes=[m2B])
            nm, nmB = self.bank()
            self.bcast_row(nm, nmB, m2, m2B, w_)
            for j in range(4):
                i2 = self.sl_rr % 2
                self.sl_rr += 1
                z, zB = self.sl[i2], self.slB[i2]
                i3 = self.tmp_rr % 2
                self.tmp_rr += 1
                A("dve", lambda e, j=j, t=t, w_=w_, z=z, r=r: e.tensor_tensor(out=z[:, 0:w_], in0=cb[:, j, t.cols], in1=r[:, 0:w_], op=ALU.mult),
                  reads=[cbB[j][t.n], rB], writes=[zB])
                A("dve", lambda e, w_=w_, z=z, nm=nm: e.tensor_tensor(out=z[:, 0:w_], in0=z[:, 0:w_], in1=nm[:, 0:w_], op=ALU.add),
                  reads=[zB, nmB], writes=[zB])
                A("act", lambda e, j=j, t=t, z=z, w_=w_: e.activation(out=ycat[:, 4 + j, t.cols], in_=z[:, 0:w_], func=ACTF.Silu,
                                                                     bias=lnb(j), scale=lng(j)), reads=[zB, self.cB], writes=[ycatB[4 + j][t.n]])
        self.out_proj(("wout", pi, l), ycat, ycatB, 1, (l, 2))

    def attention(self, l):
        self.S.cur_tag = "attn"
        A = self.add
        pi = self.pi
        big = self.big
        self.region_reset("big", [])
        qT = big[:, 0:9216].rearrange("p (c t) -> p c t", c=8)
        oT = big[:, 9216:18432].rearrange("p (c t) -> p c t", c=8)
        kst = big[:, 18432:20480].rearrange("p (k d) -> p k d", k=2)
        vst = big[:, 20480:22528].rearrange("p (k d) -> p k d", k=2)
        kT = big[:, 22528:24576].rearrange("p (c k) -> p c k", c=8)
        ets = big[:, 24576:24640]
        qB = [[b for b in self.rbufs("big", 3)] for _ in range(8)]
        oB = [[b for b in self.rbufs("big", 3)] for _ in range(8)]
        kstB, vstB, kTB, etsB = self.rbufs("big", 4)
        self.region_reset("scr", [])
        kst1 = self.scr[:, 0:1024].bitcast(BF16).rearrange("p (k d) -> p k d", k=2)
        vst1 = self.scr[:, 1024:2048].bitcast(BF16).rearrange("p (k d) -> p k d", k=2)
        kst1B, vst1B = self.rbufs("scr", 2)
        kv = [(kst, kstB, vst, vstB), (kst1, kst1B, vst1, vst1B)]
        A("sp", lambda e: e.dma_start(out=self.kTp[:, :, :].rearrange("p c k -> p (c k)"), in_=self.kT_scr[l]),
          reads=[self.kscrB[l]], writes=[self.kTpB], dma=True)
        A("sp", lambda e: e.dma_start(out=self.vp[:, :, :].rearrange("p k d -> p (k d)"), in_=self.v_scr[l]),
          reads=[self.vscrB[l]], writes=[self.vpB], dma=True)
        for u in range(2):
            wv, wB, _ = self.wget(("wq", pi, l, u))
            for ml in range(4):
                m = u * 4 + ml
                for t in self.tiles:
                    bk, bkB = self.bank()
                    for c in range(8):
                        self.mm(bk[:, 0:t.w], wv[:, c, ml * 128:(ml + 1) * 128], self.h[:, c, t.cols], c == 0, c == 7,
                                [wB, self.hB[c][t.n]], bkB)
                    A("act", lambda e, m=m, t=t, bk=bk: e.activation(out=qT[:, m, t.cols], in_=bk[:, 0:t.w], func=ACTF.Copy, scale=1.0 / 16.0),
                      reads=[bkB], writes=[qB[m][t.n]])
        for t in self.tiles:
            if t.kind == "P":
                def scores(hd, t=t):
                    i = self.et_rr % 2
                    self.et_rr += 1
                    et, etB = self.et[i], self.etB[i]
                    den, denB = self.stbank()
                    for kc in range(2):
                        bk, bkB = self.bank()
                        for dc in range(2):
                            m = 2 * hd + dc
                            self.mm(bk[:, 0:t.w], self.kTp[:, m, kc * 128:(kc + 1) * 128], qT[:, m, t.cols], dc == 0, dc == 1,
                                    [self.kTpB, qB[m][t.n]], bkB)
                        A("act", lambda e, kc=kc, et=et, bk=bk: e.activation(out=et[:, kc, 0:t.w], in_=bk[:, 0:t.w], func=ACTF.Exp),
                          reads=[bkB], writes=[etB])
                    for kc in range(2):
                        self.mm(den[:, 0:t.w], self.ones1[:, :], et[:, kc, 0:t.w], kc == 0, kc == 1, [etB, self.constB], denB)
                    return et, etB, den, denB

                def pv(hd, ctx, t=t):
                    et, etB, den, denB = ctx
                    i2 = self.rstd_rr % 2
                    self.rstd_rr += 1
                    rd_, rdB = self.rstd[i2], self.rstdB[i2]
                    A("dve", lambda e: e.reciprocal(out=rd_[:, 0:t.w], in_=den[:, 0:t.w]), reads=[denB], writes=[rdB])
                    for dc in range(2):
                        m = 2 * hd + dc
                        bk, bkB = self.bank()
                        for kc in range(2):
                            self.mm(bk[:, 0:t.w], self.vp[:, kc, m * 128:(m + 1) * 128], et[:, kc, 0:t.w], kc == 0, kc == 1,
                                    [self.vpB, etB], bkB)
                        A("dve", lambda e, m=m, bk=bk: e.tensor_tensor(out=oT[:, m, t.cols], in0=bk[:, 0:t.w], in1=rd_[:, 0:t.w],
                                                                      op=ALU.mult), reads=[bkB, rdB], writes=[oB[m][t.n]])
                ctxs = {0: scores(0)}
                for hd in range(4):
                    if hd + 1 < 4:
                        ctxs[hd + 1] = scores(hd + 1)
                    pv(hd, ctxs.pop(hd))
            else:
                kT1 = self.scr[:, 2048:3072].bitcast(BF16).rearrange("p (c k) -> p c k", c=8)
                kT1B, ets1B = self.rbufs("scr", 2)
                ets1 = self.scr[:, 3072:3104].bitcast(BF16)
                kTs = [(kT, kTB), (kT1, kT1B)]
                etss = [(ets, etsB), (ets1, ets1B)]

                def dmaK(b):
                    kst, kstB, _, _ = kv[b % 2]
                    A("pool", lambda e: e.dma_start(out=kst, in_=self.ck[l, b].rearrange("(k p) d -> p k d", p=128)), writes=[kstB], dma=True)

                def dmaV(b):
                    _, _, vst, vstB = kv[b % 2]
                    A("pool", lambda e: e.dma_start(out=vst, in_=self.cv[l, b].rearrange("(k p) d -> p k d", p=128)), writes=[vstB], dma=True)

                def T(b):
                    kst, kstB, _, _ = kv[b % 2]
                    kT_, kT_B = kTs[b % 2]
                    for half in range(2):
                        bk, bkB = self.bank()
                        bkb = bk[:, :].bitcast(BF16)
                        for ml in range(4):
                            m = half * 4 + ml
                            for kc in range(2):
                                self.tr(bkb[:, ml * 256 + kc * 128: ml * 256 + (kc + 1) * 128], kst[:, kc, m * 128:(m + 1) * 128],
                                        self.identb[:, :], [kstB, self.constB], bkB)
                        A("act", lambda e, half=half, bkb=bkb: e.activation(out=kT_[:, half * 4:(half + 1) * 4, :],
                                                                           in_=bkb.rearrange("p (c k) -> p c k", c=4), func=ACTF.Copy),
                          reads=[bkB], writes=[kT_B])

                def S(b):
                    cs = slice(t.c0 + b * TS, t.c0 + (b + 1) * TS)
                    kT_, kT_B = kTs[b % 2]
                    ets_, ets_B = etss[b % 2]
                    sbk, sbkB = self.bank()
                    for hd in range(4):
                        for kc in range(2):
                            for dc in range(2):
                                m = 2 * hd + dc
                                self.mm(sbk[:, (hd * 2 + kc) * TS:(hd * 2 + kc + 1) * TS], kT_[:, m, kc * 128:(kc + 1) * 128], qT[:, m, cs],
                                        dc == 0, dc == 1, [kT_B, qB[m][t.n]], sbkB)
                    A("act", lambda e: e.activation(out=ets_[:, 0:64], in_=sbk[:, 0:64], func=ACTF.Exp), reads=[sbkB], writes=[ets_B])

                def PV(b):
                    cs = slice(t.c0 + b * TS, t.c0 + (b + 1) * TS)
                    _, _, vst, vstB = kv[b % 2]
                    ets_, ets_B = etss[b % 2]
                    den, denB = self.stbank()
                    e4 = ets_[:, 0:64].rearrange("p (h k q) -> p h k q", h=4, k=2)
                    for kc in range(2):
                        self.mm(den[:, 0:32].rearrange("p (h q) -> p h q", h=4), self.ones1[:, :], e4[:, :, kc, :], kc == 0, kc == 1,
                                [ets_B, self.constB], denB)
                    i2 = self.rstd_rr % 2
                    self.rstd_rr += 1
                    rd_, rdB = self.rstd[i2], self.rstdB[i2]
                    A("dve", lambda e: e.reciprocal(out=rd_[:, 0:32], in_=den[:, 0:32]), reads=[denB], writes=[rdB])
                    obk, obkB = self.bank()
                    for m in range(8):
                        hd = m // 2
                        for kc in range(2):
                            self.mm(obk[:, m * TS:(m + 1) * TS], vst[:, kc, m * 128:(m + 1) * 128],
                                    ets_[:, (hd * 2 + kc) * TS:(hd * 2 + kc + 1) * TS], kc == 0, kc == 1, [vstB, ets_B], obkB)
                    o4 = obk[:, 0:64].rearrange("p (h d q) -> p h d q", h=4, d=2)
                    for dc in range(2):
                        dst = oT[:, :, cs].rearrange("p (h d) q -> p h d q", d=2)[:, :, dc, :]
                        A("dve", lambda e, dst=dst, dc=dc: e.tensor_tensor(
                            out=dst, in0=o4[:, :, dc, :], in1=rd_[:, 0:32].rearrange("p (h q) -> p h q", h=4), op=ALU.mult),
                          reads=[obkB, rdB], writes=[oB[2 * hh + dc][t.n] for hh in range(4)])

                dmaK(0); dmaV(0); dmaK(1); dmaV(1)
                T(0); dmaK(2)
                T(1); dmaK(3)
                S(0)
                for b in range(NB_S):
                    if b + 1 < NB_S:
                        S(b + 1)
                    PV(b)
                    if b + 2 < NB_S:
                        dmaV(b + 2)
                        T(b + 2)
                        if b + 4 < NB_S:
                            dmaK(b + 4)
        self.out_proj(("wo", pi, l), oT, oB, 3, (l, 4))

    def ffn(self, l):
        self.S.cur_tag = "ffn"
        A = self.add
        pi = self.pi
        big = self.big
        self.region_reset("big", [])
        self.region_reset("scr", [])
        act = big[:, 0:22 * TCOLS].rearrange("p (c t) -> p c t", c=22)
        actB = [[b for b in self.rbufs("big", 3)] for _ in range(22)]
        scr = self.scr
        GE = 1186
        gext = scr[:, 0:GE]
        gextB = self.rbufs("scr", 3)
        gctxB = self.rbufs("scr", 1)[0]
        accs = [scr[:, 1536:2048], scr[:, 2048:2560], scr[:, 2560:3072]]
        accB = self.rbufs("scr", 3)
        fw = lambda m, k: self.c2816[:, m, l * 3 + k:l * 3 + k + 1]
        fb = lambda m: self.c2816[:, m, 12 + l:12 + l + 1]

        def gcols(t, shift):
            if t.kind == "P":
                return gext[:, t.c0 + shift:t.c0 + shift + t.w]
            return gext[:, 1026:1186].rearrange("p (b s) -> p b s", b=NB_S)[:, :, shift:shift + TS]

        def view3(ap, t):
            return ap if t.kind == "P" else ap.rearrange("p (b s) -> p b s", b=NB_S)

        if pi == 0:
            for c0 in range(0, DFF, 1024):
                cw_ = min(1024, DFF - c0)

                def ev(cc, k, view, bkB, base=c0 // 128):
                    A("dve", lambda e: e.tensor_copy(out=self.fctx[:, base + cc:base + cc + k, :], in_=view), reads=[bkB], writes=[self.fctxB])
                self.load_rows_T(self.st_ffn[l].rearrange("b r d -> (b r) d")[:, c0:c0 + cw_], 32, cw_, ev)
        for u in range(6):
            wg, wgB, cwid = self.wget(("wg", pi, l, u))
            wu, wuB, _ = self.wget(("wu", pi, l, u), cont=True)
            for ml in range(cwid // 128):
                m = u * 4 + ml
                if pi == 0:
                    A("pool", lambda e: e.tensor_copy(out=gext[:, 0:2], in_=self.zeros[:, 0:2]), reads=[self.constB], writes=[gctxB])
                    A("pool", lambda e, m=m: e.tensor_copy(out=gext[:, 1026:1186].rearrange("p (b s) -> p b s", b=NB_S)[:, :, 0:2],
                                                          in_=self.fctx[:, m, :].rearrange("p (b r) -> p b r", b=NB_S)),
                      reads=[self.fctxB], writes=[gctxB])
                else:
                    A("pool", lambda e, m=m: e.tensor_copy(out=gext[:, 0:2], in_=self.car_f[l][:, m, :]), reads=[self.car_fB[l]], writes=[gctxB])
                T_ = self.tiles
                bkg_l, bku_l = [], []
                for t in T_:
                    bkg, bkgB = self.bank()
                    for c in range(8):
                        self.mm(bkg[:, 0:t.w], wg[:, c, ml * 128:(ml + 1) * 128], self.h[:, c, t.cols], c == 0, c == 7, [wgB, self.hB[c][t.n]], bkgB)
                    A("act", lambda e, t=t, bkg=bkg: e.activation(out=gcols(t, 2), in_=view3(bkg[:, 0:t.w], t), func=ACTF.Copy),
                      reads=[bkgB], writes=[gextB[t.n]])
                    acc, aB = accs[t.n], accB[t.n]
                    A("act", lambda e, m=m, t=t, bkg=bkg, acc=acc: e.activation(out=acc[:, 0:t.w], in_=bkg[:, 0:t.w], func=ACTF.Identity,
                                                                              bias=fb(m), scale=fw(m, 2)), reads=[bkgB, self.cB], writes=[aB])
                    bkg_l.append((bkg, bkgB))
                for t in T_:
                    bku, bkuB = self.bank()
                    for c in range(8):
                        self.mm(bku[:, 0:t.w], wu[:, c, ml * 128:(ml + 1) * 128], self.h[:, c, t.cols], c == 0, c == 7, [wuB, self.hB[c][t.n]], bkuB)
                    bku_l.append((bku, bkuB))
                for k in (0, 1):
                    for t in T_:
                        acc, aB = accs[t.n], accB[t.n]
                        av = view3(acc[:, 0:t.w], t)
                        rd = [gextB[t.n], gctxB, self.cB] + ([gextB[t.n - 1]] if (t.kind == "P" and t.n > 0) else [])
                        A("dve", lambda e, m=m, t=t, av=av, k=k: e.scalar_tensor_tensor(out=av, in0=gcols(t, k), scalar=fw(m, k), in1=av,
                                                                                       op0=ALU.mult, op1=ALU.add), reads=rd + [aB], writes=[aB])
                sls = []
                for t in T_:
                    acc, aB = accs[t.n], accB[t.n]
                    i2 = self.sq_rr % 4
                    self.sq_rr += 1
                    sl, slB = self.sq[i2], self.sqB[i2]
                    A("act", lambda e, t=t, acc=acc, sl=sl: e.activation(out=sl[:, 0:t.w], in_=acc[:, 0:t.w], func=ACTF.Silu), reads=[aB], writes=[slB])
                    sls.append((sl, slB))
                for ti, t in enumerate(T_):
                    bku, bkuB = bku_l[ti]
                    sl, slB = sls[ti]
                    A("dve", lambda e, m=m, t=t, bku=bku, sl=sl: e.tensor_tensor(out=act[:, m, t.cols], in0=bku[:, 0:t.w], in1=sl[:, 0:t.w], op=ALU.mult),
                      reads=[bkuB, slB], writes=[actB[m][t.n]])
                if pi == 0:
                    A("pool", lambda e, m=m: e.tensor_copy(out=self.car_f[l][:, m, :], in_=gext[:, 1024:1026]), reads=[gextB[1]], writes=[self.car_fB[l]])
                    src = gext[:, 1026:1186].rearrange("p (b s) -> p b s", b=NB_S)[:, :, 8:10].rearrange("p b r -> p r b")
                    A("pool", lambda e, m=m, src=src: e.tensor_copy(out=self.fstate[:, m, :].rearrange("p (r b) -> p r b", r=2), in_=src),
                      reads=[gextB[2]], writes=[self.fstateB])
                else:
                    A("pool", lambda e, m=m: e.tensor_copy(out=self.fstate[:, m, 0:2], in_=gext[:, 1024:1026]), reads=[gextB[1]], writes=[self.fstateB])
        if pi == 0:
            self.store_rows_T_multi([(self.fstate[:, m, :], [self.fstateB]) for m in range(22)], 32,
                                    [(r * 16, 16, self.o_ffn_s[l][:, r, :]) for r in range(2)])
        else:
            self.store_rows_T_multi([(self.fstate[:, m, 0:2], [self.fstateB]) for m in range(22)], 2, [(0, 2, self.o_ffn_p[l])])
        for ch in range(2):
            ws = [self.wget(("wd", pi, l, ch, kg), cont=(kg > 0)) for kg in range(3)]
            for ml in range(4):
                m = ch * 4 + ml
                for t in self.tiles:
                    bk, bkB = self.bank()
                    kk = 0
                    for kg, (k0, kn) in enumerate(((0, 8), (8, 8), (16, 6))):
                        wv, wB, _ = ws[kg]
                        for c in range(kn):
                            self.mm(bk[:, 0:t.w], wv[:, c, ml * 128:(ml + 1) * 128], act[:, k0 + c, t.cols], kk == 0, kk == 21,
                                    [wB, actB[k0 + c][t.n]], bkB)
                            kk += 1
                    A("act", lambda e, m=m, t=t, bk=bk: e.activation(out=self.h[:, m, t.cols], in_=bk[:, 0:t.w], func=ACTF.Copy),
                      reads=[bkB], writes=[self.hB[m][t.n]])
        nxt = (l + 1, 0) if l + 1 < DEPTH else None
        for t in self.tiles:
            self.post_norm(t, 5)
            if nxt is not None:
                self.l_next_pre(t, nxt)


_NC_CACHE = {}


def _get_nc():
    if "nc" not in _NC_CACHE:
        _NC_CACHE["nc"] = Builder().build()
    return _NC_CACHE["nc"]


def kernel(x_prompt, x_sample, state_pool, state_conf, state_gconv, state_ffn, cache_mem_k, cache_mem_v,
           mem_prompt, norm_gains, w_in_even, w_pool, pool_scale, conf_w, conf_b, conf_ln_g, conf_ln_b,
           w_out_even, w_in_odd, gconv_w, w_out_odd, w_mem_q, w_mem_k, w_mem_v, w_mem_o,
           w_ffn_gate, w_ffn_up, ffn_conv_w, ffn_conv_b, w_ffn_down):
    f = lambda a: np.ascontiguousarray(np.asarray(a, dtype=np.float32))
    shared = {
        "norm_gains": f(norm_gains).reshape(DEPTH * 7, D), "w_in_even": f(w_in_even), "w_pool": f(w_pool),
        "pool_scale": f(pool_scale), "conf_w": f(conf_w).reshape(62, 512), "conf_b": f(conf_b),
        "conf_ln_g": f(conf_ln_g), "conf_ln_b": f(conf_ln_b), "w_out_even": f(w_out_even), "w_in_odd": f(w_in_odd),
        "gconv_w": f(gconv_w).reshape(6, D), "w_out_odd": f(w_out_odd), "w_mem_q": f(w_mem_q), "w_mem_k": f(w_mem_k),
        "w_mem_v": f(w_mem_v), "w_mem_o": f(w_mem_o), "w_ffn_gate": f(w_ffn_gate), "w_ffn_up": f(w_ffn_up),
        "ffn_conv_w": f(ffn_conv_w).reshape(DEPTH * 3, DFF), "ffn_conv_b": f(ffn_conv_b), "w_ffn_down": f(w_ffn_down),
    }
    x_prompt, x_sample = f(x_prompt), f(x_sample)
    state_pool, state_conf, state_gconv, state_ffn = f(state_pool), f(state_conf), f(state_gconv), f(state_ffn)
    cache_mem_k, cache_mem_v, mem_prompt = f(cache_mem_k), f(cache_mem_v), f(mem_prompt)
    in_maps = []
    for c in range(NCORES):
        bs = slice(c * NB_S, (c + 1) * NB_S)
        m = dict(shared)
        m["xp"] = x_prompt[c]
        m["xs"] = x_sample[bs].reshape(NB_S * TS, D)
        m["st_pool"] = state_pool[:, bs]
        m["st_conf"] = state_conf[:, bs]
        m["st_gconv"] = state_gconv[:, bs]
        m["st_ffn"] = state_ffn[:, bs]
        m["ck"] = cache_mem_k[:, bs].reshape(DEPTH, NB_S, NMEM, D)
        m["cv"] = cache_mem_v[:, bs].reshape(DEPTH, NB_S, NMEM, D)
        m["memp"] = mem_prompt[c]
        in_maps.append({k: np.ascontiguousarray(v) for k, v in m.items()})
    nc = _get_nc()
    res = run_bass_kernel_spmd(nc, in_maps, core_ids=list(range(NCORES)))
    R = res.results
    cat = lambda k, ax: np.concatenate([np.asarray(r[k]) for r in R], axis=ax)
    stack = lambda k, ax: np.stack([np.asarray(r[k]) for r in R], axis=ax)
    y_prompt = stack("yp", 0)
    y_sample = cat("ys", 0).reshape(NCORES * NB_S, TS, D)
    pool_p = stack("o_pool_p", 1)
    conf_p = stack("o_conf_p", 1)
    gconv_p = stack("o_gconv_p", 1)
    ffn_p = stack("o_ffn_p", 1)
    mk_p = stack("o_mk_p", 1).reshape(DEPTH, NCORES, NMEM, 4, 256)
    mv_p = stack("o_mv_p", 1).reshape(DEPTH, NCORES, NMEM, 4, 256)
    pool_s = cat("o_pool_s", 1)
    conf_s = cat("o_conf_s", 1)
    gconv_s = cat("o_gconv_s", 1)
    ffn_s = cat("o_ffn_s", 1)
    return (y_prompt, y_sample, pool_p, conf_p, gconv_p, ffn_p, mk_p, mv_p, pool_s, conf_s, gconv_s, ffn_s)
```

```python
from contextlib import ExitStack

import numpy as np
import concourse.bass as bass
import concourse.mybir as mybir
from concourse.bass_utils import run_bass_kernel_spmd

F32 = mybir.dt.float32
BF16 = mybir.dt.bfloat16
ALU = mybir.AluOpType
ACTF = mybir.ActivationFunctionType

NCORES = 8
D = 1024
SEQ = 2048
DEPTH = 4
NB_S = 16
TS = 8
DFF = 2816
NMEM = 256
TCOLS = 1152
RMS_EPS = 1e-6
LN_EPS = 1e-5
POOLW = (2, 4, 8, 16)


class Buf:
    __slots__ = ("w", "r")

    def __init__(self):
        self.w = None
        self.r = {}


class Op:
    __slots__ = ("eng", "fn", "deps", "waited", "sig", "is_dma", "waits", "clock", "idx", "tag")


class Sched:
    def __init__(self, npool=12):
        self.streams = {k: [] for k in ("pe", "act", "dve", "pool", "sp")}
        self.order = []
        self.npool = npool
        self.hist = {k: [] for k in self.streams}

    def add(self, eng, fn, reads=(), writes=(), dma=False):
        op = Op()
        op.eng = eng
        op.fn = fn
        op.is_dma = dma
        op.waited = dma
        op.sig = None
        op.idx = len(self.order)
        op.tag = getattr(self, "cur_tag", "")
        deps = {}
        for b in reads:
            w = b.w
            if w is not None and (w.is_dma or w.eng != eng or eng != "pe"):
                deps[id(w)] = w
        for b in writes:
            w = b.w
            if w is not None and (w.is_dma or w.eng != eng or eng != "pe"):
                deps[id(w)] = w
            for o in b.r.values():
                if o.is_dma or o.eng != eng or eng != "pe":
                    deps[id(o)] = o
        if dma:
            hist = self.hist[eng]
            if len(hist) >= self.npool:
                prev = hist[len(hist) - self.npool]
                deps[id(prev)] = prev
            hist.append(op)
        op.deps = list(deps.values())
        for b in reads:
            b.r[id(op) if dma else eng] = op
        for b in writes:
            b.w = op
            b.r = {}
        self.streams[eng].append(op)
        self.order.append(op)
        return op

    def finalize(self, sems, dma_sems):
        for op in self.order:
            for d in op.deps:
                d.waited = True
        for eng in self.streams:
            cnt = 0
            dcnt = 0
            uses = [0] * self.npool
            for op in self.streams[eng]:
                if op.is_dma:
                    k = dcnt % self.npool
                    uses[k] += 1
                    op.sig = (("d", eng, k), 16 * uses[k])
                    dcnt += 1
                elif op.waited:
                    cnt += 1
                    op.sig = (("e", eng), cnt)
        seen = {eng: {} for eng in self.streams}
        for op in self.order:
            s = seen[op.eng]
            op.waits = []
            for d in sorted(op.deps, key=lambda d: -d.sig[1]):
                key, v = d.sig
                if s.get(key, 0) < v:
                    op.waits.append((key, v))
                    for k2, v2 in d.clock.items():
                        if s.get(k2, 0) < v2:
                            s[k2] = v2
            if op.waited:
                c = dict(s)
                c[op.sig[0]] = op.sig[1]
                op.clock = c
            else:
                op.clock = None

        def handle(key):
            if key[0] == "e":
                return sems[key[1]]
            return dma_sems[key[1]][key[2]]

        def emit(name):
            def body(e):
                for op in self.streams[name]:
                    for key, v in op.waits:
                        e.wait_ge(handle(key), v)
                    ins = op.fn(e)
                    if op.waited:
                        ins.then_inc(handle(op.sig[0]), 16 if op.is_dma else 1)
            return body
        return emit


def merged_hazards(bufs):
    out = {}
    for b in bufs:
        items = list(b.r.items())
        if b.w is not None:
            items.append((id(b.w) if b.w.is_dma else b.w.eng, b.w))
        for k, o in items:
            if k not in out or out[k].idx < o.idx:
                out[k] = o
    return out


class Tile:
    def __init__(self, n, col0, w, kind, tok0):
        self.n, self.c0, self.w, self.kind, self.tok0 = n, col0, w, kind, tok0
        self.cols = slice(col0, col0 + w)


class Builder:
    def __init__(self):
        self.nc = bass.Bass("TRN2", target_bir_lowering=False)
        self.S = Sched()
        self.es = ExitStack()
        self.out_bufs = []
        self.bank_rr = 0
        self.st_rr = 0
        self.region_bufs = {"big": [], "scr": []}
        self.cfg = dict(layers=DEPTH, passes=2, sub=("m", "a", "f"), phase0=True)

    def sb(self, name, shape, dt):
        return self.es.enter_context(self.nc.sbuf_tensor(name, shape, dt))

    def dram_in(self, name, shape):
        return self.nc.dram_tensor(name, list(shape), F32, kind="ExternalInput").ap()

    def dram_out(self, name, shape):
        return self.nc.dram_tensor(name, list(shape), F32, kind="ExternalOutput").ap()

    def rbufs(self, region, n):
        hz = merged_hazards(self.region_bufs[region])
        bs = []
        for _ in range(n):
            b = Buf()
            b.r = dict(hz)
            bs.append(b)
        self.region_bufs[region] = self.region_bufs[region] + bs
        return bs

    def region_reset(self, region, keep):
        hz = merged_hazards(self.region_bufs[region])
        carrier = Buf()
        carrier.r = hz
        self.region_bufs[region] = [carrier] + list(keep)

    def bank(self):
        i = self.bank_rr % 6
        self.bank_rr += 1
        return self.banks[i], self.bankB[i]

    def stbank(self):
        i = 6 + self.st_rr % 2
        self.st_rr += 1
        return self.banks[i], self.bankB[i]

    def add(self, *a, **k):
        return self.S.add(*a, **k)

    def build(self):
        nc = self.nc
        A = self.add
        di = self.dram_in
        self.xp = di("xp", [SEQ, D])
        self.xs = di("xs", [NB_S * TS, D])
        self.st_pool = di("st_pool", [2, NB_S, 15, 512])
        self.st_conf = di("st_conf", [2, NB_S, 30, 512])
        self.st_gconv = di("st_gconv", [2, NB_S, 2, D])
        self.st_ffn = di("st_ffn", [DEPTH, NB_S, 2, DFF])
        self.ck = di("ck", [DEPTH, NB_S, NMEM, D])
        self.cv = di("cv", [DEPTH, NB_S, NMEM, D])
        self.memp = di("memp", [NMEM, D])
        self.norm_gains = di("norm_gains", [DEPTH * 7, D])
        self.w_in_even = di("w_in_even", [2, D, 1536])
        self.w_pool = di("w_pool", [2, 4, 128, 128])
        self.pool_scale = di("pool_scale", [2, 512])
        self.conf_w = di("conf_w", [2 * 31, 512])
        self.conf_b = di("conf_b", [2, 512])
        self.conf_ln_g = di("conf_ln_g", [2, 512])
        self.conf_ln_b = di("conf_ln_b", [2, 512])
        self.w_out_even = di("w_out_even", [2, D, D])
        self.w_in_odd = di("w_in_odd", [2, D, 3 * D])
        self.gconv_w = di("gconv_w", [2 * 3, D])
        self.w_out_odd = di("w_out_odd", [2, D, D])
        self.w_mem_q = di("w_mem_q", [DEPTH, D, D])
        self.w_mem_k = di("w_mem_k", [DEPTH, D, D])
        self.w_mem_v = di("w_mem_v", [DEPTH, D, D])
        self.w_mem_o = di("w_mem_o", [DEPTH, D, D])
        self.w_ffn_gate = di("w_ffn_gate", [DEPTH, D, DFF])
        self.w_ffn_up = di("w_ffn_up", [DEPTH, D, DFF])
        self.ffn_conv_w = di("ffn_conv_w", [DEPTH * 3, DFF])
        self.ffn_conv_b = di("ffn_conv_b", [DEPTH, DFF])
        self.w_ffn_down = di("w_ffn_down", [DEPTH, DFF, D])
        do = self.dram_out
        self.yp = do("yp", [SEQ, D])
        self.ys = do("ys", [NB_S * TS, D])
        self.o_pool_p = do("o_pool_p", [2, 15, 512])
        self.o_conf_p = do("o_conf_p", [2, 30, 512])
        self.o_gconv_p = do("o_gconv_p", [2, 2, D])
        self.o_ffn_p = do("o_ffn_p", [DEPTH, 2, DFF])
        self.o_mk_p = do("o_mk_p", [DEPTH, NMEM, D])
        self.o_mv_p = do("o_mv_p", [DEPTH, NMEM, D])
        self.o_pool_s = do("o_pool_s", [2, NB_S, 15, 512])
        self.o_conf_s = do("o_conf_s", [2, NB_S, 30, 512])
        self.o_gconv_s = do("o_gconv_s", [2, NB_S, 2, D])
        self.o_ffn_s = do("o_ffn_s", [DEPTH, NB_S, 2, DFF])
        self.kT_scr = nc.dram_tensor("kT_scr", [DEPTH, 128, 8 * NMEM], BF16, kind="Internal").ap()
        self.v_scr = nc.dram_tensor("v_scr", [DEPTH, 128, 2 * D], BF16, kind="Internal").ap()

        sb = self.sb
        self.x = sb("x", [128, 8, TCOLS], F32)
        self.h = sb("h", [128, 8, TCOLS], BF16)
        self.BIGN = 25344
        self.big = sb("big", [128, self.BIGN], BF16)
        self.SCRN = 4352
        self.scr = sb("scr", [128, self.SCRN], F32)
        self.ring = [sb(f"ring{i}", [128, 4096], BF16) for i in range(4)]
        self.ringB = [Buf() for _ in range(4)]
        self.stg = [sb(f"stg{i}", [128, 1024], F32) for i in range(2)]
        self.stgB = [Buf() for _ in range(2)]
        self.stg_rr = 0
        self.c1024 = sb("c1024", [128, 8, 34], F32)
        self.c512 = sb("c512", [128, 4, 70], F32)
        self.c2816 = sb("c2816", [128, 22, 16], F32)
        self.cB = Buf()
        self.ident = sb("ident", [128, 128], F32)
        self.identb = sb("identb", [128, 128], BF16)
        self.ones1024 = sb("ones1024", [128, 128], BF16)
        self.ones512 = sb("ones512", [128, 128], BF16)
        self.ones1 = sb("ones1", [128, 128], BF16)
        self.onesf = sb("onesf", [128, 128], F32)
        self.zeros = sb("zeros", [128, 64], F32)
        self.poolfix = sb("poolfix", [128, 4, 15], F32)
        self.constB = Buf()
        self.vsb = sb("vsb", [128, 512], F32)
        self.vsbB = Buf()
        self.rstd = [sb(f"rstd{i}", [128, 512], F32) for i in range(2)]
        self.rstdB = [Buf() for _ in range(2)]
        self.rstd_rr = 0
        self.tmp = [sb(f"tmp{i}", [128, 512], F32) for i in range(2)]
        self.tmpB = [Buf() for _ in range(2)]
        self.tmp_rr = 0
        self.sq = [sb(f"sq{i}", [128, 512], BF16) for i in range(4)]
        self.sqB = [Buf() for _ in range(4)]
        self.sq_rr = 0
        self.et = [sb(f"et{i}", [128, 2, 512], BF16) for i in range(2)]
        self.etB = [Buf() for _ in range(2)]
        self.et_rr = 0
        self.sl = [sb(f"sl{i}", [128, 512], BF16) for i in range(2)]
        self.slB = [Buf() for _ in range(2)]
        self.sl_rr = 0
        self.fstate = sb("fstate", [128, 22, 32], F32)
        self.fstateB = Buf()
        self.fctx = sb("fctx", [128, 22, 32], F32)
        self.fctxB = Buf()
        self.cstate = sb("cstate", [128, 8, 128], F32)
        self.cstateB = Buf()
        self.kTp = sb("kTp", [128, 8, NMEM], BF16)
        self.kTpB = Buf()
        self.vp = sb("vp", [128, 2, D], BF16)
        self.vpB = Buf()
        self.car_a = [sb(f"car_a{e}", [128, 4, 15], BF16) for e in range(2)]
        self.car_u = [sb(f"car_u{e}", [128, 4, 30], BF16) for e in range(2)]
        self.car_g = [sb(f"car_g{o}", [128, 8, 2], BF16) for o in range(2)]
        self.car_f = [sb(f"car_f{l}", [128, 22, 2], F32) for l in range(DEPTH)]
        self.car_aB = [Buf() for _ in range(2)]
        self.car_uB = [Buf() for _ in range(2)]
        self.car_gB = [Buf() for _ in range(2)]
        self.car_fB = [Buf() for _ in range(DEPTH)]
        self.banks = [self.es.enter_context(nc.psum_tensor(f"bk{i}", [128, 512], F32)) for i in range(8)]
        self.bankB = [Buf() for _ in range(8)]
        self.xB = [[Buf() for _ in range(3)] for _ in range(8)]
        self.hB = [[Buf() for _ in range(3)] for _ in range(8)]

        sems = {k: self.es.enter_context(nc.semaphore("s_" + k)) for k in self.S.streams}
        dsems = {k: [self.es.enter_context(nc.semaphore(f"d_{k}_{i}")) for i in range(self.S.npool)]
                 for k in ("sp", "pool")}

        self.plan = []
        self.w_issued = 0
        self.w_consumed = 0
        self.make_plan()

        self.setup_consts()
        passes = [
            [Tile(0, 0, 512, "P", 0), Tile(1, 512, 512, "P", 512), Tile(2, 1024, 128, "S", 0)],
            [Tile(0, 0, 512, "P", 1024), Tile(1, 512, 512, "P", 1536)],
        ]
        if self.cfg["passes"] > 0:
            self.pi = 0
            self.tiles = passes[0]
            self.load_x()
        if self.cfg["phase0"]:
            self.phase0_memkv()
        for pi, tiles in enumerate(passes[:self.cfg["passes"]]):
            self.pi = pi
            self.tiles = tiles
            if pi > 0:
                self.load_x()
            for l in range(self.cfg["layers"]):
                self.l = l
                if "m" in self.cfg["sub"]:
                    if l % 2 == 0:
                        self.even_mixer(l // 2)
                    else:
                        self.odd_mixer(l // 2)
                if "a" in self.cfg["sub"]:
                    self.attention(l)
                if "f" in self.cfg["sub"]:
                    self.ffn(l)
            self.store_x()
        assert self.w_consumed == len(self.plan), (self.w_consumed, len(self.plan))
        A("sp", lambda e: e.nop(), reads=self.out_bufs)

        emit = self.S.finalize(sems, dsems)
        with nc.Block() as block:
            block.tensor(emit("pe"))
            block.scalar(emit("act"))
            block.vector(emit("dve"))
            block.gpsimd(emit("pool"))
            block.sync(emit("sp"))
        self.es.close()
        return nc

    def make_plan(self):
        P = self.plan

        def mat(key, w, kch, ncols, csz=512):
            for u, c0 in enumerate(range(0, ncols, csz)):
                cw = min(csz, ncols - c0)
                P.append((key + (u,), w[:, c0:c0 + cw], kch, cw))

        cfg = self.cfg
        for l in range(DEPTH if cfg["phase0"] else 0):
            mat(("wk", l), self.w_mem_k[l], 8, D)
            mat(("wv", l), self.w_mem_v[l], 8, D)
        for pi in range(cfg["passes"]):
            for l in range(cfg["layers"]):
                if "m" not in cfg["sub"]:
                    pass
                elif l % 2 == 0:
                    e = l // 2
                    wie = self.w_in_even[e]
                    P.append((("win", pi, l, 0), wie[:, 0:512], 8, 512))
                    P.append((("wpool", pi, l), self.w_pool[e], None, None))
                    P.append((("win", pi, l, 1), wie[:, 512:1024], 8, 512))
                    P.append((("win", pi, l, 2), wie[:, 1024:1536], 8, 512))
                    mat(("wout", pi, l), self.w_out_even[e], 8, D)
                else:
                    o = l // 2
                    wi = self.w_in_odd[o]
                    for part in (0, 2, 1):
                        mat(("win", pi, l, part), wi[:, part * D:(part + 1) * D], 8, D)
                    mat(("wout", pi, l), self.w_out_odd[o], 8, D)
                if "a" in cfg["sub"]:
                    mat(("wq", pi, l), self.w_mem_q[l], 8, D)
                    mat(("wo", pi, l), self.w_mem_o[l], 8, D)
                if "f" not in cfg["sub"]:
                    continue
                for u, c0 in enumerate(range(0, DFF, 512)):
                    cw = min(512, DFF - c0)
                    P.append((("wg", pi, l, u), self.w_ffn_gate[l][:, c0:c0 + cw], 8, cw))
                    P.append((("wu", pi, l, u), self.w_ffn_up[l][:, c0:c0 + cw], 8, cw))
                for ch in range(2):
                    for kg, (k0, kn) in enumerate(((0, 8), (8, 8), (16, 6))):
                        P.append((("wd", pi, l, ch, kg),
                                  self.w_ffn_down[l][k0 * 128:(k0 + kn) * 128, ch * 512:(ch + 1) * 512], kn, 512))

    def w_issue(self, i):
        key, w, kch, cw = self.plan[i]
        slot = i % 4
        ring = self.ring[slot]
        if kch is None:
            dst = ring[:, 0:512].rearrange("p (g d) -> p g d", g=4)
            src = w.rearrange("g c d -> c g d")
        else:
            dst = ring[:, 0:kch * cw].rearrange("p (c n) -> p c n", c=kch)
            src = w.rearrange("(c p) n -> p c n", p=128)
        self.add("pool", lambda e: e.dma_start(out=dst, in_=src), writes=[self.ringB[slot]], dma=True)

    def wget(self, key, cont=False):
        i = self.w_consumed
        pkey, w, kch, cw = self.plan[i]
        assert pkey == key, (pkey, key)
        lim = min(len(self.plan), i + 4)
        while (not cont) and self.w_issued < lim:
            self.w_issue(self.w_issued)
            self.w_issued += 1
        assert self.w_issued > i
        self.w_consumed += 1
        slot = i % 4
        ring = self.ring[slot]
        if kch is None:
            view = ring[:, 0:512].rearrange("p (g d) -> p g d", g=4)
        else:
            view = ring[:, 0:kch * cw].rearrange("p (c n) -> p c n", c=kch)
        return view, self.ringB[slot], cw

    def next_stg(self):
        i = self.stg_rr % 2
        self.stg_rr += 1
        return self.stg[i], self.stgB[i]

    def mm(self, out, lhsT, rhs, start, stop, reads, wbuf):
        self.add("pe", lambda e: e.matmul(out, lhsT=lhsT, rhs=rhs, start=start, stop=stop),
                 reads=reads, writes=[wbuf])

    def tr(self, out, in_, ident, reads, wbuf):
        self.add("pe", lambda e: e.transpose(out, in_, ident), reads=reads, writes=[wbuf])

    def load_rows_T(self, rows_ap, R, C, evac):
        nch = C // 128
        stg, stgB = self.next_stg()
        self.add("sp", lambda e: e.dma_start(out=stg[0:R, 0:C], in_=rows_ap), writes=[stgB], dma=True)
        done = 0
        per = max(1, 512 // R)
        while done < nch:
            k = min(per, nch - done)
            bk, bkB = self.bank()
            for j in range(k):
                self.tr(bk[:, j * R:(j + 1) * R], stg[0:R, (done + j) * 128:(done + j + 1) * 128],
                        self.ident[0:R, 0:R], [stgB, self.constB], bkB)
            evac(done, k, bk[:, 0:k * R].rearrange("p (c r) -> p c r", c=k), bkB)
            done += k

    def store_rows_T(self, srcs, R, dram_rows, evac_eng="act"):
        nch = len(srcs)
        assert nch <= 8
        stg, stgB = self.next_stg()
        done = 0
        while done < nch:
            k = min(4, nch - done)
            bk, bkB = self.bank()
            for j in range(k):
                ap, bufs = srcs[done + j]
                self.tr(bk[0:R, j * 128:(j + 1) * 128], ap, self.ident[:, :], list(bufs) + [self.constB], bkB)
            o = stg[0:R, done * 128:(done + k) * 128]
            i_ = bk[0:R, 0:k * 128]
            if evac_eng == "act":
                self.add("act", lambda e, o=o, i_=i_: e.activation(out=o, in_=i_, func=ACTF.Copy), reads=[bkB], writes=[stgB])
            else:
                self.add("dve", lambda e, o=o, i_=i_: e.tensor_copy(out=o, in_=i_), reads=[bkB], writes=[stgB])
            done += k
        ob = Buf()
        self.add("sp", lambda e: e.dma_start(out=dram_rows, in_=stg[0:R, 0:nch * 128]), reads=[stgB], writes=[ob], dma=True)
        self.out_bufs.append(ob)

    def gain(self, l, i):
        return lambda c: self.c1024[:, c, l * 7 + i:l * 7 + i + 1]

    def setup_consts(self):
        self.S.cur_tag = "consts"
        A = self.add
        ident, identb = self.ident, self.identb

        cB = self.constB
        A("pool", lambda e: e.memset(ident[:], 0.0), writes=[cB])
        A("pool", lambda e: e.affine_select(out=ident[:], in_=ident[:], compare_op=ALU.not_equal, fill=1.0, base=0,
                                            pattern=[[-1, 128]], channel_multiplier=1), reads=[cB], writes=[cB])
        A("pool", lambda e: e.tensor_copy(out=identb[:], in_=ident[:]), reads=[cB], writes=[cB])
        for tl, val in ((self.ones1024, 1.0 / 1024), (self.ones512, 1.0 / 512), (self.ones1, 1.0), (self.zeros, 0.0), (self.onesf, 1.0)):
            A("pool", lambda e, tl=tl, val=val: e.memset(tl[:], val), writes=[cB])
        for g, w in enumerate(POOLW):
            A("pool", lambda e, g=g: e.memset(self.poolfix[:, g, :], 1.0), writes=[cB])
            for t in range(w - 1):
                A("pool", lambda e, g=g, t=t, w=w: e.memset(self.poolfix[:, g, t:t + 1], float(w) / float(t + 1)), writes=[cB])
        groups = [
            (self.c1024, 8, [(self.norm_gains, 28), (self.gconv_w, 6)]),
            (self.c512, 4, [(self.pool_scale, 2), (self.conf_w, 62), (self.conf_b, 2), (self.conf_ln_g, 2),
                            (self.conf_ln_b, 2)]),
        ]
        for dst, nch, items in groups:
            r0 = 0
            for src, R in items:
                def ev(c0, k, view, bkB, dst=dst, r0=r0, R=R):
                    A("dve", lambda e: e.tensor_copy(out=dst[:, c0:c0 + k, r0:r0 + R], in_=view), reads=[bkB], writes=[self.cB])
                self.load_rows_T(src, R, nch * 128, ev)
                r0 += R
        r0 = 0
        for src, R in [(self.ffn_conv_w, 12), (self.ffn_conv_b, 4)]:
            for c0 in range(0, DFF, 1024):
                cw = min(1024, DFF - c0)

                def ev(cc, k, view, bkB, r0=r0, R=R, base=c0 // 128):
                    A("dve", lambda e: e.tensor_copy(out=self.c2816[:, base + cc:base + cc + k, r0:r0 + R], in_=view),
                      reads=[bkB], writes=[self.cB])
                self.load_rows_T(src[:, c0:c0 + cw], R, cw, ev)
            r0 += R

    def stats_rstd(self, src_fn, w, ones, eps, nchunks=8):
        A = self.add
        st, stB = self.stbank()
        for c in range(nchunks):
            ap, bufs = src_fn(c)
            i = self.sq_rr % 4
            self.sq_rr += 1
            sq, sqB = self.sq[i], self.sqB[i]
            A("act", lambda e, ap=ap, sq=sq: e.activation(out=sq[:, 0:w], in_=ap, func=ACTF.Square), reads=bufs, writes=[sqB])
            self.mm(st[:, 0:w], ones[:, :], sq[:, 0:w], c == 0, c == nchunks - 1, [sqB, self.constB], stB)
        A("act", lambda e: e.activation(out=self.vsb[0:1, 0:w], in_=st[0:1, 0:w], func=ACTF.Sqrt, bias=eps, scale=1.0),
          reads=[stB], writes=[self.vsbB])
        return self.row_pow_bcast(w)

    def row_pow_bcast(self, w, extra=None):
        A = self.add
        i = self.rstd_rr % 2
        self.rstd_rr += 1
        r, rB = self.rstd[i], self.rstdB[i]
        A("dve", lambda e: e.reciprocal(out=r[0:1, 0:w], in_=self.vsb[0:1, 0:w]), reads=[self.vsbB], writes=[rB])
        bc, bcB = self.bank()
        self.bcast_row(bc, bcB, r, rB, w)
        return bc, bcB, r, rB

    def bcast_row(self, bc, bcB, r, rB, w):
        for c0 in range(0, w, 128):
            self.mm(bc[:, c0:c0 + 128], self.onesf[0:1, :], r[0:1, c0:c0 + 128], True, True, [rB, self.constB], bcB)

    def _pre_norm(self, t, gi):
        A = self.add
        x, h = self.x, self.h
        g = self.gain(self.l, gi)
        r, rB, _, _ = self.stats_rstd(lambda c: (x[:, c, t.cols], [self.xB[c][t.n]]), t.w, self.ones1024, RMS_EPS)
        for c in range(8):
            A("dve", lambda e, c=c: e.scalar_tensor_tensor(out=h[:, c, t.cols], in0=x[:, c, t.cols], scalar=g(c),
                                                           in1=r[:, 0:t.w], op0=ALU.mult, op1=ALU.mult),
              reads=[self.xB[c][t.n], rB, self.cB], writes=[self.hB[c][t.n]])

    def _post_norm(self, t, gi):
        A = self.add
        x, y = self.x, self.h
        g = self.gain(self.l, gi)
        r, rB, _, _ = self.stats_rstd(lambda c: (y[:, c, t.cols], [self.hB[c][t.n]]), t.w, self.ones1024, RMS_EPS)
        for c in range(8):
            i = self.tmp_rr % 2
            self.tmp_rr += 1
            tm, tmB = self.tmp[i], self.tmpB[i]
            A("dve", lambda e, c=c, tm=tm: e.scalar_tensor_tensor(out=tm[:, 0:t.w], in0=y[:, c, t.cols], scalar=g(c),
                                                                  in1=r[:, 0:t.w], op0=ALU.mult, op1=ALU.mult),
              reads=[self.hB[c][t.n], rB, self.cB], writes=[tmB])
            A("dve", lambda e, c=c, tm=tm: e.tensor_tensor(out=x[:, c, t.cols], in0=x[:, c, t.cols], in1=tm[:, 0:t.w], op=ALU.add),
              reads=[tmB, self.xB[c][t.n]], writes=[self.xB[c][t.n]])

    def _tagged(self, suffix, fn, *a):
        old = self.S.cur_tag
        self.S.cur_tag = old.split(".")[0] + suffix
        try:
            return fn(*a)
        finally:
            self.S.cur_tag = old

    def pre_norm(self, t, gi):
        return self._tagged(".pre", self._pre_norm, t, gi)

    def post_norm(self, t, gi):
        return self._tagged(".post", self._post_norm, t, gi)

    def out_proj(self, *a):
        return self._tagged(".out", self._out_proj, *a)

    def _out_proj(self, keybase, src, srcB, gi_post, gi_pre_next):
        A = self.add
        pending = []

        def norms(t):
            self.post_norm(t, gi_post)
            if gi_pre_next is not None:
                self.l_next_pre(t, gi_pre_next)
        for u in range(2):
            wv, wB, _ = self.wget(keybase + (u,))
            for t in self.tiles:
                for ml in range(4):
                    m = u * 4 + ml
                    bk, bkB = self.bank()
                    for c in range(8):
                        self.mm(bk[:, 0:t.w], wv[:, c, ml * 128:(ml + 1) * 128], src[:, c, t.cols], c == 0, c == 7,
                                [wB, srcB[c][t.n]], bkB)
                    A("act", lambda e, m=m, t=t, bk=bk: e.activation(out=self.h[:, m, t.cols], in_=bk[:, 0:t.w], func=ACTF.Copy),
                      reads=[bkB], writes=[self.hB[m][t.n]])
                if u == 1:
                    pending.append(t)
                    if len(pending) > 1:
                        norms(pending.pop(0))
        while pending:
            norms(pending.pop(0))

    def l_next_pre(self, t, spec):
        l_save = self.l
        self.l = spec[0]
        self.pre_norm(t, spec[1])
        self.l = l_save

    def load_x(self):
        self.S.cur_tag = "load"
        A = self.add
        for t in self.tiles:
            for blk in range(t.w // 128):
                if t.kind == "P":
                    rows = self.xp[t.tok0 + blk * 128: t.tok0 + (blk + 1) * 128, :]
                else:
                    rows = self.xs[:, :]
                cs = slice(t.c0 + blk * 128, t.c0 + (blk + 1) * 128)

                def ev(c0, k, view, bkB, cs=cs, t=t):
                    A("act", lambda e: e.activation(out=self.x[:, c0:c0 + k, cs], in_=view, func=ACTF.Copy), reads=[bkB],
                      writes=[self.xB[c][t.n] for c in range(c0, c0 + k)])
                self.load_rows_T(rows, 128, D, ev)
        self.l = 0
        for t in self.tiles:
            self.pre_norm(t, 0)

    def store_x(self):
        self.S.cur_tag = "store"
        for t in self.tiles:
            for blk in range(t.w // 128):
                cs = slice(t.c0 + blk * 128, t.c0 + (blk + 1) * 128)
                srcs = [(self.x[:, c, cs], [self.xB[c][t.n]]) for c in range(8)]
                if t.kind == "P":
                    rows = self.yp[t.tok0 + blk * 128: t.tok0 + (blk + 1) * 128, :]
                else:
                    rows = self.ys[:, :]
                self.store_rows_T(srcs, 128, rows, evac_eng="act" if blk % 2 == 0 else "dve")

    def phase0_memkv(self):
        self.S.cur_tag = "p0"
        A = self.add
        big = self.big
        self.region_reset("big", [])
        mhat = big[:, 0:2048].rearrange("p (c k) -> p c k", c=8)
        mT = big[:, 2048:4096].rearrange("p (c k) -> p c k", c=8)
        vbf = big[:, 4096:6144].rearrange("p (k d) -> p k d", k=2)
        kTb = big[:, 6144:8192].rearrange("p (c k) -> p c k", c=8)
        mhatB, mTB, vbfB, kTbB = self.rbufs("big", 4)
        self.region_reset("scr", [])
        mraw = self.scr[:, 0:2048].rearrange("p (c k) -> p c k", c=8)
        mrawB = self.rbufs("scr", 1)[0]
        for kc in range(2):
            def ev(c0, k, view, bkB, kc=kc):
                A("dve", lambda e: e.tensor_copy(out=mraw[:, c0:c0 + k, kc * 128:(kc + 1) * 128], in_=view), reads=[bkB], writes=[mrawB])
            self.load_rows_T(self.memp[kc * 128:(kc + 1) * 128, :], 128, D, ev)
        rbk, rbkB, _, _ = self.stats_rstd(lambda c: (mraw[:, c, :], [mrawB]), NMEM, self.ones1024, RMS_EPS)
        r, rB = self.tmp[0], self.tmpB[0]
        A("act", lambda e: e.activation(out=r[:, 0:NMEM], in_=rbk[:, 0:NMEM], func=ACTF.Copy), reads=[rbkB], writes=[rB])
        for l in range(DEPTH):
            for c in range(8):
                A("dve", lambda e, c=c, l=l: e.scalar_tensor_tensor(out=mT[:, c, :], in0=mraw[:, c, :], scalar=self.c1024[:, c, l * 7 + 6:l * 7 + 7],
                                                                   in1=r[:, 0:NMEM], op0=ALU.mult, op1=ALU.mult),
                  reads=[mrawB, rB, self.cB], writes=[mTB])
            for which, dram_o in (("wk", self.o_mk_p), ("wv", self.o_mv_p)):
                for u in range(2):
                    wv, wB, _ = self.wget((which, l, u))
                    for kc in range(2):
                        bk, bkB = self.bank()
                        for c in range(8):
                            self.mm(bk[:, :], mT[:, c, kc * 128:(kc + 1) * 128], wv[:, c, :], c == 0, c == 7, [mTB, wB], bkB)
                        stg, stgB = self.next_stg()
                        A("act", lambda e, stg=stg, bk=bk: e.activation(out=stg[:, 0:512], in_=bk[:, :], func=ACTF.Copy),
                          reads=[bkB], writes=[stgB])
                        ob = Buf()
                        if "out" in self.cfg.get("p0", ("out", "kT", "vbf")):
                            A("sp", lambda e, stg=stg, kc=kc, u=u, dram_o=dram_o, l=l: e.dma_start(
                                out=dram_o[l, kc * 128:(kc + 1) * 128, u * 512:(u + 1) * 512], in_=stg[:, 0:512]),
                              reads=[stgB], writes=[ob], dma=True)
                            self.out_bufs.append(ob)
                        if which == "wv" and "vbf" in self.cfg.get("p0", ("out", "kT", "vbf")):
                            A("dve", lambda e, stg=stg, kc=kc, u=u: e.tensor_copy(out=vbf[:, kc, u * 512:(u + 1) * 512], in_=stg[:, 0:512]),
                              reads=[stgB], writes=[vbfB])
                    if which == "wk" and "kT" in self.cfg.get("p0", ("out", "kT", "vbf")):
                        for ml in range(4):
                            m = u * 4 + ml
                            bk, bkB = self.bank()
                            for c in range(8):
                                self.mm(bk[:, 0:256], wv[:, c, ml * 128:(ml + 1) * 128], mT[:, c, :], c == 0, c == 7, [mTB, wB], bkB)
                            A("dve", lambda e, bk=bk, m=m: e.tensor_copy(out=kTb[:, m, :], in_=bk[:, 0:256]), reads=[bkB], writes=[kTbB])
            s1, s2 = Buf(), Buf()
            if self.cfg.get("scr", True):
                A("sp", lambda e, l=l: e.dma_start(out=self.kT_scr[l], in_=big[:, 6144:8192]), reads=[kTbB], writes=[s1], dma=True)
                A("sp", lambda e, l=l: e.dma_start(out=self.v_scr[l], in_=big[:, 4096:6144]), reads=[vbfB], writes=[s2], dma=True)
            if l == 0:
                self.kscrB, self.vscrB = [], []
            self.kscrB.append(s1)
            self.vscrB.append(s2)

    def scrB0(self):
        if not hasattr(self, "_scrB0"):
            self._scrB0 = self.rbufs("scr", 1)[0]
        return self._scrB0

    def odd_mixer(self, o):
        self.S.cur_tag = "odd"
        A = self.add
        l, pi = self.l, self.pi
        big = self.big
        self.region_reset("big", [])
        self.region_reset("scr", [])
        EXT = 1186
        uext = big[:, 0:8 * EXT].rearrange("p (c t) -> p c t", c=8)
        ycv = big[:, 8 * EXT:8 * EXT + 8 * TCOLS].rearrange("p (c t) -> p c t", c=8)
        uB = [[b for b in self.rbufs("big", 3)] for _ in range(8)]
        uctxB = self.rbufs("big", 1)[0]
        ycvB = [[b for b in self.rbufs("big", 3)] for _ in range(8)]
        accs = [self.scr[:, 0:512], self.scr[:, 512:1024]]
        accB = self.rbufs("scr", 2)
        gw = lambda j, k: self.c1024[:, j, 28 + o * 3 + k:28 + o * 3 + k + 1]

        def ucols(j, t, shift=0):
            if t.kind == "P":
                return uext[:, j, t.c0 + shift:t.c0 + shift + t.w]
            return uext[:, j, 1026:1186].rearrange("p (b s) -> p b s", b=NB_S)[:, :, shift:shift + TS]

        if pi == 0:
            A("dve", lambda e: e.memset(uext[:, :, 0:2], 0.0), writes=[uctxB])
            def ev(c0, k, view, bkB):
                for j in range(c0, c0 + k):
                    dst = uext[:, j, 1026:1186].rearrange("p (b s) -> p b s", b=NB_S)[:, :, 0:2]
                    src = view[:, j - c0, :].rearrange("p (b r) -> p b r", b=NB_S)
                    A("dve", lambda e, dst=dst, src=src: e.tensor_copy(out=dst, in_=src), reads=[bkB], writes=[uctxB])
            self.load_rows_T(self.st_gconv[o].rearrange("b r d -> (b r) d"), 32, D, ev)
        else:
            A("dve", lambda e: e.tensor_copy(out=uext[:, :, 0:2], in_=self.car_g[o][:, :, :]), reads=[self.car_gB[o]], writes=[uctxB])

        def view3(ap, t):
            return ap if t.kind == "P" else ap.rearrange("p (b s) -> p b s", b=NB_S)

        for part, name in ((0, "xin"), (2, "gc"), (1, "gb")):
            for u in range(2):
                wv, wB, _ = self.wget(("win", pi, l, part, u))
                for ml in range(4):
                    j = u * 4 + ml
                    for t in self.tiles:
                        bk, bkB = self.bank()
                        for c in range(8):
                            self.mm(bk[:, 0:t.w], wv[:, c, ml * 128:(ml + 1) * 128], self.h[:, c, t.cols], c == 0, c == 7,
                                    [wB, self.hB[c][t.n]], bkB)
                        bv = view3(bk[:, 0:t.w], t)
                        if part == 0:
                            A("act", lambda e, j=j, t=t, bv=bv: e.activation(out=ucols(j, t, 2), in_=bv, func=ACTF.Copy),
                              reads=[bkB], writes=[uB[j][t.n]])
                        elif part == 2:
                            A("dve", lambda e, j=j, t=t, bv=bv: e.tensor_tensor(out=ucols(j, t, 2), in0=bv, in1=ucols(j, t, 2), op=ALU.mult),
                              reads=[bkB, uB[j][t.n]], writes=[uB[j][t.n]])
                        else:
                            i = (j * 3 + t.n) % 2
                            acc, aB = accs[i], accB[i]
                            av = view3(acc[:, 0:t.w], t)
                            rd = [uB[j][t.n], uctxB, self.cB] + ([uB[j][t.n - 1]] if (t.kind == "P" and t.n > 0) else [])
                            A("dve", lambda e, j=j, t=t, av=av: e.tensor_scalar(out=av, in0=ucols(j, t, 0), scalar1=gw(j, 0), scalar2=None,
                                                                               op0=ALU.mult), reads=rd, writes=[aB])
                            for k in (1, 2):
                                A("dve", lambda e, j=j, t=t, av=av, k=k: e.scalar_tensor_tensor(
                                    out=av, in0=ucols(j, t, k), scalar=gw(j, k), in1=av, op0=ALU.mult, op1=ALU.add),
                                  reads=rd + [aB], writes=[aB])
                            A("dve", lambda e, j=j, t=t, acc=acc, bk=bk: e.tensor_tensor(out=ycv[:, j, t.cols], in0=bk[:, 0:t.w],
                                                                                        in1=acc[:, 0:t.w], op=ALU.mult),
                              reads=[bkB, aB], writes=[ycvB[j][t.n]])
        last = self.tiles[-1]
        if pi == 0:
            A("act", lambda e: e.activation(out=self.car_g[o][:, :, :], in_=uext[:, :, 1024:1026], func=ACTF.Copy),
              reads=[uB[j][1] for j in range(8)], writes=[self.car_gB[o]])
            for j in range(8):
                src = uext[:, j, 1026:1186].rearrange("p (b s) -> p b s", b=NB_S)[:, :, 8:10].rearrange("p b r -> p r b")
                dst = self.cstate[:, j, 0:32].rearrange("p (r b) -> p r b", r=2)
                A("act", lambda e, src=src, dst=dst: e.activation(out=dst, in_=src, func=ACTF.Copy), reads=[uB[j][2]], writes=[self.cstateB])
            srcs = [(self.cstate[:, j, 0:32], [self.cstateB]) for j in range(8)]
            self.store_rows_T_multi(srcs, 32, [(r * 16, 16, self.o_gconv_s[o][:, r, :]) for r in range(2)])
        else:
            for j in range(8):
                A("act", lambda e, j=j: e.activation(out=self.cstate[:, j, 0:2], in_=uext[:, j, 1024:1026], func=ACTF.Copy),
                  reads=[uB[j][1]], writes=[self.cstateB])
            srcs = [(self.cstate[:, j, 0:2], [self.cstateB]) for j in range(8)]
            self.store_rows_T_multi(srcs, 2, [(0, 2, self.o_gconv_p[o])])
        self.out_proj(("wout", pi, l), ycv, ycvB, 1, (l, 2))

    def store_rows_T_multi(self, srcs, R, dsts):
        nch = len(srcs)
        for g0 in range(0, nch, 8):
            g = srcs[g0:g0 + 8]
            stg, stgB = self.next_stg()
            done = 0
            while done < len(g):
                k = min(4, len(g) - done)
                bk, bkB = self.bank()
                for j in range(k):
                    ap, bufs = g[done + j]
                    self.tr(bk[0:R, j * 128:(j + 1) * 128], ap, self.ident[:, :], list(bufs) + [self.constB], bkB)
                o = stg[0:R, done * 128:(done + k) * 128]
                i_ = bk[0:R, 0:k * 128]
                self.add("act", lambda e, o=o, i_=i_: e.activation(out=o, in_=i_, func=ACTF.Copy), reads=[bkB], writes=[stgB])
                done += k
            for (r0, nr, dap) in dsts:
                ob = Buf()
                self.add("sp", lambda e, r0=r0, nr=nr, dap=dap, stg=stg, g0=g0, ng=len(g): e.dma_start(
                    out=dap[:, g0 * 128:(g0 + ng) * 128], in_=stg[r0:r0 + nr, 0:ng * 128]), reads=[stgB], writes=[ob], dma=True)
                self.out_bufs.append(ob)

    def even_mixer(self, e_):
        self.S.cur_tag = "even"
        A = self.add
        l, pi = self.l, self.pi
        big = self.big
        self.region_reset("big", [])
        self.region_reset("scr", [])
        AE, UE = 1407, 1662
        aext = big[:, 0:4 * AE].rearrange("p (c t) -> p c t", c=4)
        uext = big[:, 4 * AE:4 * AE + 4 * UE].rearrange("p (c t) -> p c t", c=4)
        o0 = 4 * AE + 4 * UE
        ycat = big[:, o0:o0 + 8 * TCOLS].rearrange("p (c t) -> p c t", c=8)
        o1 = o0 + 8 * TCOLS
        cbb = big[:, o1:o1 + 2048].rearrange("p (c t) -> p c t", c=4)
        aB = [[b for b in self.rbufs("big", 3)] for _ in range(4)]
        actxB = self.rbufs("big", 1)[0]
        uB = [[b for b in self.rbufs("big", 3)] for _ in range(4)]
        uctxB = self.rbufs("big", 1)[0]
        ycatB = [[b for b in self.rbufs("big", 3)] for _ in range(8)]
        cbbB = self.rbufs("big", 4)
        scr = self.scr
        ping = [scr[:, 0:768], scr[:, 768:1536]]
        pingB = self.rbufs("scr", 2)
        lnt = [scr[:, 1536 + i * 512:1536 + (i + 1) * 512] for i in range(3)]
        lntB = self.rbufs("scr", 3)
        cst = lambda row: (lambda j: self.c512[:, j, row:row + 1])
        pscale = cst(e_)
        cw = lambda j, k: self.c512[:, j, 2 + e_ * 31 + k:2 + e_ * 31 + k + 1]
        cbias, lng, lnb = cst(64 + e_), cst(66 + e_), cst(68 + e_)

        def acols(g, t, lo, n):
            if t.kind == "P":
                return aext[:, g, 15 + t.c0 + lo:15 + t.c0 + lo + n]
            return aext[:, g, 1039:1407].rearrange("p (b s) -> p b s", b=NB_S)[:, :, 15 + lo:15 + lo + n]

        def ucols(j, t, lo, n):
            if t.kind == "P":
                return uext[:, j, 30 + t.c0 + lo:30 + t.c0 + lo + n]
            return uext[:, j, 1054:1662].rearrange("p (b s) -> p b s", b=NB_S)[:, :, 30 + lo:30 + lo + n]

        def view3(ap, t):
            return ap if t.kind == "P" else ap.rearrange("p (b s) -> p b s", b=NB_S)

        if pi == 0:
            A("dve", lambda e: e.memset(aext[:, :, 0:15], 0.0), writes=[actxB])
            A("dve", lambda e: e.memset(uext[:, :, 0:30], 0.0), writes=[uctxB])
            for (st, nr, ext, base, tot, ctxB) in ((self.st_pool[e_], 15, aext, 1039, 23, actxB), (self.st_conf[e_], 30, uext, 1054, 38, uctxB)):
                bper = 128 // nr
                for b0 in range(0, NB_S, bper):
                    nb = min(bper, NB_S - b0)
                    R = nb * nr

                    def ev(c0, k, view, bkB, ext=ext, base=base, tot=tot, nr=nr, b0=b0, nb=nb, ctxB=ctxB):
                        for j in range(c0, c0 + k):
                            dst = ext[:, j, base + b0 * tot:base + (b0 + nb) * tot].rearrange("p (b s) -> p b s", b=nb)[:, :, 0:nr]
                            src = view[:, j - c0, :].rearrange("p (b r) -> p b r", b=nb)
                            A("dve", lambda e, dst=dst, src=src: e.tensor_copy(out=dst, in_=src), reads=[bkB], writes=[ctxB])
                    self.load_rows_T(st[b0:b0 + nb].rearrange("b r d -> (b r) d"), R, 512, ev)
            for (st, ost, keep, nr) in ((self.st_pool[e_], self.o_pool_s[e_], 7, 15), (self.st_conf[e_], self.o_conf_s[e_], 22, 30)):
                ob = Buf()
                A("sp", lambda e, st=st, ost=ost, keep=keep, nr=nr: e.dma_start(
                    out=ost[:, 0:keep, :].rearrange("b r d -> b (r d)"), in_=st[:, nr - keep:nr, :].rearrange("b r d -> b (r d)")),
                  writes=[ob], dma=True)
                self.out_bufs.append(ob)
        else:
            A("dve", lambda e: e.tensor_copy(out=aext[:, :, 0:15], in_=self.car_a[e_][:, :, :]), reads=[self.car_aB[e_]], writes=[actxB])
            A("dve", lambda e: e.tensor_copy(out=uext[:, :, 0:30], in_=self.car_u[e_][:, :, :]), reads=[self.car_uB[e_]], writes=[uctxB])

        wa, waB, _ = self.wget(("win", pi, l, 0))
        for g in range(4):
            for t in self.tiles:
                bk, bkB = self.bank()
                for c in range(8):
                    self.mm(bk[:, 0:t.w], wa[:, c, g * 128:(g + 1) * 128], self.h[:, c, t.cols], c == 0, c == 7, [waB, self.hB[c][t.n]], bkB)
                A("act", lambda e, g=g, t=t, bk=bk: e.activation(out=acols(g, t, 0, TS if t.kind == "S" else t.w), in_=view3(bk[:, 0:t.w], t),
                                                                func=ACTF.Copy), reads=[bkB], writes=[aB[g][t.n]])
        wp, wpB, _ = self.wget(("wpool", pi, l))
        for g in range(4):
            wlen = POOLW[g]
            for t in self.tiles:
                n = TS if t.kind == "S" else t.w
                nseq = NB_S if t.kind == "S" else 1
                rd = [aB[g][t.n], actxB] + ([aB[g][t.n - 1]] if (t.kind == "P" and t.n > 0) else [])
                prev = lambda lo, cnt, g=g, t=t: acols(g, t, lo, cnt)
                prevB = rd
                s = 1
                k = 0
                while s < wlen:
                    lo = -(wlen - 2 * s)
                    cnt = n - lo
                    buf, bufB = ping[k % 2], pingB[k % 2]

                    def lvl(lo2, cnt2, buf=buf, lo=lo, cnt=cnt, nseq=nseq):
                        if nseq == 1:
                            return buf[:, lo2 - lo:lo2 - lo + cnt2]
                        return buf[:, 0:nseq * cnt].rearrange("p (b s) -> p b s", b=nseq)[:, :, lo2 - lo:lo2 - lo + cnt2]
                    A("dve", lambda e, lvl=lvl, prev=prev, lo=lo, cnt=cnt, s=s: e.tensor_tensor(
                        out=lvl(lo, cnt), in0=prev(lo, cnt), in1=prev(lo - s, cnt), op=ALU.add), reads=prevB, writes=[bufB])
                    prev, prevB = lvl, [bufB]
                    s *= 2
                    k += 1
                if t.kind == "P" and t.tok0 == 0:
                    A("dve", lambda e, prev=prev, g=g: e.tensor_tensor(out=prev(0, 15), in0=prev(0, 15), in1=self.poolfix[:, g, :], op=ALU.mult),
                      reads=prevB + [self.constB], writes=prevB)
                i = self.sl_rr % 2
                self.sl_rr += 1
                pl, plB = self.sl[i], self.slB[i]
                A("dve", lambda e, prev=prev, pl=pl, t=t, g=g, n=n, wlen=wlen: e.scalar_tensor_tensor(
                    out=view3(pl[:, 0:t.w], t), in0=prev(0, n), scalar=1.0 / wlen, in1=acols(g, t, 0, n), op0=ALU.mult, op1=ALU.subtract),
                  reads=prevB + [aB[g][t.n]], writes=[plB])
                bk, bkB = self.bank()
                self.mm(bk[:, 0:t.w], wp[:, g, :], pl[:, 0:t.w], True, True, [wpB, plB], bkB)
                A("act", lambda e, g=g, t=t, bk=bk: e.activation(out=ycat[:, g, t.cols], in_=bk[:, 0:t.w], func=ACTF.Copy, scale=pscale(g)),
                  reads=[bkB, self.cB], writes=[ycatB[g][t.n]])
        if pi == 0:
            A("act", lambda e: e.activation(out=self.car_a[e_][:, :, :], in_=aext[:, :, 1024:1039], func=ACTF.Copy),
              reads=[aB[g][1] for g in range(4)], writes=[self.car_aB[e_]])
            for g in range(4):
                src = aext[:, g, 1039:1407].rearrange("p (b s) -> p b s", b=NB_S)[:, :, 15:23].rearrange("p b r -> p r b")
                dst = self.cstate[:, g, :].rearrange("p (r b) -> p r b", r=TS)
                A("act", lambda e, src=src, dst=dst: e.activation(out=dst, in_=src, func=ACTF.Copy), reads=[aB[g][2]], writes=[self.cstateB])
            self.store_rows_T_multi([(self.cstate[:, g, :], [self.cstateB]) for g in range(4)], 128,
                                    [(r * 16, 16, self.o_pool_s[e_][:, 7 + r, :]) for r in range(TS)])
        else:
            for g in range(4):
                A("act", lambda e, g=g: e.activation(out=self.cstate[:, g, 0:15], in_=aext[:, g, 1024:1039], func=ACTF.Copy),
                  reads=[aB[g][1]], writes=[self.cstateB])
            self.store_rows_T_multi([(self.cstate[:, g, 0:15], [self.cstateB]) for g in range(4)], 15, [(0, 15, self.o_pool_p[e_])])

        w1, w1B, _ = self.wget(("win", pi, l, 1))
        w2, w2B, _ = self.wget(("win", pi, l, 2), cont=True)
        for j in range(4):
            for t in self.tiles:
                n = TS if t.kind == "S" else t.w
                bk2, bk2B = self.bank()
                for c in range(8):
                    self.mm(bk2[:, 0:t.w], w2[:, c, j * 128:(j + 1) * 128], self.h[:, c, t.cols], c == 0, c == 7, [w2B, self.hB[c][t.n]], bk2B)
                i = self.sl_rr % 2
                self.sl_rr += 1
                sg, sgB = self.sl[i], self.slB[i]
                A("act", lambda e, sg=sg, bk2=bk2, t=t: e.activation(out=sg[:, 0:t.w], in_=bk2[:, 0:t.w], func=ACTF.Sigmoid),
                  reads=[bk2B], writes=[sgB])
                bk1, bk1B = self.bank()
                for c in range(8):
                    self.mm(bk1[:, 0:t.w], w1[:, c, j * 128:(j + 1) * 128], self.h[:, c, t.cols], c == 0, c == 7, [w1B, self.hB[c][t.n]], bk1B)
                A("dve", lambda e, j=j, t=t, n=n, bk1=bk1, sg=sg: e.tensor_tensor(out=ucols(j, t, 0, n), in0=view3(bk1[:, 0:t.w], t),
                                                                                 in1=view3(sg[:, 0:t.w], t), op=ALU.mult),
                  reads=[bk1B, sgB], writes=[uB[j][t.n]])
        if pi == 0:
            A("act", lambda e: e.activation(out=self.car_u[e_][:, :, :], in_=uext[:, :, 1024:1054], func=ACTF.Copy),
              reads=[uB[j][1] for j in range(4)], writes=[self.car_uB[e_]])
            for j in range(4):
                src = uext[:, j, 1054:1662].rearrange("p (b s) -> p b s", b=NB_S)[:, :, 30:38].rearrange("p b r -> p r b")
                dst = self.cstate[:, 4 + j, :].rearrange("p (r b) -> p r b", r=TS)
                A("act", lambda e, src=src, dst=dst: e.activation(out=dst, in_=src, func=ACTF.Copy), reads=[uB[j][2]], writes=[self.cstateB])
            self.store_rows_T_multi([(self.cstate[:, 4 + j, :], [self.cstateB]) for j in range(4)], 128,
                                    [(r * 16, 16, self.o_conf_s[e_][:, 22 + r, :]) for r in range(TS)])
        else:
            for j in range(4):
                A("act", lambda e, j=j: e.activation(out=self.cstate[:, 4 + j, 0:30], in_=uext[:, j, 1024:1054], func=ACTF.Copy),
                  reads=[uB[j][1]], writes=[self.cstateB])
            self.store_rows_T_multi([(self.cstate[:, 4 + j, 0:30], [self.cstateB]) for j in range(4)], 30, [(0, 30, self.o_conf_p[e_])])

        self.region_reset("scr", [])
        scrb = scr[:, :].bitcast(BF16)
        cb = scrb[:, 0:4 * TCOLS].rearrange("p (c t) -> p c t", c=4)
        cbB = [[b for b in self.rbufs("scr", 3)] for _ in range(4)]
        dg0 = 4 * TCOLS
        assert dg0 + 31 * 128 <= 2 * self.SCRN
        diag = scrb[:, dg0:dg0 + 31 * 128].rearrange("p (k d) -> p k d", k=31)
        diagB = self.rbufs("scr", 1)[0]
        for j in range(4):
            for k in range(31):
                A("dve", lambda e, j=j, k=k: e.tensor_scalar(out=diag[:, k, :], in0=self.identb[:, :], scalar1=cw(j, k), scalar2=None,
                                                            op0=ALU.mult), reads=[self.constB, self.cB], writes=[diagB])
            for t in self.tiles:
                n = TS if t.kind == "S" else t.w
                bk, bkB = self.bank()
                rd = [diagB, uB[j][t.n], uctxB] + ([uB[j][t.n - 1]] if (t.kind == "P" and t.n > 0) else [])
                for k in range(31):
                    self.mm(view3(bk[:, 0:t.w], t), diag[:, k, :], ucols(j, t, k - 30, n), k == 0, k == 30, rd, bkB)
                A("act", lambda e, j=j, t=t, bk=bk: e.activation(out=cb[:, j, t.cols], in_=bk[:, 0:t.w], func=ACTF.Identity, bias=cbias(j), scale=1.0),
                  reads=[bkB, self.cB], writes=[cbB[j][t.n]])
        for t in self.tiles:
            w_ = t.w
            stA, stAB = self.stbank()
            stB_, stBB = self.stbank()
            for j in range(4):
                self.mm(stA[:, 0:w_], self.ones512[:, :], cb[:, j, t.cols], j == 0, j == 3, [cbB[j][t.n], self.constB], stAB)
                i = self.sq_rr % 4
                self.sq_rr += 1
                sq, sqB = self.sq[i], self.sqB[i]
                A("act", lambda e, j=j, t=t, sq=sq: e.activation(out=sq[:, 0:t.w], in_=cb[:, j, t.cols], func=ACTF.Square), reads=[cbB[j][t.n]], writes=[sqB])
                self.mm(stB_[:, 0:w_], self.ones512[:, :], sq[:, 0:w_], j == 0, j == 3, [sqB, self.constB], stBB)
            mean_sb, meanB = self.tmp[0], self.tmpB[0]
            m2, m2B = self.tmp[1], self.tmpB[1]
            A("act", lambda e, w_=w_, stA=stA: e.activation(out=mean_sb[0:1, 0:w_], in_=stA[0:1, 0:w_], func=ACTF.Copy), reads=[stAB], writes=[meanB])
            A("dve", lambda e, w_=w_: e.tensor_tensor(out=m2[0:1, 0:w_], in0=mean_sb[0:1, 0:w_], in1=mean_sb[0:1, 0:w_], op=ALU.mult), reads=[meanB], writes=[m2B])
            A("dve", lambda e, w_=w_, stB_=stB_: e.scalar_tensor_tensor(out=self.vsb[0:1, 0:w_], in0=stB_[0:1, 0:w_], scalar=LN_EPS, in1=m2[0:1, 0:w_],
                                                                        op0=ALU.add, op1=ALU.subtract), reads=[stBB, m2B], writes=[self.vsbB])
            A("act", lambda e, w_=w_: e.activation(out=self.vsb[0:1, 0:w_], in_=self.vsb[0:1, 0:w_], func=ACTF.Sqrt), reads=[self.vsbB], writes=[self.vsbB])
            r, rB, r1, r1B = self.row_pow_bcast(w_)
            A("dve", lambda e, w_=w_, r1=r1: e.scalar_tensor_tensor(out=m2[0:1, 0:w_], in0=mean_sb[0:1, 0:w_], scalar=-1.0, in1=r1[0:1, 0:w_],
                                                                    op0=ALU.mult, op1=ALU.mult), reads=[meanB, r1B], writes=[m2B])
            nm, nmB = self.bank()
            self.bcast_row(nm, nmB, m2, m2B, w_)
            for j in range(4):
                i2 = self.sl_rr % 2
                self.sl_rr += 1
                z, zB = self.sl[i2], self.slB[i2]
                i3 = self.tmp_rr % 2
                self.tmp_rr += 1
                A("dve", lambda e, j=j, t=t, w_=w_, z=z, r=r: e.tensor_tensor(out=z[:, 0:w_], in0=cb[:, j, t.cols], in1=r[:, 0:w_], op=ALU.mult),
                  reads=[cbB[j][t.n], rB], writes=[zB])
                A("dve", lambda e, w_=w_, z=z, nm=nm: e.tensor_tensor(out=z[:, 0:w_], in0=z[:, 0:w_], in1=nm[:, 0:w_], op=ALU.add),
                  reads=[zB, nmB], writes=[zB])
                A("act", lambda e, j=j, t=t, z=z, w_=w_: e.activation(out=ycat[:, 4 + j, t.cols], in_=z[:, 0:w_], func=ACTF.Silu,
                                                                     bias=lnb(j), scale=lng(j)), reads=[zB, self.cB], writes=[ycatB[4 + j][t.n]])
        self.out_proj(("wout", pi, l), ycat, ycatB, 1, (l, 2))

    def attention(self, l):
        self.S.cur_tag = "attn"
        A = self.add
        pi = self.pi
        big = self.big
        self.region_reset("big", [])
        qT = big[:, 0:9216].rearrange("p (c t) -> p c t", c=8)
        oT = big[:, 9216:18432].rearrange("p (c t) -> p c t", c=8)
        kst = big[:, 18432:20480].rearrange("p (k d) -> p k d", k=2)
        vst = big[:, 20480:22528].rearrange("p (k d) -> p k d", k=2)
        kT = big[:, 22528:24576].rearrange("p (c k) -> p c k", c=8)
        ets = big[:, 24576:24640]
        qB = [[b for b in self.rbufs("big", 3)] for _ in range(8)]
        oB = [[b for b in self.rbufs("big", 3)] for _ in range(8)]
        kstB, vstB, kTB, etsB = self.rbufs("big", 4)
        self.region_reset("scr", [])
        kst1 = self.scr[:, 0:1024].bitcast(BF16).rearrange("p (k d) -> p k d", k=2)
        vst1 = self.scr[:, 1024:2048].bitcast(BF16).rearrange("p (k d) -> p k d", k=2)
        kst1B, vst1B = self.rbufs("scr", 2)
        kv = [(kst, kstB, vst, vstB), (kst1, kst1B, vst1, vst1B)]
        A("sp", lambda e: e.dma_start(out=self.kTp[:, :, :].rearrange("p c k -> p (c k)"), in_=self.kT_scr[l]),
          reads=[self.kscrB[l]], writes=[self.kTpB], dma=True)
        A("sp", lambda e: e.dma_start(out=self.vp[:, :, :].rearrange("p k d -> p (k d)"), in_=self.v_scr[l]),
          reads=[self.vscrB[l]], writes=[self.vpB], dma=True)
        for u in range(2):
            wv, wB, _ = self.wget(("wq", pi, l, u))
            for ml in range(4):
                m = u * 4 + ml
                for t in self.tiles:
                    bk, bkB = self.bank()
                    for c in range(8):
                        self.mm(bk[:, 0:t.w], wv[:, c, ml * 128:(ml + 1) * 128], self.h[:, c, t.cols], c == 0, c == 7,
                                [wB, self.hB[c][t.n]], bkB)
                    A("act", lambda e, m=m, t=t, bk=bk: e.activation(out=qT[:, m, t.cols], in_=bk[:, 0:t.w], func=ACTF.Copy, scale=1.0 / 16.0),
                      reads=[bkB], writes=[qB[m][t.n]])
        for t in self.tiles:
            if t.kind == "P":
                for hd in range(4):
                    i = self.et_rr % 2
                    self.et_rr += 1
                    et, etB = self.et[i], self.etB[i]
                    den, denB = self.stbank()
                    for kc in range(2):
                        bk, bkB = self.bank()
                        for dc in range(2):
                            m = 2 * hd + dc
                            self.mm(bk[:, 0:t.w], self.kTp[:, m, kc * 128:(kc + 1) * 128], qT[:, m, t.cols], dc == 0, dc == 1,
                                    [self.kTpB, qB[m][t.n]], bkB)
                        A("act", lambda e, kc=kc, et=et, bk=bk, t=t: e.activation(out=et[:, kc, 0:t.w], in_=bk[:, 0:t.w], func=ACTF.Exp),
                          reads=[bkB], writes=[etB])
                        self.mm(den[:, 0:t.w], self.ones1[:, :], et[:, kc, 0:t.w], kc == 0, kc == 1, [etB, self.constB], denB)
                    i2 = self.rstd_rr % 2
                    self.rstd_rr += 1
                    rd_, rdB = self.rstd[i2], self.rstdB[i2]
                    A("dve", lambda e, rd_=rd_, den=den, t=t: e.reciprocal(out=rd_[:, 0:t.w], in_=den[:, 0:t.w]), reads=[denB], writes=[rdB])
                    for dc in range(2):
                        m = 2 * hd + dc
                        bk, bkB = self.bank()
                        for kc in range(2):
                            self.mm(bk[:, 0:t.w], self.vp[:, kc, m * 128:(m + 1) * 128], et[:, kc, 0:t.w], kc == 0, kc == 1,
                                    [self.vpB, etB], bkB)
                        A("dve", lambda e, m=m, t=t, bk=bk, rd_=rd_: e.tensor_tensor(out=oT[:, m, t.cols], in0=bk[:, 0:t.w], in1=rd_[:, 0:t.w],
                                                                                    op=ALU.mult), reads=[bkB, rdB], writes=[oB[m][t.n]])
            else:
                for b in range(NB_S):
                    cs = slice(t.c0 + b * TS, t.c0 + (b + 1) * TS)
                    kst, kstB, vst, vstB = kv[b % 2]
                    A("pool", lambda e, b=b, kst=kst: e.dma_start(out=kst, in_=self.ck[l, b].rearrange("(k p) d -> p k d", p=128)),
                      writes=[kstB], dma=True)
                    A("pool", lambda e, b=b, vst=vst: e.dma_start(out=vst, in_=self.cv[l, b].rearrange("(k p) d -> p k d", p=128)),
                      writes=[vstB], dma=True)
                    for half in range(2):
                        bk, bkB = self.bank()
                        bkb = bk[:, :].bitcast(BF16)
                        for ml in range(4):
                            m = half * 4 + ml
                            for kc in range(2):
                                self.tr(bkb[:, ml * 256 + kc * 128: ml * 256 + (kc + 1) * 128], kst[:, kc, m * 128:(m + 1) * 128],
                                        self.identb[:, :], [kstB, self.constB], bkB)
                        A("act", lambda e, half=half, bkb=bkb: e.activation(out=kT[:, half * 4:(half + 1) * 4, :],
                                                                           in_=bkb.rearrange("p (c k) -> p c k", c=4), func=ACTF.Copy),
                          reads=[bkB], writes=[kTB])
                    sbk, sbkB = self.bank()
                    for hd in range(4):
                        for kc in range(2):
                            for dc in range(2):
                                m = 2 * hd + dc
                                self.mm(sbk[:, (hd * 2 + kc) * TS:(hd * 2 + kc + 1) * TS], kT[:, m, kc * 128:(kc + 1) * 128], qT[:, m, cs],
                                        dc == 0, dc == 1, [kTB, qB[m][t.n]], sbkB)
                    A("act", lambda e, sbk=sbk: e.activation(out=ets[:, 0:64], in_=sbk[:, 0:64], func=ACTF.Exp), reads=[sbkB], writes=[etsB])
                    den, denB = self.stbank()
                    e4 = ets[:, 0:64].rearrange("p (h k q) -> p h k q", h=4, k=2)
                    for kc in range(2):
                        self.mm(den[:, 0:32].rearrange("p (h q) -> p h q", h=4), self.ones1[:, :], e4[:, :, kc, :], kc == 0, kc == 1,
                                [etsB, self.constB], denB)
                    i2 = self.rstd_rr % 2
                    self.rstd_rr += 1
                    rd_, rdB = self.rstd[i2], self.rstdB[i2]
                    A("dve", lambda e, rd_=rd_, den=den: e.reciprocal(out=rd_[:, 0:32], in_=den[:, 0:32]), reads=[denB], writes=[rdB])
                    obk, obkB = self.bank()
                    for m in range(8):
                        hd = m // 2
                        for kc in range(2):
                            self.mm(obk[:, m * TS:(m + 1) * TS], vst[:, kc, m * 128:(m + 1) * 128],
                                    ets[:, (hd * 2 + kc) * TS:(hd * 2 + kc + 1) * TS], kc == 0, kc == 1, [vstB, etsB], obkB)
                    o4 = obk[:, 0:64].rearrange("p (h d q) -> p h d q", h=4, d=2)
                    for dc in range(2):
                        dst = oT[:, :, cs].rearrange("p (h d) q -> p h d q", d=2)[:, :, dc, :]
                        A("dve", lambda e, dst=dst, dc=dc, o4=o4, rd_=rd_: e.tensor_tensor(
                            out=dst, in0=o4[:, :, dc, :], in1=rd_[:, 0:32].rearrange("p (h q) -> p h q", h=4), op=ALU.mult),
                          reads=[obkB, rdB], writes=[oB[2 * hh + dc][t.n] for hh in range(4)])
        self.out_proj(("wo", pi, l), oT, oB, 3, (l, 4))

    def ffn(self, l):
        self.S.cur_tag = "ffn"
        A = self.add
        pi = self.pi
        big = self.big
        self.region_reset("big", [])
        self.region_reset("scr", [])
        act = big[:, 0:22 * TCOLS].rearrange("p (c t) -> p c t", c=22)
        actB = [[b for b in self.rbufs("big", 3)] for _ in range(22)]
        scr = self.scr
        GE = 1186
        gext = scr[:, 0:GE]
        gextB = self.rbufs("scr", 3)
        gctxB = self.rbufs("scr", 1)[0]
        accs = [scr[:, 1536:2048], scr[:, 2048:2560], scr[:, 2560:3072]]
        accB = self.rbufs("scr", 3)
        fw = lambda m, k: self.c2816[:, m, l * 3 + k:l * 3 + k + 1]
        fb = lambda m: self.c2816[:, m, 12 + l:12 + l + 1]

        def gcols(t, shift):
            if t.kind == "P":
                return gext[:, t.c0 + shift:t.c0 + shift + t.w]
            return gext[:, 1026:1186].rearrange("p (b s) -> p b s", b=NB_S)[:, :, shift:shift + TS]

        def view3(ap, t):
            return ap if t.kind == "P" else ap.rearrange("p (b s) -> p b s", b=NB_S)

        if pi == 0:
            for c0 in range(0, DFF, 1024):
                cw_ = min(1024, DFF - c0)

                def ev(cc, k, view, bkB, base=c0 // 128):
                    A("dve", lambda e: e.tensor_copy(out=self.fctx[:, base + cc:base + cc + k, :], in_=view), reads=[bkB], writes=[self.fctxB])
                self.load_rows_T(self.st_ffn[l].rearrange("b r d -> (b r) d")[:, c0:c0 + cw_], 32, cw_, ev)
        for u in range(6):
            wg, wgB, cwid = self.wget(("wg", pi, l, u))
            wu, wuB, _ = self.wget(("wu", pi, l, u), cont=True)
            for ml in range(cwid // 128):
                m = u * 4 + ml
                if pi == 0:
                    A("pool", lambda e: e.tensor_copy(out=gext[:, 0:2], in_=self.zeros[:, 0:2]), reads=[self.constB], writes=[gctxB])
                    A("pool", lambda e, m=m: e.tensor_copy(out=gext[:, 1026:1186].rearrange("p (b s) -> p b s", b=NB_S)[:, :, 0:2],
                                                          in_=self.fctx[:, m, :].rearrange("p (b r) -> p b r", b=NB_S)),
                      reads=[self.fctxB], writes=[gctxB])
                else:
                    A("pool", lambda e, m=m: e.tensor_copy(out=gext[:, 0:2], in_=self.car_f[l][:, m, :]), reads=[self.car_fB[l]], writes=[gctxB])
                T_ = self.tiles
                bkg_l, bku_l = [], []
                for t in T_:
                    bkg, bkgB = self.bank()
                    for c in range(8):
                        self.mm(bkg[:, 0:t.w], wg[:, c, ml * 128:(ml + 1) * 128], self.h[:, c, t.cols], c == 0, c == 7, [wgB, self.hB[c][t.n]], bkgB)
                    A("act", lambda e, t=t, bkg=bkg: e.activation(out=gcols(t, 2), in_=view3(bkg[:, 0:t.w], t), func=ACTF.Copy),
                      reads=[bkgB], writes=[gextB[t.n]])
                    acc, aB = accs[t.n], accB[t.n]
                    A("act", lambda e, m=m, t=t, bkg=bkg, acc=acc: e.activation(out=acc[:, 0:t.w], in_=bkg[:, 0:t.w], func=ACTF.Identity,
                                                                              bias=fb(m), scale=fw(m, 2)), reads=[bkgB, self.cB], writes=[aB])
                    bkg_l.append((bkg, bkgB))
                for t in T_:
                    bku, bkuB = self.bank()
                    for c in range(8):
                        self.mm(bku[:, 0:t.w], wu[:, c, ml * 128:(ml + 1) * 128], self.h[:, c, t.cols], c == 0, c == 7, [wuB, self.hB[c][t.n]], bkuB)
                    bku_l.append((bku, bkuB))
                for k in (0, 1):
                    for t in T_:
                        acc, aB = accs[t.n], accB[t.n]
                        av = view3(acc[:, 0:t.w], t)
                        rd = [gextB[t.n], gctxB, self.cB] + ([gextB[t.n - 1]] if (t.kind == "P" and t.n > 0) else [])
                        A("dve", lambda e, m=m, t=t, av=av, k=k: e.scalar_tensor_tensor(out=av, in0=gcols(t, k), scalar=fw(m, k), in1=av,
                                                                                       op0=ALU.mult, op1=ALU.add), reads=rd + [aB], writes=[aB])
                sls = []
                for t in T_:
                    acc, aB = accs[t.n], accB[t.n]
                    i2 = self.sq_rr % 4
                    self.sq_rr += 1
                    sl, slB = self.sq[i2], self.sqB[i2]
                    A("act", lambda e, t=t, acc=acc, sl=sl: e.activation(out=sl[:, 0:t.w], in_=acc[:, 0:t.w], func=ACTF.Silu), reads=[aB], writes=[slB])
                    sls.append((sl, slB))
                for ti, t in enumerate(T_):
                    bku, bkuB = bku_l[ti]
                    sl, slB = sls[ti]
                    A("dve", lambda e, m=m, t=t, bku=bku, sl=sl: e.tensor_tensor(out=act[:, m, t.cols], in0=bku[:, 0:t.w], in1=sl[:, 0:t.w], op=ALU.mult),
                      reads=[bkuB, slB], writes=[actB[m][t.n]])
                if pi == 0:
                    A("pool", lambda e, m=m: e.tensor_copy(out=self.car_f[l][:, m, :], in_=gext[:, 1024:1026]), reads=[gextB[1]], writes=[self.car_fB[l]])
                    src = gext[:, 1026:1186].rearrange("p (b s) -> p b s", b=NB_S)[:, :, 8:10].rearrange("p b r -> p r b")
                    A("pool", lambda e, m=m, src=src: e.tensor_copy(out=self.fstate[:, m, :].rearrange("p (r b) -> p r b", r=2), in_=src),
                      reads=[gextB[2]], writes=[self.fstateB])
                else:
                    A("pool", lambda e, m=m: e.tensor_copy(out=self.fstate[:, m, 0:2], in_=gext[:, 1024:1026]), reads=[gextB[1]], writes=[self.fstateB])
        if pi == 0:
            self.store_rows_T_multi([(self.fstate[:, m, :], [self.fstateB]) for m in range(22)], 32,
                                    [(r * 16, 16, self.o_ffn_s[l][:, r, :]) for r in range(2)])
        else:
            self.store_rows_T_multi([(self.fstate[:, m, 0:2], [self.fstateB]) for m in range(22)], 2, [(0, 2, self.o_ffn_p[l])])
        for ch in range(2):
            ws = [self.wget(("wd", pi, l, ch, kg), cont=(kg > 0)) for kg in range(3)]
            for ml in range(4):
                m = ch * 4 + ml
                for t in self.tiles:
                    bk, bkB = self.bank()
                    kk = 0
                    for kg, (k0, kn) in enumerate(((0, 8), (8, 8), (16, 6))):
                        wv, wB, _ = ws[kg]
                        for c in range(kn):
                            self.mm(bk[:, 0:t.w], wv[:, c, ml * 128:(ml + 1) * 128], act[:, k0 + c, t.cols], kk == 0, kk == 21,
                                    [wB, actB[k0 + c][t.n]], bkB)
                            kk += 1
                    A("act", lambda e, m=m, t=t, bk=bk: e.activation(out=self.h[:, m, t.cols], in_=bk[:, 0:t.w], func=ACTF.Copy),
                      reads=[bkB], writes=[self.hB[m][t.n]])
        nxt = (l + 1, 0) if l + 1 < DEPTH else None
        for t in self.tiles:
            self.post_norm(t, 5)
            if nxt is not None:
                self.l_next_pre(t, nxt)


_NC_CACHE = {}


def _get_nc():
    if "nc" not in _NC_CACHE:
        _NC_CACHE["nc"] = Builder().build()
    return _NC_CACHE["nc"]


def kernel(x_prompt, x_sample, state_pool, state_conf, state_gconv, state_ffn, cache_mem_k, cache_mem_v,
           mem_prompt, norm_gains, w_in_even, w_pool, pool_scale, conf_w, conf_b, conf_ln_g, conf_ln_b,
           w_out_even, w_in_odd, gconv_w, w_out_odd, w_mem_q, w_mem_k, w_mem_v, w_mem_o,
           w_ffn_gate, w_ffn_up, ffn_conv_w, ffn_conv_b, w_ffn_down):
    f = lambda a: np.ascontiguousarray(np.asarray(a, dtype=np.float32))
    shared = {
        "norm_gains": f(norm_gains).reshape(DEPTH * 7, D), "w_in_even": f(w_in_even), "w_pool": f(w_pool),
        "pool_scale": f(pool_scale), "conf_w": f(conf_w).reshape(62, 512), "conf_b": f(conf_b),
        "conf_ln_g": f(conf_ln_g), "conf_ln_b": f(conf_ln_b), "w_out_even": f(w_out_even), "w_in_odd": f(w_in_odd),
        "gconv_w": f(gconv_w).reshape(6, D), "w_out_odd": f(w_out_odd), "w_mem_q": f(w_mem_q), "w_mem_k": f(w_mem_k),
        "w_mem_v": f(w_mem_v), "w_mem_o": f(w_mem_o), "w_ffn_gate": f(w_ffn_gate), "w_ffn_up": f(w_ffn_up),
        "ffn_conv_w": f(ffn_conv_w).reshape(DEPTH * 3, DFF), "ffn_conv_b": f(ffn_conv_b), "w_ffn_down": f(w_ffn_down),
    }
    x_prompt, x_sample = f(x_prompt), f(x_sample)
    state_pool, state_conf, state_gconv, state_ffn = f(state_pool), f(state_conf), f(state_gconv), f(state_ffn)
    cache_mem_k, cache_mem_v, mem_prompt = f(cache_mem_k), f(cache_mem_v), f(mem_prompt)
    in_maps = []
    for c in range(NCORES):
        bs = slice(c * NB_S, (c + 1) * NB_S)
        m = dict(shared)
        m["xp"] = x_prompt[c]
        m["xs"] = x_sample[bs].reshape(NB_S * TS, D)
        m["st_pool"] = state_pool[:, bs]
        m["st_conf"] = state_conf[:, bs]
        m["st_gconv"] = state_gconv[:, bs]
        m["st_ffn"] = state_ffn[:, bs]
        m["ck"] = cache_mem_k[:, bs].reshape(DEPTH, NB_S, NMEM, D)
        m["cv"] = cache_mem_v[:, bs].reshape(DEPTH, NB_S, NMEM, D)
        m["memp"] = mem_prompt[c]
        in_maps.append({k: np.ascontiguousarray(v) for k, v in m.items()})
    nc = _get_nc()
    res = run_bass_kernel_spmd(nc, in_maps, core_ids=list(range(NCORES)))
    R = res.results
    cat = lambda k, ax: np.concatenate([np.asarray(r[k]) for r in R], axis=ax)
    stack = lambda k, ax: np.stack([np.asarray(r[k]) for r in R], axis=ax)
    y_prompt = stack("yp", 0)
    y_sample = cat("ys", 0).reshape(NCORES * NB_S, TS, D)
    pool_p = stack("o_pool_p", 1)
    conf_p = stack("o_conf_p", 1)
    gconv_p = stack("o_gconv_p", 1)
    ffn_p = stack("o_ffn_p", 1)
    mk_p = stack("o_mk_p", 1).reshape(DEPTH, NCORES, NMEM, 4, 256)
    mv_p = stack("o_mv_p", 1).reshape(DEPTH, NCORES, NMEM, 4, 256)
    pool_s = cat("o_pool_s", 1)
    conf_s = cat("o_conf_s", 1)
    gconv_s = cat("o_gconv_s", 1)
    ffn_s = cat("o_ffn_s", 1)
    return (y_prompt, y_sample, pool_p, conf_p, gconv_p, ffn_p, mk_p, mv_p, pool_s, conf_s, gconv_s, ffn_s)
```
